# Optimizing a Trainium2 kernel written in Bass

```python
import jax, jax.numpy as jnp
from jax import lax
import numpy as np

D_MODEL = 2048
BATCH = 8
SEQ = 2048
DEPTH = 2

GRID_W = 64
CTX_LEN = 256
HEAD_DIM = 128
NA_HEADS = 8
NA_WIN_H = 8
NA_WIN_W = 16
GQA_HEADS = 8
GQA_KV_HEADS = 2
MLA_HEADS = 8
MLA_Q_RANK = 512
MLA_KV_RANK = 512
MLA_NOPE = 128
MLA_ROPE = 64
MLA_V = 128
A_W = NA_HEADS * HEAD_DIM
B_QW = GQA_HEADS * HEAD_DIM
B_KVW = GQA_KV_HEADS * HEAD_DIM
C_W = MLA_HEADS * MLA_V
D_FF = -(-8 * D_MODEL // (3 * 256)) * 256
IN_SIZES = (A_W, A_W, A_W, B_QW, B_KVW, B_KVW, MLA_Q_RANK, MLA_KV_RANK, MLA_ROPE, D_MODEL, D_MODEL, D_MODEL)
IN_WIDTH = sum(IN_SIZES)
ROPE_THETA = 10000.0
Q_BLOCK = 128
EPS = 1e-6
NEG_INF = -1e30

kernel_name = 'hybrid_natten_gqa_mla_prefix_dit'


def _rms_norm(x, g):
    xf = x.astype(jnp.float32)
    y = xf * lax.rsqrt(jnp.mean(xf * xf, axis=-1, keepdims=True) + EPS)
    return (y * g.astype(jnp.float32)).astype(x.dtype)


def _rope_1d(x, pos):
    d = x.shape[-1]
    freqs = ROPE_THETA ** (-jnp.arange(0, d, 2, dtype=jnp.float32) / d)
    ang = pos.astype(jnp.float32)[:, None] * freqs[None, :]
    cos, sin = jnp.cos(ang), jnp.sin(ang)
    xf = x.astype(jnp.float32)
    x1, x2 = xf[..., : d // 2], xf[..., d // 2:]
    return jnp.concatenate([x1 * cos - x2 * sin, x2 * cos + x1 * sin], axis=-1).astype(x.dtype)


def _rope_2d(x, row, col):
    half = x.shape[-1] // 2
    return jnp.concatenate([_rope_1d(x[..., :half], row), _rope_1d(x[..., half:], col)], axis=-1)


def _heads(z, n):
    b, t, _ = z.shape
    return z.reshape(b, t, n, -1).transpose(0, 2, 1, 3)


def _merge_heads(o):
    b, h, t, d = o.shape
    return o.transpose(0, 2, 1, 3).reshape(b, t, h * d)


def _project_in(h, w_in):
    z = jnp.einsum('btd,de->bte', h, w_in)
    return jnp.split(z, np.cumsum(IN_SIZES)[:-1].tolist(), axis=-1)


def _attend(q, k, v, scale):
    s = jnp.einsum('bhgqd,bhkd->bhgqk', q, k, preferred_element_type=jnp.float32) * scale
    p = jax.nn.softmax(s, axis=-1).astype(v.dtype)
    return jnp.einsum('bhgqk,bhkd->bhgqd', p, v)


def _blocked_attention(q, k, v, k_ctx, v_ctx, scale):
    b, hk, g, s, d = q.shape
    nb = s // Q_BLOCK
    k_all = jnp.concatenate([k, k_ctx], axis=2)
    v_all = jnp.concatenate([v, v_ctx], axis=2)
    qb = jnp.moveaxis(q.reshape(b, hk, g, nb, Q_BLOCK, d), 3, 0)
    o = lax.map(lambda qi: _attend(qi, k_all, v_all, scale), qb)
    return jnp.moveaxis(o, 0, 3).reshape(b, hk, g, s, v.shape[-1])


def _neighborhood_attention(q, k, v, k_ctx, v_ctx, rpb):
    b, h, s, d = q.shape
    rows = s // GRID_W
    kh, kw = min(NA_WIN_H, rows), NA_WIN_W
    scale = d ** -0.5
    qg = q.reshape(b, h, rows, GRID_W, d)
    kg = k.reshape(b, h, rows, GRID_W, d)
    vg = v.reshape(b, h, rows, GRID_W, d)
    qc = np.arange(GRID_W)
    c0 = np.clip(qc - kw // 2, 0, GRID_W - kw)
    in_win = (qc[None, :] >= c0[:, None]) & (qc[None, :] < c0[:, None] + kw)
    col_mask = jnp.where(jnp.asarray(in_win), 0.0, NEG_INF)[:, None, :]
    dc_idx = np.clip(qc[None, :] - qc[:, None], -(kw - 1), kw - 1) + (NA_WIN_W - 1)

    def row_block(r):
        r0 = jnp.clip(r - kh // 2, 0, rows - kh)
        q_r = lax.dynamic_index_in_dim(qg, r, axis=2, keepdims=False)
        k_b = lax.dynamic_slice_in_dim(kg, r0, kh, axis=2)
        v_b = lax.dynamic_slice_in_dim(vg, r0, kh, axis=2)
        dr_idx = r0 + jnp.arange(kh) - r + (NA_WIN_H - 1)
        bias = rpb[:, dr_idx[None, :, None], dc_idx[:, None, :]]
        s_win = jnp.einsum('bhqd,bhiwd->bhqiw', q_r, k_b, preferred_element_type=jnp.float32) * scale + bias + col_mask
        s_ctx = jnp.einsum('bhqd,bhcd->bhqc', q_r, k_ctx, preferred_element_type=jnp.float32) * scale
        s_all = jnp.concatenate([s_win.reshape(b, h, GRID_W, kh * GRID_W), s_ctx], axis=-1)
        p = jax.nn.softmax(s_all, axis=-1).astype(v.dtype)
        p_win = p[..., : kh * GRID_W].reshape(b, h, GRID_W, kh, GRID_W)
        p_ctx = p[..., kh * GRID_W:]
        return (jnp.einsum('bhqiw,bhiwd->bhqd', p_win, v_b)
                + jnp.einsum('bhqc,bhcd->bhqd', p_ctx, v_ctx))

    o = lax.map(row_block, jnp.arange(rows))
    return jnp.moveaxis(o, 0, 2).reshape(b, h, s, d)


def _gqa_qkv(bq, bk, bv, q_norm, k_norm, row, col):
    q = _rms_norm(_heads(bq, GQA_HEADS), q_norm)
    k = _rms_norm(_heads(bk, GQA_KV_HEADS), k_norm)
    v = _heads(bv, GQA_KV_HEADS)
    if row is not None:
        q, k = _rope_2d(q, row, col), _rope_2d(k, row, col)
    b, hq, t, d = q.shape
    return q.reshape(b, GQA_KV_HEADS, hq // GQA_KV_HEADS, t, d), k, v


def _mla_qkv(cq, ckv, ckr, q_norm, kv_norm, w_uq, w_ukv, row, col):
    q = _heads(jnp.einsum('btr,re->bte', _rms_norm(cq, q_norm), w_uq), MLA_HEADS)
    kv = _heads(jnp.einsum('btr,re->bte', _rms_norm(ckv, kv_norm), w_ukv), MLA_HEADS)
    q_nope, q_rope = q[..., :MLA_NOPE], q[..., MLA_NOPE:]
    k_nope, v = kv[..., :MLA_NOPE], kv[..., MLA_NOPE:]
    k_rope = ckr[:, None]
    if row is not None:
        q_rope, k_rope = _rope_2d(q_rope, row, col), _rope_2d(k_rope, row, col)
    k_rope = jnp.broadcast_to(k_rope, k_nope.shape[:-1] + (MLA_ROPE,))
    q = jnp.concatenate([q_nope, q_rope], axis=-1)[:, :, None]
    k = jnp.concatenate([k_nope, k_rope], axis=-1)
    return q, k, v


def _merge(o_a, o_b, o_c, ga, gb, gc, w_br_a, w_br_b, w_br_c, w_o):
    def branch(o, w):
        return jnp.einsum('bte,ed->btd', _merge_heads(o), w)
    y = (jax.nn.sigmoid(ga) * branch(o_a, w_br_a)
         + jax.nn.sigmoid(gb) * branch(o_b, w_br_b)
         + jax.nn.sigmoid(gc) * branch(o_c, w_br_c))
    return jnp.einsum('btd,de->bte', y, w_o)


def _token_mixer(h, hc, w_in, rpb, gqa_qn, gqa_kn, mla_qn, mla_kvn, w_uq, w_ukv,
                 w_br_a, w_br_b, w_br_c, w_o, row, col, ctx_out):
    aq, ak, av, bq, bk, bv, cq, ckv, ckr, ga, gb, gc = _project_in(h, w_in)
    aq_c, ak_c, av_c, bq_c, bk_c, bv_c, cq_c, ckv_c, ckr_c, ga_c, gb_c, gc_c = _project_in(hc, w_in)
    ka_c, va_c = _heads(ak_c, NA_HEADS), _heads(av_c, NA_HEADS)
    o_a = _neighborhood_attention(_heads(aq, NA_HEADS), _heads(ak, NA_HEADS), _heads(av, NA_HEADS), ka_c, va_c, rpb)
    qb, kb, vb = _gqa_qkv(bq, bk, bv, gqa_qn, gqa_kn, row, col)
    qb_c, kb_c, vb_c = _gqa_qkv(bq_c, bk_c, bv_c, gqa_qn, gqa_kn, None, None)
    scale_b = HEAD_DIM ** -0.5
    o_b = _blocked_attention(qb, kb, vb, kb_c, vb_c, scale_b)
    o_b = o_b.reshape(o_b.shape[0], GQA_HEADS, o_b.shape[3], HEAD_DIM)
    qc, kc, vc = _mla_qkv(cq, ckv, ckr, mla_qn, mla_kvn, w_uq, w_ukv, row, col)
    qc_c, kc_c, vc_c = _mla_qkv(cq_c, ckv_c, ckr_c, mla_qn, mla_kvn, w_uq, w_ukv, None, None)
    scale_c = (MLA_NOPE + MLA_ROPE) ** -0.5
    o_c = _blocked_attention(qc, kc, vc, kc_c, vc_c, scale_c)[:, :, 0]
    y = _merge(o_a, o_b, o_c, ga, gb, gc, w_br_a, w_br_b, w_br_c, w_o)
    if not ctx_out:
        return y, None
    o_a_c = _attend(_heads(aq_c, NA_HEADS)[:, :, None], ka_c, va_c, HEAD_DIM ** -0.5)[:, :, 0]
    o_b_c = _attend(qb_c, kb_c, vb_c, scale_b)
    o_b_c = o_b_c.reshape(o_b_c.shape[0], GQA_HEADS, o_b_c.shape[3], HEAD_DIM)
    o_c_c = _attend(qc_c, kc_c, vc_c, scale_c)[:, :, 0]
    yc = _merge(o_a_c, o_b_c, o_c_c, ga_c, gb_c, gc_c, w_br_a, w_br_b, w_br_c, w_o)
    return y, yc


def _swiglu(h, w1, w3, w2):
    a = jnp.einsum('btd,df->btf', h, w1)
    g = jnp.einsum('btd,df->btf', h, w3)
    return jnp.einsum('btf,fd->btd', jax.nn.silu(a) * g, w2)


def setup_inputs(seed: int = 0) -> dict:
    key = jax.random.key(seed)
    ks = jax.random.split(key, 25)
    L, D = DEPTH, D_MODEL

    def nrm(k, shape, std):
        return jax.random.normal(k, shape, jnp.float32) * std

    def gain(k, shape):
        return 1.0 + 0.1 * jax.random.normal(k, shape, jnp.float32)

    return {
        'x': nrm(ks[0], (BATCH, SEQ, D), 1.0),
        'c': nrm(ks[1], (BATCH, D), 1.0),
        'ctx': nrm(ks[2], (BATCH, CTX_LEN, D), 1.0),
        'c_ctx': nrm(ks[3], (D,), 1.0),
        'w_ada': nrm(ks[4], (L, D, 6 * D), 0.5 * D ** -0.5),
        'b_ada': nrm(ks[5], (L, 6 * D), 0.01),
        'g_pre1': gain(ks[6], (L, D)),
        'g_post1': gain(ks[7], (L, D)),
        'g_pre2': gain(ks[8], (L, D)),
        'g_post2': gain(ks[9], (L, D)),
        'w_in': nrm(ks[10], (L, D, IN_WIDTH), D ** -0.5),
        'rpb': nrm(ks[11], (L, NA_HEADS, 2 * NA_WIN_H - 1, 2 * NA_WIN_W - 1), 0.5),
        'gqa_q_norm': gain(ks[12], (L, HEAD_DIM)),
        'gqa_k_norm': gain(ks[13], (L, HEAD_DIM)),
        'mla_q_norm': gain(ks[14], (L, MLA_Q_RANK)),
        'mla_kv_norm': gain(ks[15], (L, MLA_KV_RANK)),
        'w_uq': nrm(ks[16], (L, MLA_Q_RANK, MLA_HEADS * (MLA_NOPE + MLA_ROPE)), MLA_Q_RANK ** -0.5),
        'w_ukv': nrm(ks[17], (L, MLA_KV_RANK, MLA_HEADS * (MLA_NOPE + MLA_V)), MLA_KV_RANK ** -0.5),
        'w_br_a': nrm(ks[18], (L, A_W, D), A_W ** -0.5),
        'w_br_b': nrm(ks[19], (L, B_QW, D), B_QW ** -0.5),
        'w_br_c': nrm(ks[20], (L, C_W, D), C_W ** -0.5),
        'w_o': nrm(ks[21], (L, D, D), D ** -0.5),
        'w_ff1': nrm(ks[22], (L, D, D_FF), D ** -0.5),
        'w_ff3': nrm(ks[23], (L, D, D_FF), D ** -0.5),
        'w_ff2': nrm(ks[24], (L, D_FF, D), D_FF ** -0.5),
    }


def reference(x, c, ctx, c_ctx, w_ada, b_ada, g_pre1, g_post1, g_pre2, g_post2, w_in, rpb,
              gqa_q_norm, gqa_k_norm, mla_q_norm, mla_kv_norm, w_uq, w_ukv,
              w_br_a, w_br_b, w_br_c, w_o, w_ff1, w_ff3, w_ff2):
    seq = x.shape[1]
    t = jnp.arange(seq)
    row, col = t // GRID_W, t % GRID_W
    cx = ctx
    silu_c = jax.nn.silu(c)
    silu_cc = jax.nn.silu(c_ctx)
    for l in range(DEPTH):
        ctx_out = l < DEPTH - 1
        mod = jnp.einsum('bd,de->be', silu_c, w_ada[l]) + b_ada[l]
        sh1, sc1, gt1, sh2, sc2, gt2 = [m[:, None, :] for m in jnp.split(mod, 6, axis=-1)]
        mod_c = jnp.einsum('d,de->e', silu_cc, w_ada[l]) + b_ada[l]
        sh1c, sc1c, gt1c, sh2c, sc2c, gt2c = jnp.split(mod_c, 6, axis=-1)
        h = _rms_norm(x, g_pre1[l]) * (1.0 + sc1) + sh1
        hc = _rms_norm(cx, g_pre1[l]) * (1.0 + sc1c) + sh1c
        y, yc = _token_mixer(h, hc, w_in[l], rpb[l], gqa_q_norm[l], gqa_k_norm[l], mla_q_norm[l], mla_kv_norm[l],
                             w_uq[l], w_ukv[l], w_br_a[l], w_br_b[l], w_br_c[l], w_o[l], row, col, ctx_out)
        x = x + gt1 * _rms_norm(y, g_post1[l])
        h2 = _rms_norm(x, g_pre2[l]) * (1.0 + sc2) + sh2
        x = x + gt2 * _rms_norm(_swiglu(h2, w_ff1[l], w_ff3[l], w_ff2[l]), g_post2[l])
        if ctx_out:
            cx = cx + gt1c * _rms_norm(yc, g_post1[l])
            h2c = _rms_norm(cx, g_pre2[l]) * (1.0 + sc2c) + sh2c
            cx = cx + gt2c * _rms_norm(_swiglu(h2c, w_ff1[l], w_ff3[l], w_ff2[l]), g_post2[l])
    return x
```

```python
import contextlib
import numpy as np
import concourse.bass as bass
import concourse.mybir as mybir
from concourse.bass_utils import run_bass_kernel_spmd

F32 = mybir.dt.float32
BF16 = mybir.dt.bfloat16
ALU = mybir.AluOpType
AF = mybir.ActivationFunctionType

D = 2048
TL = 2048
TC = 256
T = TL + TC
NCH = 16
DFF = 5632
FCH = 44
IN_W = 11840
EPS = 1e-6
NEG = -30000.0
NV = 170
O_AQ, O_AK, O_AV, O_BQ, O_BK, O_BV, O_CQ, O_CKV, O_CKR, O_GA = 0, 1024, 2048, 3072, 4096, 4352, 4608, 5120, 5632, 5696

ENGS = ["sp", "pe", "dve", "act", "pool"]


class Op:
    __slots__ = ("eng", "fn", "deps", "signal", "count", "chan", "chan_val")

    def __init__(self, eng, fn):
        self.eng = eng
        self.fn = fn
        self.deps = ()
        self.signal = False
        self.count = 0
        self.chan = None
        self.chan_val = 0


class Prog:
    def __init__(self, nc, same_engine_sync=True):
        self.nc = nc
        self.streams = {e: [] for e in ENGS}
        self.last_w = {}
        self.readers = {}
        self.chan_count = {}
        self.last_dma = {}
        self.same_engine_sync = same_engine_sync

    def add(self, eng, fn, reads=(), writes=(), chan=None):
        op = Op(eng, fn)
        deps = {}
        lw = self.last_w
        rdrs = self.readers
        for r in reads:
            w = lw.get(r)
            if w is not None:
                deps[id(w)] = w
        for t in writes:
            w = lw.get(t)
            if w is not None:
                deps[id(w)] = w
            lst = rdrs.get(t)
            if lst:
                for rd in lst:
                    deps[id(rd)] = rd
        op.deps = list(deps.values())
        for r in reads:
            l = rdrs.get(r)
            if l is None:
                rdrs[r] = [op]
            elif chan is None:
                for i, o in enumerate(l):
                    if o.eng == eng and o.chan is None:
                        l[i] = op
                        break
                else:
                    l.append(op)
            else:
                l.append(op)
        for t in writes:
            lw[t] = op
            rdrs[t] = []
        if chan is not None:
            c = self.chan_count.get(chan, 0) + 1
            self.chan_count[chan] = c
            op.chan = chan
            op.chan_val = 16 * c
            self.last_dma[chan] = op
        self.streams[eng].append(op)
        return op

    def barrier(self):
        deps = []
        for e in ENGS:
            for x in reversed(self.streams[e]):
                if x.chan is None and x.fn is not None:
                    deps.append(x)
                    break
        deps.extend(self.last_dma.values())
        for e in ENGS:
            op = Op(e, None)
            op.deps = list(deps)
            self.streams[e].append(op)
        self.last_w = {}
        self.readers = {}

    def _needs_sem(self, x, d):
        if d.chan is not None:
            return True
        if d.eng == x.eng and x.chan is None:
            if x.eng == "pe":
                return False
            return self.same_engine_sync or x.fn is None
        return True

    def emit(self):
        nc = self.nc
        for e in ENGS:
            for x in self.streams[e]:
                for d in x.deps:
                    if d.chan is None and self._needs_sem(x, d):
                        d.signal = True
        for e in ENGS:
            c = 0
            for x in self.streams[e]:
                if x.chan is None and x.signal:
                    c += 1
                    x.count = c
        chans = sorted(self.chan_count.keys())
        with contextlib.ExitStack() as st:
            esem = {e: st.enter_context(nc.semaphore("s_" + e)) for e in ENGS}
            csem = {c: st.enter_context(nc.semaphore("c_" + str(c))) for c in chans}
            block = st.enter_context(nc.Block())

            def run_stream(ename, eng):
                waited = {}
                for x in self.streams[ename]:
                    need = {}
                    for d in x.deps:
                        if not self._needs_sem(x, d):
                            continue
                        if d.chan is not None:
                            key = ("c", d.chan)
                            val = d.chan_val
                        else:
                            key = ("e", d.eng)
                            val = d.count
                        if waited.get(key, 0) >= val:
                            continue
                        if need.get(key, 0) < val:
                            need[key] = val
                    for key, val in need.items():
                        sem = csem[key[1]] if key[0] == "c" else esem[key[1]]
                        eng.wait_ge(sem, val)
                        waited[key] = val
                    if x.fn is None:
                        continue
                    ins = x.fn(eng)
                    if x.chan is not None:
                        ins.then_inc(csem[x.chan], 16)
                    elif x.signal:
                        ins.then_inc(esem[ename], 1)
                if ename == "sp":
                    for c in chans:
                        v = 16 * self.chan_count[c]
                        if waited.get(("c", c), 0) < v:
                            eng.wait_ge(csem[c], v)

            @block.sync
            def _(e):
                run_stream("sp", e)

            @block.tensor
            def _(e):
                run_stream("pe", e)

            @block.vector
            def _(e):
                run_stream("dve", e)

            @block.scalar
            def _(e):
                run_stream("act", e)

            @block.gpsimd
            def _(e):
                run_stream("pool", e)


def _rope_tables():
    t = np.arange(TL)
    row, col = (t // 64).astype(np.float64), (t % 64).astype(np.float64)

    def tab(dh):
        half = dh // 2
        q = half // 2
        f = 10000.0 ** (-np.arange(0, half, 2, dtype=np.float64) / half)
        cos = np.zeros((dh, TL)); sin = np.zeros((dh, TL))
        for ax, pos in enumerate((row, col)):
            ang = pos[None, :] * f[:, None]
            b = ax * half
            cos[b:b + q] = np.cos(ang); cos[b + q:b + half] = np.cos(ang)
            sin[b:b + q] = np.sin(ang); sin[b + q:b + half] = np.sin(ang)
        rot = np.zeros((dh, dh))
        for ax in range(2):
            b = ax * half
            for i in range(q):
                rot[b + i, b + q + i] = -1.0
                rot[b + q + i, b + i] = 1.0
        return np.stack([cos, sin]).astype(np.float32), np.ascontiguousarray(rot.T).astype(np.float32)

    return tab(128), tab(64)


def _abias_index():
    tiles = []
    for g, ms in ((0, range(0, 6)), (1, range(2, 10)), (3, range(10, 16))):
        for m in ms:
            a = np.arange(128) // 64
            w = np.arange(128) % 64
            rr = np.arange(512) // 64
            qc = np.arange(512) % 64
            j = (2 * m + a)[:, None]
            r = (8 * g + rr)[None, :]
            r0 = np.clip(r - 4, 0, 24)
            valid = (j >= r0) & (j < r0 + 8)
            c0 = np.clip(qc - 8, 0, 48)[None, :]
            valid = valid & (w[:, None] >= c0) & (w[:, None] < c0 + 16)
            dr = np.clip(j - r + 7, 0, 14)
            dc = np.clip(w[:, None] - qc[None, :], -15, 15) + 15
            tiles.append((np.broadcast_to(dr, (128, 512)), dc, valid))
    dr = np.stack([t[0] for t in tiles]); dc = np.stack([t[1] for t in tiles]); va = np.stack([t[2] for t in tiles])
    return dr, dc, va


def _abias_tile_id(g, m):
    if g == 0:
        return m
    if g == 1:
        return 6 + (m - 2)
    if g == 2:
        return 6 + (m - 6)
    return 14 + (m - 10)


def _a_chunks(g):
    return {0: range(0, 6), 1: range(2, 10), 2: range(6, 14), 3: range(10, 16)}[g]


def _pc(v):
    v = np.asarray(v, np.float32)
    return np.ascontiguousarray(v.reshape(-1, 128).T)


class Cfg:
    def __init__(self, layers=2, taps=(), stop=None, x_in="xT"):
        self.layers = layers
        self.taps = set(taps)
        self.stop = stop
        self.x_in = x_in


def build(cfg):
    nc = bass.Bass("TRN2", target_bir_lowering=False)
    in_shapes = {}

    def inp(name, shape):
        in_shapes[name] = tuple(shape)
        return nc.dram_tensor(name, list(shape), F32, kind="ExternalInput").ap()

    out_names = []

    def scratch(name, shape, dt):
        if name in cfg.taps:
            out_names.append(name)
            return nc.dram_tensor(name, list(shape), dt, kind="ExternalOutput").ap()
        return nc.dram_tensor(name, list(shape), dt, kind="Internal").ap()

    st = contextlib.ExitStack()
    with st:
        RESb = st.enter_context(nc.sbuf_tensor("RES", [128, 36864], BF16))
        AR = st.enter_context(nc.sbuf_tensor("ARENA", [128, 30720], F32))
        ONES = st.enter_context(nc.sbuf_tensor("ONES", [128, 128], BF16))
        ROTB = st.enter_context(nc.sbuf_tensor("ROTB", [128, 128], BF16))
        ROTC = st.enter_context(nc.sbuf_tensor("ROTC", [64, 64], BF16))
        CV = st.enter_context(nc.sbuf_tensor("CV", [128, 32], F32))
        CVS = st.enter_context(nc.sbuf_tensor("CVS", [128, 32], BF16))
        VEC = st.enter_context(nc.sbuf_tensor("VEC", [128, 2 * NV], F32))
        MOD = st.enter_context(nc.sbuf_tensor("MOD", [128, 2 * 192], F32))
        COEF = st.enter_context(nc.sbuf_tensor("COEF", [128, 2 * 192], F32))
        PS = st.enter_context(nc.psum_tensor("PS", [128, 8 * 512], F32))
        P = Prog(nc)

        def bank(b):
            return PS[:, b * 512:(b + 1) * 512]

        def arf(off, n):
            return AR[:, off:off + n]

        def arb(off, n):
            return AR[:, off:off + n].bitcast(BF16)

        def vec(l, col, n=1):
            return VEC[:, l * NV + col:l * NV + col + n]

        V_GPRE1, V_GPOST1, V_GPRE2, V_GPOST2, V_BADA, V_QN, V_KN, V_MQN, V_MKVN = 0, 16, 32, 48, 64, 160, 161, 162, 166

        def coef(l, kind, c, w):
            o = l * 192 + kind * 32 + c * 2 + w
            return COEF[:, o:o + 1]

        stop = {"flag": False}

        def phase_end(l, name):
            P.barrier()
            if cfg.stop == (l, name):
                stop["flag"] = True

        xT = inp("xT", [D, T])
        cvec = inp("cvec", [128, 32])
        rotb_in = inp("rotB", [128, 128])
        rotc_in = inp("rotC", [64, 64])
        ropeB = inp("ropeB", [2, 128, TL])
        ropeC = inp("ropeC", [2, 64, TL])
        vecs_in = inp("vecs", [128, 2 * NV])

        X = [xT] + [scratch(f"X{i}", [D, T], F32) for i in (1, 2, 3)]
        OUT = nc.dram_tensor("outT", [D, TL], F32, kind="ExternalOutput").ap()
        out_names.append("outT")
        QKA = scratch("QKA", [2048, T], BF16)
        VA = scratch("VA", [T, 1024], BF16)
        QB = scratch("QB", [1024, T], BF16)
        KB = scratch("KB", [256, T], BF16)
        VB = scratch("VB", [T, 256], BF16)
        CQN = scratch("CQN", [1024, T], BF16)
        CKR = scratch("CKR", [64, T], F32)
        SG = scratch("SG", [6144, T], BF16)
        QCN = scratch("QCN", [1024, T], BF16)
        QCR = scratch("QCR", [512, T], BF16)
        KCN = scratch("KCN", [1024, T], BF16)
        KRP = scratch("KRP", [64, T], BF16)
        VC = scratch("VC", [T, 1024], BF16)
        OT = scratch("OT", [3072, T], BF16)
        YT = scratch("YT", [D, T], BF16)
        UT = scratch("UT", [D, T], F32)
        U2T = scratch("U2T", [D, T], F32)
        ACTT = scratch("ACTT", [DFF, T], BF16)

        P.add("pool", lambda e: e.memset(ONES[:], 1.0), writes=["ones"])
        P.add("pool", lambda e: e.dma_start(out=ROTB[:], in_=rotb_in), writes=["rotb"], chan="k0")
        P.add("pool", lambda e: e.dma_start(out=ROTC[:], in_=rotc_in), writes=["rotc"], chan="k1")
        P.add("sp", lambda e: e.dma_start(out=CV[:], in_=cvec), writes=["cv"], chan="k2")
        P.add("sp", lambda e: e.dma_start(out=VEC[:], in_=vecs_in), writes=["vec"], chan="k3")
        P.add("act", lambda e: e.activation(out=CVS[:], in_=CV[:], func=AF.Silu), reads=["cv"], writes=["cvs"])
        P.barrier()

        wcount = {"n": 0}

        def load_slab(src3, kc, ncols, nslots, slot_words, base=0, split=4):
            s = wcount["n"] % nslots
            wcount["n"] += 1
            view = arb(base + s * slot_words, slot_words)[:, 0:kc * ncols].rearrange("p (c e) -> p c e", c=kc)
            per = (kc + split - 1) // split
            for j in range(split):
                k0, k1 = j * per, min(kc, (j + 1) * per)
                if k0 >= k1:
                    continue
                P.add("pool", lambda e, v=view, k0=k0, k1=k1: e.dma_start(out=v[:, k0:k1, :], in_=src3[:, k0:k1, :]),
                      writes=[f"wb{s}.{j}"], chan=f"w{s}.{j}")
            return view, f"wb{s}", per

        def wsrc(wap, K, c0, nc_):
            return wap[0:K, c0:c0 + nc_].rearrange("(c p) e -> p c e", p=128)

        def phase_ada(l):
            w_ada = inp(f"w_ada{l}", [D, 12 * 1024])
            psm = bank(7)
            cvs3 = CVS[:].rearrange("p (c w) -> p c w", w=2)
            for s in range(24):
                view, tok, per = load_slab(wsrc(w_ada, D, s * 512, 512), 16, 512, 3, 4096)
                for ec in range(4):
                    col = (s * 4 + ec) * 2
                    for k in range(16):
                        P.add("pe", lambda e, view=view, ec=ec, k=k, col=col: e.matmul(
                            psm[:, col:col + 2], lhsT=view[:, k, ec * 128:(ec + 1) * 128], rhs=cvs3[:, k, :],
                            start=(k == 0), stop=(k == 15)),
                            reads=[f"{tok}.{k // per}", "cvs"], writes=["ps7"])
            mod = MOD[:, l * 192:(l + 1) * 192]
            mod3 = mod.rearrange("p (j w) -> p j w", w=2)
            bada = vec(l, V_BADA, 96).unsqueeze(2).broadcast_to([128, 96, 2])
            P.add("dve", lambda e: e.tensor_tensor(out=mod3, in0=psm[:, 0:192].rearrange("p (j w) -> p j w", w=2), in1=bada, op=ALU.add),
                  reads=["ps7", "vec"], writes=["mod"])
            cf = COEF[:, l * 192:(l + 1) * 192].rearrange("p (k c w) -> p k c w", k=6, w=2)

            def g2(col):
                return vec(l, col, 16).unsqueeze(2).broadcast_to([128, 16, 2])

            def m3(j):
                return mod3[:, j * 16:(j + 1) * 16, :]
            for kind, jsc, gcol in ((0, 1, V_GPRE1), (3, 4, V_GPRE2)):
                P.add("dve", lambda e, kind=kind, jsc=jsc: e.tensor_scalar_add(out=cf[:, kind], in0=m3(jsc), scalar1=1.0),
                      reads=["mod"], writes=["coef"])
                P.add("dve", lambda e, kind=kind, gcol=gcol: e.tensor_tensor(out=cf[:, kind], in0=cf[:, kind], in1=g2(gcol), op=ALU.mult),
                      reads=["coef", "vec"], writes=["coef"])
            for kind, j in ((1, 0), (4, 3)):
                P.add("dve", lambda e, kind=kind, j=j: e.tensor_copy(out=cf[:, kind], in_=m3(j)), reads=["mod"], writes=["coef"])
            for kind, j, gcol in ((2, 2, V_GPOST1), (5, 5, V_GPOST2)):
                P.add("dve", lambda e, kind=kind, j=j, gcol=gcol: e.tensor_tensor(out=cf[:, kind], in0=m3(j), in1=g2(gcol), op=ALU.mult),
                      reads=["mod", "vec"], writes=["coef"])
            P.barrier()

        def phase_rn(l, x_src, x_dst, u_src, ckind, nkind, ntok, out_final=False, norm_ntok=None):
            GW = 256
            res3 = RESb[:].rearrange("p (c t) -> p c t", c=NCH)
            ng_all = (norm_ntok if norm_ntok is not None else ntok) // GW
            ng_u = ntok // GW
            for g in range(max(ng_all, ng_u)):
                t0 = g * GW
                w = 1 if t0 >= TL else 0
                pb = g % 2
                base = pb * 10752
                uw = arf(base, 4096).rearrange("p (c t) -> p c t", c=NCH)
                xw = arf(base + 4096, 4096).rearrange("p (c t) -> p c t", c=NCH)
                sqb = arb(base + 8192, 2048).rearrange("p (c t) -> p c t", c=NCH)
                r1 = arf(base + 10240, 256)
                r2 = arf(base + 10496, 256)
                tk = f"rn{pb}"
                xsrc3 = x_src[:, t0:t0 + GW].rearrange("(c p) t -> p c t", p=128)
                P.add("sp", lambda e, xw=xw, xsrc3=xsrc3: e.dma_start(out=xw, in_=xsrc3), writes=[tk + "x"], chan=tk + "x")
                has_u = u_src is not None and g < ng_u
                if has_u:
                    usrc3 = u_src[:, t0:t0 + GW].rearrange("(c p) t -> p c t", p=128)
                    P.add("sp", lambda e, uw=uw, usrc3=usrc3: e.dma_start(out=uw, in_=usrc3), writes=[tk + "u"], chan=tk + "u")
                    P.add("act", lambda e, sqb=sqb, uw=uw: e.activation(out=sqb, in_=uw, func=AF.Square), reads=[tk + "u"], writes=[tk + "sq"])
                    sb = 4 + pb
                    for c in range(NCH):
                        P.add("pe", lambda e, sb=sb, sqb=sqb, c=c: e.matmul(bank(sb)[:, 0:GW], lhsT=ONES[:], rhs=sqb[:, c, :], start=(c == 0), stop=(c == NCH - 1)),
                              reads=[tk + "sq", "ones"], writes=[f"ps{sb}"])
                    P.add("act", lambda e, sb=sb, r1=r1: e.activation(out=r1, in_=bank(sb)[:, 0:GW], func=AF.Sqrt, scale=1.0 / D, bias=EPS),
                          reads=[f"ps{sb}"], writes=[tk + "r1"])
                    P.add("dve", lambda e, r1=r1: e.reciprocal(out=r1, in_=r1), reads=[tk + "r1"], writes=[tk + "r1"])
                    for c in range(NCH):
                        P.add("dve", lambda e, uw=uw, r1=r1, c=c, w=w: e.scalar_tensor_tensor(
                            out=uw[:, c, :], in0=uw[:, c, :], scalar=coef(l, ckind, c, w), in1=r1, op0=ALU.mult, op1=ALU.mult),
                            reads=[tk + "u", tk + "r1", "coef"], writes=[tk + "u"])
                    P.add("pool", lambda e, xw=xw, uw=uw: e.tensor_tensor(out=xw, in0=xw, in1=uw, op=ALU.add), reads=[tk + "x", tk + "u"], writes=[tk + "x"])
                    if out_final:
                        if t0 < TL:
                            dst3 = OUT[:, t0:t0 + GW].rearrange("(c p) t -> p c t", p=128)
                            P.add("sp", lambda e, xw=xw, dst3=dst3: e.dma_start(out=dst3, in_=xw), reads=[tk + "x"], writes=["outd"], chan=tk + "o")
                    else:
                        dst3 = x_dst[:, t0:t0 + GW].rearrange("(c p) t -> p c t", p=128)
                        P.add("sp", lambda e, xw=xw, dst3=dst3: e.dma_start(out=dst3, in_=xw), reads=[tk + "x"], writes=[f"xd{g}"], chan=tk + "o")
                if nkind is not None and g < ng_all:
                    gk, sk, nl = nkind
                    P.add("act", lambda e, sqb=sqb, xw=xw: e.activation(out=sqb, in_=xw, func=AF.Square), reads=[tk + "x"], writes=[tk + "sq"])
                    sb = 6 + pb
                    for c in range(NCH):
                        P.add("pe", lambda e, sb=sb, sqb=sqb, c=c: e.matmul(bank(sb)[:, 0:GW], lhsT=ONES[:], rhs=sqb[:, c, :], start=(c == 0), stop=(c == NCH - 1)),
                              reads=[tk + "sq", "ones"], writes=[f"ps{sb}"])
                    P.add("act", lambda e, sb=sb, r2=r2: e.activation(out=r2, in_=bank(sb)[:, 0:GW], func=AF.Sqrt, scale=1.0 / D, bias=EPS),
                          reads=[f"ps{sb}"], writes=[tk + "r2"])
                    P.add("dve", lambda e, r2=r2: e.reciprocal(out=r2, in_=r2), reads=[tk + "r2"], writes=[tk + "r2"])
                    r2b = r2.unsqueeze(1).broadcast_to([128, NCH, GW])
                    P.add("dve", lambda e, uw=uw, xw=xw, r2b=r2b: e.tensor_tensor(out=uw, in0=xw, in1=r2b, op=ALU.mult),
                          reads=[tk + "x", tk + "r2", tk + "u"], writes=[tk + "u"])
                    for c in range(NCH):
                        P.add("act", lambda e, uw=uw, c=c, w=w, t0=t0, gk=gk, sk=sk, nl=nl: e.activation(
                            out=res3[:, c, t0:t0 + GW], in_=uw[:, c, :], func=AF.Identity, scale=coef(nl, gk, c, w), bias=coef(nl, sk, c, w)),
                            reads=[tk + "u", "coef"], writes=[f"res.{t0 // 512}"])
            P.barrier()

        def tgroups(ntok):
            gs = []
            t0 = 0
            while t0 < ntok:
                n = min(512, ntok - t0)
                gs.append((t0, n))
                t0 += n
            return gs

        gemm_bank = {"n": 0}
        deferred = []

        def run_deferred():
            cur = list(deferred)
            del deferred[:]
            for f in cur:
                f()

        def gemm_fm(slabs, kc, rhs_fn, rhs_tok_fn, tgs, evac, post_ec=None, nslots=3, slot_words=4096, wbase=0, nbanks=4, order="ec"):
            for si, (src3, ncols, tag) in enumerate(slabs):
                view, tok, per = load_slab(src3, kc, ncols, nslots, slot_words, base=wbase)
                necs = (ncols + 127) // 128
                for ec in range(necs):
                    m = min(128, ncols - ec * 128)
                    for gi, (t0, n) in enumerate(tgs):
                        b = gemm_bank["n"] % nbanks
                        gemm_bank["n"] += 1
                        for k in range(kc):
                            P.add("pe", lambda e, b=b, m=m, n=n, view=view, k=k, ec=ec, gi=gi: e.matmul(
                                bank(b)[0:m, 0:n], lhsT=view[:, k, ec * 128:ec * 128 + m], rhs=rhs_fn(k, gi),
                                start=(k == 0), stop=(k == kc - 1)),
                                reads=[f"{tok}.{k // per}", rhs_tok_fn(k, gi)], writes=[f"ps{b}"])
                        run_deferred()
                        evac(tag, si, ec, gi, t0, n, m, b)
                    if post_ec is not None:
                        post_ec(tag, si, ec, m)
            run_deferred()
            run_deferred()

        def gemm_tm(slabs, kc, lhs_fn, lhs_tok_fn, ntok, evac, nslots=3, slot_words=4096, wbase=0, nbanks=4):
            for si, (src3, ncols, tag) in enumerate(slabs):
                view, tok, per = load_slab(src3, kc, ncols, nslots, slot_words, base=wbase)
                for tt in range(ntok // 128):
                    b = gemm_bank["n"] % nbanks
                    gemm_bank["n"] += 1
                    for k in range(kc):
                        P.add("pe", lambda e, b=b, view=view, k=k, tt=tt, ncols=ncols: e.matmul(
                            bank(b)[:, 0:ncols], lhsT=lhs_fn(k, tt), rhs=view[:, k, 0:ncols], start=(k == 0), stop=(k == kc - 1)),
                            reads=[f"{tok}.{k // per}", lhs_tok_fn(tt)], writes=[f"ps{b}"])
                    evac(tag, si, tt, ncols, b)

        res16 = RESb[:].rearrange("p (c t) -> p c t", c=NCH)

        stg_n = {"n": 0}

        def copy_alt(i, out, in_, reads, writes):
            if i % 2 == 0:
                P.add("act", lambda e: e.activation(out=out, in_=in_, func=AF.Copy), reads=reads, writes=writes)
            else:
                P.add("dve", lambda e: e.tensor_copy(out=out, in_=in_), reads=reads, writes=writes)

        def phase_inproj(l):
            w_in = inp(f"w_in{l}", [D, IN_W])
            STG = [arb(12288 + i * 1152, 1152) for i in range(3)]
            cosB = arf(15744, 2048)
            sinB = arf(17792, 2048)
            sqt = [arb(19840 + i * 256, 256) for i in range(2)]
            rt = [arf(20352 + i * 512, 512) for i in range(2)]
            qn32 = [arf(21376 + i * 512, 512) for i in range(2)]
            qnb = [arb(22400 + i * 256, 256) for i in range(2)]
            t1 = [arf(22912 + i * 512, 512) for i in range(2)]
            cqtmp = arf(23936, 2048).rearrange("p (c t) -> p c t", c=4)
            STG32 = [arf(25984 + i * 2304, 2304) for i in range(2)]
            P.add("sp", lambda e: e.dma_start(out=cosB, in_=ropeB[0]), writes=["cosB"], chan="k0")
            P.add("sp", lambda e: e.dma_start(out=sinB, in_=ropeB[1]), writes=["sinB"], chan="k1")
            tgs = tgroups(T)
            rhs_fn = lambda k, gi: res16[:, k, tgs[gi][0]:tgs[gi][0] + tgs[gi][1]]
            rhs_tok = lambda k, gi: f"res.{gi}"
            cnt = {"i": 0, "q": 0, "c": 0}
            cur = {}

            def stage_slot():
                s = stg_n["n"] % 3
                stg_n["n"] += 1
                return s

            def mk_plain(dst, row0_fn, func):
                def evac(tag, si, ec, gi, t0, n, m, b):
                    if gi == 0:
                        cur["s"] = stage_slot()
                    s = cur["s"]
                    i = cnt["i"]
                    cnt["i"] += 1
                    if func is None:
                        copy_alt(i, STG[s][0:m, t0:t0 + n], bank(b)[0:m, 0:n], [f"ps{b}"], [f"stg{s}.{gi}"])
                    else:
                        P.add("act", lambda e: e.activation(out=STG[s][0:m, t0:t0 + n], in_=bank(b)[0:m, 0:n], func=func),
                              reads=[f"ps{b}"], writes=[f"stg{s}.{gi}"])

                def post(tag, si, ec, m):
                    s = cur["s"]
                    r0 = row0_fn(si, ec)
                    P.add("sp", lambda e: e.dma_start(out=dst[r0:r0 + m, :], in_=STG[s][0:m, :]),
                          reads=[f"stg{s}.{gi}" for gi in range(len(tgs))], writes=[f"d.{id(dst)}.{r0}"], chan=f"st{s}")
                return evac, post

            def slabs_of(c0, width):
                out = []
                o = 0
                while o < width:
                    n = min(512, width - o)
                    out.append((wsrc(w_in, D, c0 + o, n), n, o))
                    o += n
                return out

            ev, po = mk_plain(QKA, lambda si, ec: si * 512 + ec * 128, None)
            gemm_fm(slabs_of(O_AQ, 2048), NCH, rhs_fn, rhs_tok, tgs, ev, po)

            def mk_tm(dst, c0_fn):
                def evac(tag, si, tt, ncols, b):
                    s = stage_slot()
                    i = cnt["i"]
                    cnt["i"] += 1
                    copy_alt(i, STG[s][:, 0:ncols], bank(b)[:, 0:ncols], [f"ps{b}"], [f"stg{s}.0"])
                    c0 = c0_fn(si)
                    P.add("sp", lambda e: e.dma_start(out=dst[tt * 128:(tt + 1) * 128, c0:c0 + ncols], in_=STG[s][:, 0:ncols]),
                          reads=[f"stg{s}.0"], writes=[f"d.{id(dst)}.{tt}.{c0}"], chan=f"st{s}")
                return evac
            lhs_fn = lambda k, tt: res16[:, k, tt * 128:(tt + 1) * 128]
            lhs_tok = lambda tt: f"res.{tt // 4}"
            gemm_tm(slabs_of(O_AV, 1024), NCH, lhs_fn, lhs_tok, T, mk_tm(VA, lambda si: si * 512))

            def mk_qk(dst, gcol):
                def evac(tag, si, ec, gi, t0, n, m, b):
                    if gi == 0:
                        cur["s"] = stage_slot()
                    s = cur["s"]
                    q = cnt["q"] % 2
                    cnt["q"] += 1
                    P.add("act", lambda e: e.activation(out=sqt[q][:, 0:n], in_=bank(b)[:, 0:n], func=AF.Square),
                          reads=[f"ps{b}"], writes=[f"sqt{q}"])
                    sb = 4 + q

                    def st1():
                        P.add("pe", lambda e: e.matmul(bank(sb)[:, 0:n], lhsT=ONES[:], rhs=sqt[q][:, 0:n], start=True, stop=True),
                              reads=[f"sqt{q}", "ones"], writes=[f"ps{sb}"])
                        P.add("act", lambda e: e.activation(out=rt[q][:, 0:n], in_=bank(sb)[:, 0:n], func=AF.Sqrt, scale=1.0 / 128, bias=EPS),
                              reads=[f"ps{sb}"], writes=[f"rt{q}"])
                        P.add("dve", lambda e: e.reciprocal(out=rt[q][:, 0:n], in_=rt[q][:, 0:n]), reads=[f"rt{q}"], writes=[f"rt{q}"])
                        P.add("dve", lambda e: e.scalar_tensor_tensor(out=qn32[q][:, 0:n], in0=bank(b)[:, 0:n], scalar=vec(l, gcol), in1=rt[q][:, 0:n],
                                                                      op0=ALU.mult, op1=ALU.mult),
                              reads=[f"ps{b}", f"rt{q}", "vec"], writes=[f"qn32{q}"])
                        if t0 >= TL:
                            P.add("act", lambda e: e.activation(out=STG[s][:, t0:t0 + n], in_=qn32[q][:, 0:n], func=AF.Copy),
                                  reads=[f"qn32{q}"], writes=[f"stg{s}.{gi}"])
                            return
                        P.add("act", lambda e: e.activation(out=qnb[q][:, 0:n], in_=qn32[q][:, 0:n], func=AF.Copy),
                              reads=[f"qn32{q}"], writes=[f"qnb{q}"])
                        rb = 6 + q

                        def st2():
                            P.add("pe", lambda e: e.matmul(bank(rb)[:, 0:n], lhsT=ROTB[:], rhs=qnb[q][:, 0:n], start=True, stop=True),
                                  reads=[f"qnb{q}", "rotb"], writes=[f"ps{rb}"])
                            P.add("dve", lambda e: e.tensor_tensor(out=t1[q][:, 0:n], in0=bank(rb)[:, 0:n], in1=sinB[:, t0:t0 + n], op=ALU.mult),
                                  reads=[f"ps{rb}", "sinB"], writes=[f"t1{q}"])
                            P.add("pool", lambda e: e.tensor_tensor(out=qn32[q][:, 0:n], in0=qn32[q][:, 0:n], in1=cosB[:, t0:t0 + n], op=ALU.mult),
                                  reads=[f"qn32{q}", "cosB"], writes=[f"qn32{q}"])
                            P.add("dve", lambda e: e.tensor_tensor(out=STG[s][:, t0:t0 + n], in0=qn32[q][:, 0:n], in1=t1[q][:, 0:n], op=ALU.add),
                                  reads=[f"qn32{q}", f"t1{q}"], writes=[f"stg{s}.{gi}"])
                        deferred.append(st2)
                    deferred.append(st1)

                def post(tag, si, ec, m):
                    s = cur["s"]
                    r0 = si * 512 + ec * 128

                    def do():
                        P.add("sp", lambda e: e.dma_start(out=dst[r0:r0 + m, :], in_=STG[s][0:m, :]),
                              reads=[f"stg{s}.{gi}" for gi in range(len(tgs))], writes=[f"d.{id(dst)}.{r0}"], chan=f"st{s}")
                    deferred.append(lambda: deferred.append(lambda: deferred.append(do)))
                return evac, post
            ev, po = mk_qk(QB, V_QN)
            gemm_fm(slabs_of(O_BQ, 1024), NCH, rhs_fn, rhs_tok, tgs, ev, po)
            run_deferred(); run_deferred()
            ev, po = mk_qk(KB, V_KN)
            gemm_fm(slabs_of(O_BK, 256), NCH, rhs_fn, rhs_tok, tgs, ev, po)
            run_deferred(); run_deferred()
            gemm_tm(slabs_of(O_BV, 256), NCH, lhs_fn, lhs_tok, T, mk_tm(VB, lambda si: 0))

            def mla_c(c0, row_base, gcol):
                src3 = wsrc(w_in, D, c0, 512)
                view, tok, per = load_slab(src3, NCH, 512, 3, 4096)
                ss = [stage_slot() for _ in range(4)]
                for gi, (t0, n) in enumerate(tgs):
                    sb = 4 + (gi % 2)
                    for ec in range(4):
                        b = gemm_bank["n"] % 4
                        gemm_bank["n"] += 1
                        for k in range(NCH):
                            P.add("pe", lambda e, b=b, n=n, k=k, ec=ec, gi=gi: e.matmul(
                                bank(b)[:, 0:n], lhsT=view[:, k, ec * 128:(ec + 1) * 128], rhs=rhs_fn(k, gi), start=(k == 0), stop=(k == NCH - 1)),
                                reads=[f"{tok}.{k // per}", rhs_tok(k, gi)], writes=[f"ps{b}"])
                        q = cnt["q"] % 2
                        cnt["q"] += 1
                        P.add("act", lambda e, b=b, n=n, ec=ec: e.activation(out=cqtmp[:, ec, 0:n], in_=bank(b)[:, 0:n], func=AF.Copy),
                              reads=[f"ps{b}"], writes=[f"cqt{ec}"])
                        P.add("act", lambda e, n=n, ec=ec, q=q: e.activation(out=sqt[q][:, 0:n], in_=cqtmp[:, ec, 0:n], func=AF.Square),
                              reads=[f"cqt{ec}"], writes=[f"sqt{q}"])
                        P.add("pe", lambda e, sb=sb, n=n, q=q, ec=ec: e.matmul(bank(sb)[:, 0:n], lhsT=ONES[:], rhs=sqt[q][:, 0:n], start=(ec == 0), stop=(ec == 3)),
                              reads=[f"sqt{q}", "ones"], writes=[f"ps{sb}"])
                    P.add("act", lambda e, sb=sb, n=n: e.activation(out=rt[0][:, 0:n], in_=bank(sb)[:, 0:n], func=AF.Sqrt, scale=1.0 / 512, bias=EPS),
                          reads=[f"ps{sb}"], writes=["rt0"])
                    P.add("dve", lambda e, n=n: e.reciprocal(out=rt[0][:, 0:n], in_=rt[0][:, 0:n]), reads=["rt0"], writes=["rt0"])
                    for ec in range(4):
                        P.add("dve", lambda e, n=n, ec=ec, t0=t0: e.scalar_tensor_tensor(
                            out=STG[ss[ec]][:, t0:t0 + n], in0=cqtmp[:, ec, 0:n], scalar=vec(l, gcol + ec), in1=rt[0][:, 0:n], op0=ALU.mult, op1=ALU.mult),
                            reads=[f"cqt{ec}", "rt0", "vec"], writes=[f"stg{ss[ec]}.{gi}"])
                for ec in range(4):
                    r0 = row_base + ec * 128
                    P.add("sp", lambda e, r0=r0, ec=ec: e.dma_start(out=CQN[r0:r0 + 128, :], in_=STG[ss[ec]][:, :]),
                          reads=[f"stg{ss[ec]}.{gi}" for gi in range(len(tgs))], writes=[f"d.cqn.{r0}"], chan=f"st{ss[ec]}")
            STG.append(STG32[0].bitcast(BF16)[:, 0:T])
            _orig_slot = stage_slot

            def stage_slot4():
                s = stg_n["n"] % 4
                stg_n["n"] += 1
                return s
            stage_slot = stage_slot4
            mla_c(O_CQ, 0, V_MQN)
            mla_c(O_CKV, 512, V_MKVN)
            P.barrier()
            stage_slot = _orig_slot
            stg_n["n"] = 0

            def ev_ckr(tag, si, ec, gi, t0, n, m, b):
                P.add("act", lambda e: e.activation(out=STG32[1][0:m, t0:t0 + n], in_=bank(b)[0:m, 0:n], func=AF.Copy),
                      reads=[f"ps{b}"], writes=[f"s32.{gi}"])

            def po_ckr(tag, si, ec, m):
                P.add("sp", lambda e: e.dma_start(out=CKR[:, :], in_=STG32[1][0:64, :]),
                      reads=[f"s32.{gi}" for gi in range(len(tgs))], writes=["d.ckr"], chan="st32")
            gemm_fm(slabs_of(O_CKR, 64), NCH, rhs_fn, rhs_tok, tgs, ev_ckr, po_ckr)

            ev, po = mk_plain(SG, lambda si, ec: si * 512 + ec * 128, AF.Sigmoid)
            gemm_fm(slabs_of(O_GA, 6144), NCH, rhs_fn, rhs_tok, tgs, ev, po)
            P.barrier()

        def phase_mla(l):
            w_uq = inp(f"w_uq{l}", [512, 1536])
            w_ukv = inp(f"w_ukv{l}", [512, 2048])
            STG = [arb(12288 + i * 1152, 1152) for i in range(3)]
            cosC = arf(15744, 2048)
            sinC = arf(17792, 2048)
            qr32 = [arf(19840 + i * 512, 512) for i in range(2)]
            qrb = [arb(20864 + i * 256, 256) for i in range(2)]
            t1 = [arf(21376 + i * 512, 512) for i in range(2)]
            ckr = arf(22400, 2304)
            krp = arb(24704, 1152)
            P.add("sp", lambda e: e.dma_start(out=cosC[0:64, :], in_=ropeC[0]), writes=["cosC"], chan="k0")
            P.add("sp", lambda e: e.dma_start(out=sinC[0:64, :], in_=ropeC[1]), writes=["sinC"], chan="k1")
            res4 = RESb[:, 0:4 * T].rearrange("p (c t) -> p c t", c=4)
            tgs = tgroups(T)
            rhs_fn = lambda k, gi: res4[:, k, tgs[gi][0]:tgs[gi][0] + tgs[gi][1]]
            rhs_tok = lambda k, gi: "res"
            cnt = {"i": 0, "q": 0}
            cur = {}

            def stage_slot():
                s = stg_n["n"] % 3
                stg_n["n"] += 1
                return s

            def rope64(src_ap, src_reads, n, t0, out_ap, out_writes, q):
                if t0 >= TL:
                    P.add("act", lambda e: e.activation(out=out_ap, in_=src_ap, func=AF.Copy), reads=src_reads, writes=out_writes)
                    return
                P.add("act", lambda e: e.activation(out=qrb[q][0:64, 0:n], in_=src_ap, func=AF.Copy), reads=src_reads, writes=[f"qrb{q}"])
                rb = 6 + q
                P.add("pe", lambda e: e.matmul(bank(rb)[0:64, 0:n], lhsT=ROTC[:], rhs=qrb[q][0:64, 0:n], start=True, stop=True),
                      reads=[f"qrb{q}", "rotc"], writes=[f"ps{rb}"])
                P.add("dve", lambda e: e.tensor_tensor(out=t1[q][0:64, 0:n], in0=bank(rb)[0:64, 0:n], in1=sinC[0:64, t0:t0 + n], op=ALU.mult),
                      reads=[f"ps{rb}", "sinC"], writes=[f"t1{q}"])
                P.add("pool", lambda e: e.tensor_tensor(out=src_ap, in0=src_ap, in1=cosC[0:64, t0:t0 + n], op=ALU.mult),
                      reads=src_reads + ["cosC"], writes=src_reads)
                P.add("dve", lambda e: e.tensor_tensor(out=out_ap, in0=src_ap, in1=t1[q][0:64, 0:n], op=ALU.add),
                      reads=src_reads + [f"t1{q}"], writes=out_writes)

            P.add("sp", lambda e: e.dma_start(out=ckr[0:64, :], in_=CKR[:, :]), writes=["ckr"], chan="k2")
            for gi, (t0, n) in enumerate(tgs):
                q = gi % 2
                P.add("act", lambda e, q=q, n=n, t0=t0: e.activation(out=qr32[q][0:64, 0:n], in_=ckr[0:64, t0:t0 + n], func=AF.Copy),
                      reads=["ckr"], writes=[f"qr32{q}"])
                rope64(qr32[q][0:64, 0:n], [f"qr32{q}"], n, t0, krp[0:64, t0:t0 + n], [f"krp.{gi}"], q)
            P.add("sp", lambda e: e.dma_start(out=KRP[:, :], in_=krp[0:64, :]), reads=[f"krp.{gi}" for gi in range(len(tgs))], writes=["d.krp"], chan="k3")

            for j in range(4):
                P.add("sp", lambda e, j=j: e.dma_start(out=res4[:, j, :], in_=CQN[j * 128:(j + 1) * 128, :]), writes=["res"], chan=f"k{4 + j}")

            def ev_q(tag, si, ec, gi, t0, n, m, b):
                h, part = tag
                if gi == 0:
                    cur["s"] = stage_slot()
                s = cur["s"]
                if part == 0:
                    i = cnt["i"]
                    cnt["i"] += 1
                    copy_alt(i, STG[s][:, t0:t0 + n], bank(b)[:, 0:n], [f"ps{b}"], [f"stg{s}.{gi}"])
                else:
                    q = cnt["q"] % 2
                    cnt["q"] += 1
                    P.add("act", lambda e: e.activation(out=qr32[q][0:64, 0:n], in_=bank(b)[0:64, 0:n], func=AF.Copy),
                          reads=[f"ps{b}"], writes=[f"qr32{q}"])

                    def later():
                        rope64(qr32[q][0:64, 0:n], [f"qr32{q}"], n, t0, STG[s][0:64, t0:t0 + n], [f"stg{s}.{gi}"], q)
                    deferred.append(later)

            def po_q(tag, si, ec, m):
                h, part = tag
                s = cur["s"]
                dst = QCN[h * 128:(h + 1) * 128, :] if part == 0 else QCR[h * 64:(h + 1) * 64, :]

                def do():
                    P.add("sp", lambda e: e.dma_start(out=dst, in_=STG[s][0:m, :]),
                          reads=[f"stg{s}.{gi}" for gi in range(len(tgs))], writes=[f"d.q{h}.{part}"], chan=f"st{s}")
                deferred.append(lambda: deferred.append(do))

            def small_gemm(wap, ncols_total, col_specs, evac, post):
                view, tok, per = load_slab(wsrc(wap, 512, 0, ncols_total), 4, ncols_total, 3, 4096)
                for (c0, m, tag) in col_specs:
                    for gi, (t0, n) in enumerate(tgs):
                        b = gemm_bank["n"] % 4
                        gemm_bank["n"] += 1
                        for k in range(4):
                            P.add("pe", lambda e, b=b, m=m, n=n, k=k, c0=c0, gi=gi: e.matmul(
                                bank(b)[0:m, 0:n], lhsT=view[:, k, c0:c0 + m], rhs=rhs_fn(k, gi), start=(k == 0), stop=(k == 3)),
                                reads=[f"{tok}.{k // per}", "res"], writes=[f"ps{b}"])
                        run_deferred()
                        evac(tag, 0, 0, gi, t0, n, m, b)
                    post(tag, 0, 0, m)
                run_deferred(); run_deferred(); run_deferred()
                return view, tok, per
            specs = []
            for h in range(8):
                specs.append((h * 192, 128, (h, 0)))
                specs.append((h * 192 + 128, 64, (h, 1)))
            small_gemm(w_uq, 1536, specs, ev_q, po_q)
            P.barrier()

            for j in range(4):
                P.add("sp", lambda e, j=j: e.dma_start(out=res4[:, j, :], in_=CQN[512 + j * 128:512 + (j + 1) * 128, :]), writes=["res"], chan=f"k{4 + j}")

            def ev_k(tag, si, ec, gi, t0, n, m, b):
                if gi == 0:
                    cur["s"] = stage_slot()
                s = cur["s"]
                i = cnt["i"]
                cnt["i"] += 1
                copy_alt(i, STG[s][:, t0:t0 + n], bank(b)[:, 0:n], [f"ps{b}"], [f"stg{s}.{gi}"])

            def po_k(tag, si, ec, m):
                h = tag
                s = cur["s"]
                P.add("sp", lambda e: e.dma_start(out=KCN[h * 128:(h + 1) * 128, :], in_=STG[s][:, :]),
                      reads=[f"stg{s}.{gi}" for gi in range(len(tgs))], writes=[f"d.k{h}"], chan=f"st{s}")
            view, tok, per = small_gemm(w_ukv, 2048, [(h * 256, 128, h) for h in range(8)], ev_k, po_k)
            for tt in range(T // 128):
                s = stage_slot()
                for h in range(8):
                    b = gemm_bank["n"] % 4
                    gemm_bank["n"] += 1
                    for k in range(4):
                        P.add("pe", lambda e, b=b, k=k, h=h, tt=tt: e.matmul(
                            bank(b)[:, 0:128], lhsT=res4[:, k, tt * 128:(tt + 1) * 128], rhs=view[:, k, h * 256 + 128:h * 256 + 256],
                            start=(k == 0), stop=(k == 3)),
                            reads=[f"{tok}.{k // per}", "res"], writes=[f"ps{b}"])
                    i = cnt["i"]
                    cnt["i"] += 1
                    copy_alt(i, STG[s][:, h * 128:(h + 1) * 128], bank(b)[:, 0:128], [f"ps{b}"], [f"stg{s}.{h}"])
                P.add("sp", lambda e, s=s, tt=tt: e.dma_start(out=VC[tt * 128:(tt + 1) * 128, :], in_=STG[s][:, 0:1024]),
                      reads=[f"stg{s}.{h}" for h in range(8)], writes=[f"d.vc{tt}"], chan=f"st{s}")
            P.barrier()

        def phase_attn(l, ctx_q):
            abias = inp(f"abias{l}", [8 * 20 * 128, 512])
            Vall = arb(0, 9216)
            Kb = [arb(9216 + i * 1152, 1152) for i in range(2)]
            Qb = [arb(11520 + i * 1152, 1152) for i in range(2)]
            KR = arb(13824, 1152)
            QR = [arb(14976 + i * 1152, 1152) for i in range(2)]
            PT = [arb(17280 + i * 256, 256) for i in range(4)]
            BI = [arf(18304 + i * 512, 512) for i in range(3)]
            SS = [arf(19840 + i * 512, 512) for i in range(2)]
            RC = [arf(20864 + i * 512, 512) for i in range(2)]
            OS = [arb(21888 + i * 1152, 1152) for i in range(2)]
            ctr = {"step": 0, "blk": 0, "bi": 0, "os": 0, "kq": 0}
            pend = []

            def flush(depth):
                while len(pend) > depth:
                    pend.pop(0)()

            def qblocks():
                bl = [(g * 512, 512, g) for g in range(4)]
                if ctx_q:
                    bl.append((TL, 256, 4))
                return bl

            def run_head(mixer, h, kT, kT_tok, kR, qT, qT_tok, qR, v_fn, scale, orow, os_i):
                for (q0, n, g) in qblocks():
                    if g == 4:
                        chunks = [16, 17]
                    elif mixer == "a":
                        chunks = list(_a_chunks(g)) + [16, 17]
                    else:
                        chunks = list(range(18))
                    blk = ctr["blk"] % 2
                    ctr["blk"] += 1
                    ob, db = 3 + blk, 5 + blk
                    nck = len(chunks)
                    for ci, m in enumerate(chunks):
                        sbk = ctr["step"] % 3
                        pt = ctr["step"] % 4
                        ctr["step"] += 1
                        P.add("pe", lambda e, sbk=sbk, m=m, q0=q0, n=n: e.matmul(
                            bank(sbk)[:, 0:n], lhsT=kT[:, m * 128:(m + 1) * 128], rhs=qT[:, q0:q0 + n], start=True, stop=(kR is None)),
                            reads=[kT_tok, qT_tok], writes=[f"ps{sbk}"])
                        if kR is not None:
                            P.add("pe", lambda e, sbk=sbk, m=m, q0=q0, n=n: e.matmul(
                                bank(sbk)[:, 0:n], lhsT=kR[0:64, m * 128:(m + 1) * 128], rhs=qR[0:64, q0:q0 + n], start=False, stop=True),
                                reads=["kr", qT_tok], writes=[f"ps{sbk}"])
                        if mixer == "a" and m < 16:
                            bi = ctr["bi"] % 3
                            ctr["bi"] += 1
                            tid = (h * 20 + _abias_tile_id(g, m)) * 128
                            P.add("sp", lambda e, bi=bi, tid=tid: e.dma_start(out=BI[bi], in_=abias[tid:tid + 128, :]), writes=[f"bi{bi}"], chan=f"bi{bi}")
                            ss = ctr["step"] % 2
                            P.add("dve", lambda e, ss=ss, sbk=sbk, bi=bi, n=n: e.scalar_tensor_tensor(
                                out=SS[ss][:, 0:n], in0=bank(sbk)[:, 0:n], scalar=scale, in1=BI[bi][:, 0:n], op0=ALU.mult, op1=ALU.add),
                                reads=[f"ps{sbk}", f"bi{bi}"], writes=[f"ss{ss}"])
                            P.add("act", lambda e, ss=ss, pt=pt, n=n: e.activation(out=PT[pt][:, 0:n], in_=SS[ss][:, 0:n], func=AF.Exp),
                                  reads=[f"ss{ss}"], writes=[f"pt{pt}"])
                        else:
                            P.add("act", lambda e, sbk=sbk, pt=pt, n=n: e.activation(out=PT[pt][:, 0:n], in_=bank(sbk)[:, 0:n], func=AF.Exp, scale=scale),
                                  reads=[f"ps{sbk}"], writes=[f"pt{pt}"])

                        def pv(ci=ci, m=m, pt=pt, n=n, ob=ob, db=db, nck=nck, q0=q0, blk=blk):
                            P.add("pe", lambda e: e.matmul(bank(ob)[:, 0:n], lhsT=v_fn(m), rhs=PT[pt][:, 0:n], start=(ci == 0), stop=(ci == nck - 1)),
                                  reads=[f"pt{pt}", f"vall.{m // 6}"], writes=[f"ps{ob}"])
                            P.add("pe", lambda e: e.matmul(bank(db)[:, 0:n], lhsT=ONES[:], rhs=PT[pt][:, 0:n], start=(ci == 0), stop=(ci == nck - 1)),
                                  reads=[f"pt{pt}", "ones"], writes=[f"ps{db}"])
                            if ci == nck - 1:
                                P.add("dve", lambda e: e.reciprocal(out=RC[blk][:, 0:n], in_=bank(db)[:, 0:n]), reads=[f"ps{db}"], writes=[f"rc{blk}"])
                                P.add("dve", lambda e: e.tensor_tensor(out=OS[os_i][:, q0:q0 + n], in0=bank(ob)[:, 0:n], in1=RC[blk][:, 0:n], op=ALU.mult),
                                      reads=[f"ps{ob}", f"rc{blk}"], writes=[f"os{os_i}.{q0}"])
                        pend.append(pv)
                        flush(2)
                flush(0)
                ntok = T if ctx_q else TL
                P.add("sp", lambda e: e.dma_start(out=OT[orow:orow + 128, 0:ntok], in_=OS[os_i][:, 0:ntok]),
                      reads=[f"os{os_i}.{q0}" for (q0, n, g) in qblocks()], writes=[f"d.ot{orow}"], chan=f"os{os_i}")

            def load_row(buf, src, tokname, chan, rows=128):
                P.add("sp", lambda e: e.dma_start(out=buf[0:rows, :], in_=src), writes=[tokname], chan=chan)

            def load_v(src, width):
                v3 = Vall[:, 0:18 * width].rearrange("p (c d) -> p c d", c=18)
                s3 = src.rearrange("(c p) d -> p c d", p=128)
                for j in range(3):
                    P.add("sp", lambda e, j=j: e.dma_start(out=v3[:, 6 * j:6 * j + 6, :], in_=s3[:, 6 * j:6 * j + 6, :]),
                          writes=[f"vall.{j}"], chan=f"v{j}")
                return v3

            v3 = load_v(VA, 1024)
            for h in range(8):
                i = ctr["kq"] % 2
                ctr["kq"] += 1
                load_row(Kb[i], QKA[1024 + h * 128:1024 + (h + 1) * 128, :], f"k{i}", f"k{i}")
                load_row(Qb[i], QKA[h * 128:(h + 1) * 128, :], f"q{i}", f"q{i}")
                run_head("a", h, Kb[i], f"k{i}", None, Qb[i], f"q{i}", None, lambda m, h=h, v3=v3: v3[:, m, h * 128:(h + 1) * 128], 128 ** -0.5, h * 128, h % 2)
            P.barrier()
            v3 = load_v(VB, 256)
            for h in range(8):
                kvh = h // 4
                if h % 4 == 0:
                    ki = kvh % 2
                    load_row(Kb[ki], KB[kvh * 128:(kvh + 1) * 128, :], f"k{ki}", f"k{ki}")
                i = ctr["kq"] % 2
                ctr["kq"] += 1
                load_row(Qb[i], QB[h * 128:(h + 1) * 128, :], f"q{i}", f"q{i}")
                run_head("b", h, Kb[ki], f"k{ki}", None, Qb[i], f"q{i}", None, lambda m, kvh=kvh, v3=v3: v3[:, m, kvh * 128:(kvh + 1) * 128], 128 ** -0.5, 1024 + h * 128, h % 2)
            P.barrier()
            v3 = load_v(VC, 1024)
            load_row(KR, KRP[:, :], "kr", "kr", rows=64)
            for h in range(8):
                i = ctr["kq"] % 2
                ctr["kq"] += 1
                load_row(Kb[i], KCN[h * 128:(h + 1) * 128, :], f"k{i}", f"k{i}")
                load_row(Qb[i], QCN[h * 128:(h + 1) * 128, :], f"q{i}", f"q{i}")
                load_row(QR[i], QCR[h * 64:(h + 1) * 64, :], f"q{i}", f"qr{i}", rows=64)
                run_head("c", h, Kb[i], f"k{i}", KR, Qb[i], f"q{i}", QR[i], lambda m, h=h, v3=v3: v3[:, m, h * 128:(h + 1) * 128], 192 ** -0.5, 2048 + h * 128, h % 2)
            P.barrier()

        def phase_merge(l, ntok):
            w_br = inp(f"w_br{l}", [3072, D])
            half = ntok // 2
            STG = [arb(9216 + i * 1152, 1152) for i in range(3)]
            GT = [arb(12672 + i * 1728, 1728).rearrange("p (b t) -> p b t", b=3) for i in range(2)]
            TM = [arf(16128 + i * 512, 512) for i in range(4)]
            res24 = RESb[:, 0:24 * half].rearrange("p (c t) -> p c t", c=24)
            tw = 384 if half % 384 == 0 else 512
            sg4 = SG.rearrange("(b c p) t -> c p b t", b=3, p=128)
            cnt = {"g": 0, "s": 0, "bk": 0}
            for hf in range(2):
                h0 = hf * half
                for j in range(6):
                    src = OT[j * 512:(j + 1) * 512, h0:h0 + half].rearrange("(c p) t -> p c t", p=128)
                    P.add("sp", lambda e, j=j, src=src: e.dma_start(out=res24[:, 4 * j:4 * j + 4, :], in_=src),
                          writes=[f"res.{j}"], chan=f"r{j}")
                for ds in range(8):
                    view, tok, per = load_slab(wsrc(w_br, 3072, ds * 256, 256), 24, 256, 3, 3072, split=3)
                    for ec in range(2):
                        dc = ds * 2 + ec
                        gi_ = cnt["g"] % 2
                        cnt["g"] += 1
                        gt = GT[gi_]
                        P.add("sp", lambda e, gt=gt, dc=dc, h0=h0: e.dma_start(out=gt[:, :, 0:half], in_=sg4[dc][:, :, h0:h0 + half]),
                              writes=[f"gt{gi_}"], chan=f"gt{gi_}")
                        s = cnt["s"] % 3
                        cnt["s"] += 1
                        for ti in range(half // tw):
                            t0 = ti * tw
                            banks = []
                            for br in range(3):
                                b = cnt["bk"] % 6
                                cnt["bk"] += 1
                                banks.append(b)
                                for k in range(8):
                                    kk = br * 8 + k
                                    P.add("pe", lambda e, b=b, kk=kk, ec=ec, t0=t0, view=view: e.matmul(
                                        bank(b)[:, 0:tw], lhsT=view[:, kk, ec * 128:(ec + 1) * 128], rhs=res24[:, kk, t0:t0 + tw],
                                        start=(kk % 8 == 0), stop=(kk % 8 == 7)),
                                        reads=[f"{tok}.{kk // per}", f"res.{kk // 4}"], writes=[f"ps{b}"])
                            ta, tb = TM[(ti % 2) * 2], TM[(ti % 2) * 2 + 1]
                            na, nb = f"tm{(ti % 2) * 2}", f"tm{(ti % 2) * 2 + 1}"
                            P.add("dve", lambda e, ta=ta, b=banks[0], gt=gt, t0=t0: e.tensor_tensor(out=ta[:, 0:tw], in0=bank(b)[:, 0:tw], in1=gt[:, 0, t0:t0 + tw], op=ALU.mult),
                                  reads=[f"ps{banks[0]}", f"gt{gi_}"], writes=[na])
                            P.add("dve", lambda e, tb=tb, b=banks[1], gt=gt, t0=t0: e.tensor_tensor(out=tb[:, 0:tw], in0=bank(b)[:, 0:tw], in1=gt[:, 1, t0:t0 + tw], op=ALU.mult),
                                  reads=[f"ps{banks[1]}", f"gt{gi_}"], writes=[nb])
                            P.add("pool", lambda e, ta=ta, tb=tb: e.tensor_tensor(out=ta[:, 0:tw], in0=ta[:, 0:tw], in1=tb[:, 0:tw], op=ALU.add),
                                  reads=[na, nb], writes=[na])
                            P.add("dve", lambda e, tb=tb, b=banks[2], gt=gt, t0=t0: e.tensor_tensor(out=tb[:, 0:tw], in0=bank(b)[:, 0:tw], in1=gt[:, 2, t0:t0 + tw], op=ALU.mult),
                                  reads=[f"ps{banks[2]}", f"gt{gi_}", na], writes=[nb])
                            P.add("pool", lambda e, ta=ta, tb=tb, s=s, t0=t0: e.tensor_tensor(out=STG[s][:, t0:t0 + tw], in0=ta[:, 0:tw], in1=tb[:, 0:tw], op=ALU.add),
                                  reads=[na, nb], writes=[f"stg{s}.{ti}"])
                        P.add("sp", lambda e, s=s, dc=dc, h0=h0: e.dma_start(out=YT[dc * 128:(dc + 1) * 128, h0:h0 + half], in_=STG[s][:, 0:half]),
                              reads=[f"stg{s}.{ti}" for ti in range(half // tw)], writes=[f"d.yt{dc}.{hf}"], chan=f"st{s}")
                P.barrier()

        def load_res(src, kc, ntok, t0=0):
            r3 = RESb[:, 0:kc * ntok].rearrange("p (c t) -> p c t", c=kc)
            step = 4
            for j in range(0, kc, step):
                j1 = min(kc, j + step)
                s3 = src[j * 128:j1 * 128, t0:t0 + ntok].rearrange("(c p) t -> p c t", p=128)
                P.add("sp", lambda e, j=j, j1=j1, s3=s3: e.dma_start(out=r3[:, j:j1, :], in_=s3), writes=[f"res.{j // step}"], chan=f"r{j // step}")
            return r3, step

        def phase_wo(l, ntok):
            w_o = inp(f"w_o{l}", [D, D])
            STG32 = [arf(12288 + i * 2304, 2304) for i in range(2)]
            r3, step = load_res(YT, NCH, ntok)
            tgs = tgroups(ntok)
            cur = {"s": 0, "n": 0}

            def evac(tag, si, ec, gi, t0, n, m, b):
                if gi == 0:
                    cur["s"] = cur["n"] % 2
                    cur["n"] += 1
                s = cur["s"]
                copy_alt(gi, STG32[s][:, t0:t0 + n], bank(b)[:, 0:n], [f"ps{b}"], [f"s32{s}.{gi}"])

            def post(tag, si, ec, m):
                s = cur["s"]
                r0 = si * 512 + ec * 128
                P.add("sp", lambda e: e.dma_start(out=UT[r0:r0 + 128, 0:ntok], in_=STG32[s][:, 0:ntok]),
                      reads=[f"s32{s}.{gi}" for gi in range(len(tgs))], writes=[f"d.ut{r0}"], chan=f"s32{s}")
            slabs = [(wsrc(w_o, D, s * 512, 512), 512, s) for s in range(4)]
            gemm_fm(slabs, NCH, lambda k, gi: r3[:, k, tgs[gi][0]:tgs[gi][0] + tgs[gi][1]], lambda k, gi: f"res.{k // 4}", tgs, evac, post)
            P.barrier()

        def phase_ffn1(l, ntok):
            w1 = inp(f"w_ff1{l}", [D, DFF])
            w3 = inp(f"w_ff3{l}", [D, DFF])
            STG = [arb(16384 + i * 1152, 1152) for i in range(3)]
            TM = [arf(19840 + i * 512, 512) for i in range(3)]
            tgs = tgroups(ntok)
            cnt = {"s": 0, "b": 0, "t": 0}
            for fs in range(11):
                va, ta, pa = load_slab(wsrc(w1, D, fs * 512, 512), NCH, 512, 4, 4096)
                vg, tg_, pg = load_slab(wsrc(w3, D, fs * 512, 512), NCH, 512, 4, 4096)
                for ec in range(4):
                    s = cnt["s"] % 3
                    cnt["s"] += 1
                    for gi, (t0, n) in enumerate(tgs):
                        ba = cnt["b"] % 8
                        bg = (cnt["b"] + 1) % 8
                        cnt["b"] += 2
                        for (view, tok, per, b) in ((va, ta, pa, ba), (vg, tg_, pg, bg)):
                            for k in range(NCH):
                                P.add("pe", lambda e, b=b, n=n, view=view, k=k, ec=ec, t0=t0: e.matmul(
                                    bank(b)[:, 0:n], lhsT=view[:, k, ec * 128:(ec + 1) * 128], rhs=res16[:, k, t0:t0 + n],
                                    start=(k == 0), stop=(k == NCH - 1)),
                                    reads=[f"{tok}.{k // per}", f"res.{t0 // 512}"], writes=[f"ps{b}"])
                        ti = cnt["t"] % 3
                        cnt["t"] += 1
                        P.add("act", lambda e, ti=ti, ba=ba, n=n: e.activation(out=TM[ti][:, 0:n], in_=bank(ba)[:, 0:n], func=AF.Silu),
                              reads=[f"ps{ba}"], writes=[f"tm{ti}"])
                        P.add("dve", lambda e, ti=ti, bg=bg, n=n, s=s, t0=t0: e.tensor_tensor(out=STG[s][:, t0:t0 + n], in0=bank(bg)[:, 0:n], in1=TM[ti][:, 0:n], op=ALU.mult),
                              reads=[f"ps{bg}", f"tm{ti}"], writes=[f"stg{s}.{gi}"])
                    r0 = fs * 512 + ec * 128
                    P.add("sp", lambda e, s=s, r0=r0: e.dma_start(out=ACTT[r0:r0 + 128, 0:ntok], in_=STG[s][:, 0:ntok]),
                          reads=[f"stg{s}.{gi}" for gi in range(len(tgs))], writes=[f"d.act{r0}"], chan=f"st{s}")
            P.barrier()

        def phase_ffn2(l, ntok):
            w2 = inp(f"w_ff2{l}", [DFF, D])
            S32 = [arf(11264 + i * 768, 768) for i in range(4)]
            parts = []
            t0 = 0
            while t0 < ntok:
                n = min(768, ntok - t0)
                parts.append((t0, n))
                t0 += n
            cnt = {"s": 0, "b": 0}
            for (p0, pn) in parts:
                r3, step = load_res(ACTT, FCH, pn, t0=p0)
                tiles = [(0, pn // 2), (pn // 2, pn // 2)]
                for ds in range(8):
                    view, tok, per = load_slab(wsrc(w2, DFF, ds * 256, 256), FCH, 256, 2, 5632, split=4)
                    for ec in range(2):
                        s = cnt["s"] % 4
                        cnt["s"] += 1
                        for ti, (t0, n) in enumerate(tiles):
                            b = cnt["b"] % 4
                            cnt["b"] += 1
                            for k in range(FCH):
                                P.add("pe", lambda e, b=b, n=n, view=view, k=k, ec=ec, t0=t0, r3=r3: e.matmul(
                                    bank(b)[:, 0:n], lhsT=view[:, k, ec * 128:(ec + 1) * 128], rhs=r3[:, k, t0:t0 + n],
                                    start=(k == 0), stop=(k == FCH - 1)),
                                    reads=[f"{tok}.{k // per}", f"res.{k // step}"], writes=[f"ps{b}"])
                            copy_alt(ti, S32[s][:, t0:t0 + n], bank(b)[:, 0:n], [f"ps{b}"], [f"s32{s}.{ti}"])
                        r0 = ds * 256 + ec * 128
                        P.add("sp", lambda e, s=s, r0=r0, p0=p0, pn=pn: e.dma_start(out=U2T[r0:r0 + 128, p0:p0 + pn], in_=S32[s][:, 0:pn]),
                              reads=[f"s32{s}.0", f"s32{s}.1"], writes=[f"d.u2{r0}.{p0}"], chan=f"s32{s}")
                P.barrier()

        def program():
            for l in range(cfg.layers):
                phase_ada(l)
                if stop["flag"]:
                    return
            phase_rn(0, X[0], None, None, None, (0, 1, 0), T)
            phase_end(0, "rn0")
            if stop["flag"]:
                return
            xi = 0
            for l in range(cfg.layers):
                last = (l == cfg.layers - 1)
                nq = TL if last else T
                phase_inproj(l)
                phase_end(l, "inproj")
                if stop["flag"]:
                    return
                phase_mla(l)
                phase_end(l, "mla")
                if stop["flag"]:
                    return
                phase_attn(l, ctx_q=not last)
                phase_end(l, "attn")
                if stop["flag"]:
                    return
                phase_merge(l, nq)
                phase_end(l, "merge")
                if stop["flag"]:
                    return
                phase_wo(l, nq)
                phase_end(l, "wo")
                if stop["flag"]:
                    return
                phase_rn(l, X[xi], X[xi + 1], UT, 2, (3, 4, l), nq)
                xi += 1
                phase_end(l, "rn1")
                if stop["flag"]:
                    return
                phase_ffn1(l, nq)
                phase_end(l, "ffn1")
                if stop["flag"]:
                    return
                phase_ffn2(l, nq)
                phase_end(l, "ffn2")
                if stop["flag"]:
                    return
                if last:
                    phase_rn(l, X[xi], None, U2T, 5, None, nq, out_final=True)
                else:
                    nlast = (l + 1 == cfg.layers - 1)
                    phase_rn(l, X[xi], X[xi + 1], U2T, 5, (0, 1, l + 1), T)
                    xi += 1
                phase_end(l, "rn2")
                if stop["flag"]:
                    return

        program()
        P.emit()
    return nc, in_shapes, out_names


def prep_core_inputs(inputs, b, shared):
    x = np.asarray(inputs["x"][b], np.float32)
    ctx = np.asarray(inputs["ctx"][b], np.float32)
    d = dict(shared)
    d["xT"] = np.ascontiguousarray(np.concatenate([x.T, ctx.T], axis=1))
    cv = np.stack([_pc(inputs["c"][b]), _pc(inputs["c_ctx"])], axis=2)
    d["cvec"] = np.ascontiguousarray(cv.reshape(128, 32))
    return d


def prep_shared(inputs, layers=2):
    (ropeB, rotB), (ropeC, rotC) = _rope_tables()
    sh = {"rotB": rotB, "rotC": rotC, "ropeB": ropeB, "ropeC": ropeC}
    vecs = []
    dr, dc, va = _abias_index()
    for l in range(layers):
        cols = [_pc(inputs["g_pre1"][l]), _pc(inputs["g_post1"][l]), _pc(inputs["g_pre2"][l]), _pc(inputs["g_post2"][l]),
                _pc(inputs["b_ada"][l]), _pc(inputs["gqa_q_norm"][l]), _pc(inputs["gqa_k_norm"][l]),
                _pc(inputs["mla_q_norm"][l]), _pc(inputs["mla_kv_norm"][l])]
        vecs.append(np.concatenate(cols, axis=1))
        rpb = np.asarray(inputs["rpb"][l], np.float32)
        g = rpb[:, dr, dc]
        g = np.where(va[None], g, np.float32(NEG)).astype(np.float32)
        sh[f"abias{l}"] = np.ascontiguousarray(g.reshape(8 * 20 * 128, 512))
        sh[f"w_ada{l}"] = np.asarray(inputs["w_ada"][l], np.float32)
        sh[f"w_in{l}"] = np.asarray(inputs["w_in"][l], np.float32)
        sh[f"w_uq{l}"] = np.asarray(inputs["w_uq"][l], np.float32)
        sh[f"w_ukv{l}"] = np.asarray(inputs["w_ukv"][l], np.float32)
        sh[f"w_br{l}"] = np.ascontiguousarray(np.concatenate(
            [inputs["w_br_a"][l], inputs["w_br_b"][l], inputs["w_br_c"][l]], axis=0).astype(np.float32))
        sh[f"w_o{l}"] = np.asarray(inputs["w_o"][l], np.float32)
        sh[f"w_ff1{l}"] = np.asarray(inputs["w_ff1"][l], np.float32)
        sh[f"w_ff3{l}"] = np.asarray(inputs["w_ff3"][l], np.float32)
        sh[f"w_ff2{l}"] = np.asarray(inputs["w_ff2"][l], np.float32)
    sh["vecs"] = np.ascontiguousarray(np.concatenate(vecs, axis=1))
    if layers == 1:
        sh["vecs"] = np.ascontiguousarray(np.concatenate([sh["vecs"], np.zeros_like(sh["vecs"])], axis=1))
    return sh


_CACHE = {}


def kernel(**inputs):
    n = 8
    if "nc" not in _CACHE:
        _CACHE["nc"] = build(Cfg())
    nc, in_shapes, out_names = _CACHE["nc"]
    shared = prep_shared(inputs)
    in_maps = []
    for b in range(n):
        d = prep_core_inputs(inputs, b, shared)
        in_maps.append({k: d[k] for k in in_shapes})
    res = run_bass_kernel_spmd(nc, in_maps, core_ids=list(range(n)))
    out = np.stack([np.ascontiguousarray(r["outT"].T) for r in res.results], axis=0)
    return out.astype(np.float32)
```

```python
import contextlib
import numpy as np
import concourse.bass as bass
import concourse.mybir as mybir
from concourse.bass_utils import run_bass_kernel_spmd

F32 = mybir.dt.float32
BF16 = mybir.dt.bfloat16
ALU = mybir.AluOpType
AF = mybir.ActivationFunctionType

D = 2048
TL = 2048
TC = 256
T = TL + TC
NCH = 16
DFF = 5632
FCH = 44
IN_W = 11840
EPS = 1e-6
NEG = -30000.0
NV = 170
O_AQ, O_AK, O_AV, O_BQ, O_BK, O_BV, O_CQ, O_CKV, O_CKR, O_GA = 0, 1024, 2048, 3072, 4096, 4352, 4608, 5120, 5632, 5696

ENGS = ["sp", "pe", "dve", "act", "pool"]


class Op:
    __slots__ = ("eng", "fn", "deps", "signal", "count", "chan", "chan_val")

    def __init__(self, eng, fn):
        self.eng = eng
        self.fn = fn
        self.deps = ()
        self.signal = False
        self.count = 0
        self.chan = None
        self.chan_val = 0


class Prog:
    def __init__(self, nc, same_engine_sync=True):
        self.nc = nc
        self.streams = {e: [] for e in ENGS}
        self.last_w = {}
        self.readers = {}
        self.chan_count = {}
        self.last_dma = {}
        self.same_engine_sync = same_engine_sync

    def add(self, eng, fn, reads=(), writes=(), chan=None):
        op = Op(eng, fn)
        deps = {}
        lw = self.last_w
        rdrs = self.readers
        for r in reads:
            w = lw.get(r)
            if w is not None:
                deps[id(w)] = w
        for t in writes:
            w = lw.get(t)
            if w is not None:
                deps[id(w)] = w
            lst = rdrs.get(t)
            if lst:
                for rd in lst:
                    deps[id(rd)] = rd
        op.deps = list(deps.values())
        for r in reads:
            l = rdrs.get(r)
            if l is None:
                rdrs[r] = [op]
            elif chan is None:
                for i, o in enumerate(l):
                    if o.eng == eng and o.chan is None:
                        l[i] = op
                        break
                else:
                    l.append(op)
            else:
                l.append(op)
        for t in writes:
            lw[t] = op
            rdrs[t] = []
        if chan is not None:
            c = self.chan_count.get(chan, 0) + 1
            self.chan_count[chan] = c
            op.chan = chan
            op.chan_val = 16 * c
            self.last_dma[chan] = op
        self.streams[eng].append(op)
        return op

    def barrier(self):
        deps = []
        for e in ENGS:
            for x in reversed(self.streams[e]):
                if x.chan is None and x.fn is not None:
                    deps.append(x)
                    break
        deps.extend(self.last_dma.values())
        for e in ENGS:
            op = Op(e, None)
            op.deps = list(deps)
            self.streams[e].append(op)
        self.last_w = {}
        self.readers = {}

    def _needs_sem(self, x, d):
        if d.chan is not None:
            return True
        if d.eng == x.eng and x.chan is None:
            if x.eng == "pe":
                return False
            return self.same_engine_sync or x.fn is None
        return True

    def emit(self):
        nc = self.nc
        for e in ENGS:
            for x in self.streams[e]:
                for d in x.deps:
                    if d.chan is None and self._needs_sem(x, d):
                        d.signal = True
        for e in ENGS:
            c = 0
            for x in self.streams[e]:
                if x.chan is None and x.signal:
                    c += 1
                    x.count = c
        chans = sorted(self.chan_count.keys())
        with contextlib.ExitStack() as st:
            esem = {e: st.enter_context(nc.semaphore("s_" + e)) for e in ENGS}
            csem = {c: st.enter_context(nc.semaphore("c_" + str(c))) for c in chans}
            block = st.enter_context(nc.Block())

            def run_stream(ename, eng):
                waited = {}
                for x in self.streams[ename]:
                    need = {}
                    for d in x.deps:
                        if not self._needs_sem(x, d):
                            continue
                        if d.chan is not None:
                            key = ("c", d.chan)
                            val = d.chan_val
                        else:
                            key = ("e", d.eng)
                            val = d.count
                        if waited.get(key, 0) >= val:
                            continue
                        if need.get(key, 0) < val:
                            need[key] = val
                    for key, val in need.items():
                        sem = csem[key[1]] if key[0] == "c" else esem[key[1]]
                        eng.wait_ge(sem, val)
                        waited[key] = val
                    if x.fn is None:
                        continue
                    ins = x.fn(eng)
                    if x.chan is not None:
                        ins.then_inc(csem[x.chan], 16)
                    elif x.signal:
                        ins.then_inc(esem[ename], 1)
                if ename == "sp":
                    for c in chans:
                        v = 16 * self.chan_count[c]
                        if waited.get(("c", c), 0) < v:
                            eng.wait_ge(csem[c], v)

            @block.sync
            def _(e):
                run_stream("sp", e)

            @block.tensor
            def _(e):
                run_stream("pe", e)

            @block.vector
            def _(e):
                run_stream("dve", e)

            @block.scalar
            def _(e):
                run_stream("act", e)

            @block.gpsimd
            def _(e):
                run_stream("pool", e)


def _rope_tables():
    t = np.arange(TL)
    row, col = (t // 64).astype(np.float64), (t % 64).astype(np.float64)

    def tab(dh):
        half = dh // 2
        q = half // 2
        f = 10000.0 ** (-np.arange(0, half, 2, dtype=np.float64) / half)
        cos = np.zeros((dh, TL)); sin = np.zeros((dh, TL))
        for ax, pos in enumerate((row, col)):
            ang = pos[None, :] * f[:, None]
            b = ax * half
            cos[b:b + q] = np.cos(ang); cos[b + q:b + half] = np.cos(ang)
            sin[b:b + q] = np.sin(ang); sin[b + q:b + half] = np.sin(ang)
        rot = np.zeros((dh, dh))
        for ax in range(2):
            b = ax * half
            for i in range(q):
                rot[b + i, b + q + i] = -1.0
                rot[b + q + i, b + i] = 1.0
        return np.stack([cos, sin]).astype(np.float32), np.ascontiguousarray(rot.T).astype(np.float32)

    return tab(128), tab(64)


def _abias_index():
    tiles = []
    for g, ms in ((0, range(0, 6)), (1, range(2, 10)), (3, range(10, 16))):
        for m in ms:
            a = np.arange(128) // 64
            w = np.arange(128) % 64
            rr = np.arange(512) // 64
            qc = np.arange(512) % 64
            j = (2 * m + a)[:, None]
            r = (8 * g + rr)[None, :]
            r0 = np.clip(r - 4, 0, 24)
            valid = (j >= r0) & (j < r0 + 8)
            c0 = np.clip(qc - 8, 0, 48)[None, :]
            valid = valid & (w[:, None] >= c0) & (w[:, None] < c0 + 16)
            dr = np.clip(j - r + 7, 0, 14)
            dc = np.clip(w[:, None] - qc[None, :], -15, 15) + 15
            tiles.append((np.broadcast_to(dr, (128, 512)), dc, valid))
    dr = np.stack([t[0] for t in tiles]); dc = np.stack([t[1] for t in tiles]); va = np.stack([t[2] for t in tiles])
    return dr, dc, va


def _abias_tile_id(g, m):
    if g == 0:
        return m
    if g == 1:
        return 6 + (m - 2)
    if g == 2:
        return 6 + (m - 6)
    return 14 + (m - 10)


def _a_chunks(g):
    return {0: range(0, 6), 1: range(2, 10), 2: range(6, 14), 3: range(10, 16)}[g]


def _pc(v):
    v = np.asarray(v, np.float32)
    return np.ascontiguousarray(v.reshape(-1, 128).T)


class Cfg:
    def __init__(self, layers=2, taps=(), stop=None, x_in="xT"):
        self.layers = layers
        self.taps = set(taps)
        self.stop = stop
        self.x_in = x_in


def build(cfg):
    nc = bass.Bass("TRN2", target_bir_lowering=False)
    in_shapes = {}

    def inp(name, shape):
        in_shapes[name] = tuple(shape)
        return nc.dram_tensor(name, list(shape), F32, kind="ExternalInput").ap()

    out_names = []

    def scratch(name, shape, dt):
        if name in cfg.taps:
            out_names.append(name)
            return nc.dram_tensor(name, list(shape), dt, kind="ExternalOutput").ap()
        return nc.dram_tensor(name, list(shape), dt, kind="Internal").ap()

    st = contextlib.ExitStack()
    with st:
        RESb = st.enter_context(nc.sbuf_tensor("RES", [128, 36864], BF16))
        AR = st.enter_context(nc.sbuf_tensor("ARENA", [128, 30720], F32))
        ONES = st.enter_context(nc.sbuf_tensor("ONES", [128, 128], BF16))
        ROTB = st.enter_context(nc.sbuf_tensor("ROTB", [128, 128], BF16))
        ROTC = st.enter_context(nc.sbuf_tensor("ROTC", [64, 64], BF16))
        CV = st.enter_context(nc.sbuf_tensor("CV", [128, 32], F32))
        CVS = st.enter_context(nc.sbuf_tensor("CVS", [128, 32], BF16))
        VEC = st.enter_context(nc.sbuf_tensor("VEC", [128, 2 * NV], F32))
        MOD = st.enter_context(nc.sbuf_tensor("MOD", [128, 2 * 192], F32))
        COEF = st.enter_context(nc.sbuf_tensor("COEF", [128, 2 * 192], F32))
        PS = st.enter_context(nc.psum_tensor("PS", [128, 8 * 512], F32))
        P = Prog(nc)

        def bank(b):
            return PS[:, b * 512:(b + 1) * 512]

        def arf(off, n):
            return AR[:, off:off + n]

        def arb(off, n):
            return AR[:, off:off + n].bitcast(BF16)

        def vec(l, col, n=1):
            return VEC[:, l * NV + col:l * NV + col + n]

        V_GPRE1, V_GPOST1, V_GPRE2, V_GPOST2, V_BADA, V_QN, V_KN, V_MQN, V_MKVN = 0, 16, 32, 48, 64, 160, 161, 162, 166

        def coef(l, kind, c, w):
            o = l * 192 + kind * 32 + c * 2 + w
            return COEF[:, o:o + 1]

        stop = {"flag": False}

        def phase_end(l, name):
            P.barrier()
            if cfg.stop == (l, name):
                stop["flag"] = True

        xT = inp("xT", [D, T])
        cvec = inp("cvec", [128, 32])
        rotb_in = inp("rotB", [128, 128])
        rotc_in = inp("rotC", [64, 64])
        ropeB = inp("ropeB", [2, 128, TL])
        ropeC = inp("ropeC", [2, 64, TL])
        vecs_in = inp("vecs", [128, 2 * NV])

        X = [xT] + [scratch(f"X{i}", [D, T], F32) for i in (1, 2, 3)]
        OUT = nc.dram_tensor("outT", [D, TL], F32, kind="ExternalOutput").ap()
        out_names.append("outT")
        QKA = scratch("QKA", [2048, T], BF16)
        VA = scratch("VA", [T, 1024], BF16)
        QB = scratch("QB", [1024, T], BF16)
        KB = scratch("KB", [256, T], BF16)
        VB = scratch("VB", [T, 256], BF16)
        CQN = scratch("CQN", [1024, T], BF16)
        CKR = scratch("CKR", [64, T], F32)
        SG = scratch("SG", [6144, T], BF16)
        QCN = scratch("QCN", [1024, T], BF16)
        QCR = scratch("QCR", [512, T], BF16)
        KCN = scratch("KCN", [1024, T], BF16)
        KRP = scratch("KRP", [64, T], BF16)
        VC = scratch("VC", [T, 1024], BF16)
        OT = scratch("OT", [3072, T], BF16)
        YT = scratch("YT", [D, T], BF16)
        UT = scratch("UT", [D, T], F32)
        U2T = scratch("U2T", [D, T], F32)
        ACTT = scratch("ACTT", [DFF, T], BF16)

        P.add("pool", lambda e: e.memset(ONES[:], 1.0), writes=["ones"])
        P.add("pool", lambda e: e.dma_start(out=ROTB[:], in_=rotb_in), writes=["rotb"], chan="k0")
        P.add("pool", lambda e: e.dma_start(out=ROTC[:], in_=rotc_in), writes=["rotc"], chan="k1")
        P.add("sp", lambda e: e.dma_start(out=CV[:], in_=cvec), writes=["cv"], chan="k2")
        P.add("sp", lambda e: e.dma_start(out=VEC[:], in_=vecs_in), writes=["vec"], chan="k3")
        P.add("act", lambda e: e.activation(out=CVS[:], in_=CV[:], func=AF.Silu), reads=["cv"], writes=["cvs"])
        P.barrier()

        wcount = {"n": 0}

        def load_slab(src3, kc, ncols, nslots, slot_words, base=0, split=4):
            s = wcount["n"] % nslots
            wcount["n"] += 1
            view = arb(base + s * slot_words, slot_words)[:, 0:kc * ncols].rearrange("p (c e) -> p c e", c=kc)
            per = (kc + split - 1) // split
            for j in range(split):
                k0, k1 = j * per, min(kc, (j + 1) * per)
                if k0 >= k1:
                    continue
                P.add("pool", lambda e, v=view, k0=k0, k1=k1: e.dma_start(out=v[:, k0:k1, :], in_=src3[:, k0:k1, :]),
                      writes=[f"wb{s}.{j}"], chan=f"w{s}.{j}")
            return view, f"wb{s}", per

        def slab_stream(specs, nslots, slot_words, base=0, split=4, ahead=None):
            n = len(specs)
            out = []
            li = 0
            ahead = nslots if ahead is None else ahead
            for i in range(n):
                while li < min(n, i + ahead):
                    src3, kc_, ncols_ = specs[li]
                    out.append(load_slab(src3, kc_, ncols_, nslots, slot_words, base=base, split=split))
                    li += 1
                yield out[i]

        def wsrc(wap, K, c0, nc_):
            return wap[0:K, c0:c0 + nc_].rearrange("(c p) e -> p c e", p=128)

        ada_state = {"n": 0}
        w_ada_aps = {}

        def ada_w(l):
            if l not in w_ada_aps:
                w_ada_aps[l] = inp(f"w_ada{l}", [D, 12 * 1024])
            return w_ada_aps[l]

        def ada_load(l, s):
            slot = ada_state["n"] % 4
            ada_state["n"] += 1
            view = RESb[:, slot * 8192:(slot + 1) * 8192].rearrange("p (c e) -> p c e", c=16)
            src3 = wsrc(ada_w(l), D, s * 512, 512)
            for j in range(4):
                P.add("pool", lambda e, v=view, j=j, src3=src3: e.dma_start(out=v[:, 4 * j:4 * j + 4, :], in_=src3[:, 4 * j:4 * j + 4, :]),
                      writes=[f"adw{slot}.{j}"], chan=f"aw{slot}.{j}")
            return view, slot

        def ada_mm(l, s, view, slot):
            psm = bank(7)
            cvs3 = CVS[:].rearrange("p (c w) -> p c w", w=2)
            for ec in range(4):
                col = (s * 4 + ec) * 2
                for k in range(16):
                    P.add("pe", lambda e, view=view, ec=ec, k=k, col=col: e.matmul(
                        psm[:, col:col + 2], lhsT=view[:, k, ec * 128:(ec + 1) * 128], rhs=cvs3[:, k, :],
                        start=(k == 0), stop=(k == 15)),
                        reads=[f"adw{slot}.{k // 4}", "cvs"], writes=["ps7"])

        def ada_finish(l, part):
            psm = bank(7)
            mod = MOD[:, l * 192:(l + 1) * 192]
            mod3 = mod.rearrange("p (j w) -> p j w", w=2)
            j0, j1 = (0, 32) if part == "A" else (32, 96)
            bada = vec(l, V_BADA + j0, j1 - j0).unsqueeze(2).broadcast_to([128, j1 - j0, 2])
            P.add("dve", lambda e: e.tensor_tensor(out=mod3[:, j0:j1, :], in0=psm[:, 2 * j0:2 * j1].rearrange("p (j w) -> p j w", w=2), in1=bada, op=ALU.add),
                  reads=["ps7", "vec"], writes=["mod"])
            cf = COEF[:, l * 192:(l + 1) * 192].rearrange("p (k c w) -> p k c w", k=6, w=2)

            def g2(col):
                return vec(l, col, 16).unsqueeze(2).broadcast_to([128, 16, 2])

            def m3(j):
                return mod3[:, j * 16:(j + 1) * 16, :]
            gains = ((0, 1, V_GPRE1),) if part == "A" else ((3, 4, V_GPRE2),)
            shifts = ((1, 0),) if part == "A" else ((4, 3),)
            coefs = () if part == "A" else ((2, 2, V_GPOST1), (5, 5, V_GPOST2))
            for kind, jsc, gcol in gains:
                P.add("dve", lambda e, kind=kind, jsc=jsc: e.tensor_scalar_add(out=cf[:, kind], in0=m3(jsc), scalar1=1.0),
                      reads=["mod"], writes=["coef"])
                P.add("dve", lambda e, kind=kind, gcol=gcol: e.tensor_tensor(out=cf[:, kind], in0=cf[:, kind], in1=g2(gcol), op=ALU.mult),
                      reads=["coef", "vec"], writes=["coef"])
            for kind, j in shifts:
                P.add("dve", lambda e, kind=kind, j=j: e.tensor_copy(out=cf[:, kind], in_=m3(j)), reads=["mod"], writes=["coef"])
            for kind, j, gcol in coefs:
                P.add("dve", lambda e, kind=kind, j=j, gcol=gcol: e.tensor_tensor(out=cf[:, kind], in0=m3(j), in1=g2(gcol), op=ALU.mult),
                      reads=["mod", "vec"], writes=["coef"])

        def ada_units(items):
            slabs = [it for it in items if it[0] == "s"]
            loaded = []
            li = 0
            for _ in range(3):
                if li < len(slabs):
                    loaded.append(ada_load(slabs[li][1], slabs[li][2]))
                    li += 1
            si = 0
            for it in items:
                if it[0] == "s":
                    if li < len(slabs):
                        loaded.append(ada_load(slabs[li][1], slabs[li][2]))
                        li += 1
                    view, slot = loaded[si]
                    si += 1
                    ada_mm(it[1], it[2], view, slot)
                    yield
                else:
                    ada_finish(it[1], it[2])

        def phase_ada_first():
            for _ in ada_units([("s", 0, s) for s in range(8)] + [("f", 0, "A")]):
                pass
            P.barrier()

        def ada_background_items():
            items = [("s", 0, s) for s in range(8, 24)] + [("f", 0, "B")]
            for l in range(1, cfg.layers):
                items += [("s", l, s) for s in range(8)] + [("f", l, "A")] + [("s", l, s) for s in range(8, 24)] + [("f", l, "B")]
            return items

        def phase_rn(l, x_src, x_dst, u_src, ckind, nkind, ntok, out_final=False):
            GW = 256
            res3 = RESb[:].rearrange("p (c t) -> p c t", c=NCH)
            ng = ntok // GW
            sqn = {"n": 0}

            def mk(g):
                t0 = g * GW
                w = 1 if t0 >= TL else 0
                pb = g % 3
                uw = arf(pb * 8192, 4096).rearrange("p (c t) -> p c t", c=NCH)
                xw = arf(pb * 8192 + 4096, 4096).rearrange("p (c t) -> p c t", c=NCH)
                r1 = arf(28672 + pb * 512, 256)
                r2 = arf(28672 + pb * 512 + 256, 256)
                tk = f"rn{pb}"

                def sq_buf():
                    i = sqn["n"] % 2
                    sqn["n"] += 1
                    return arb(24576 + i * 2048, 2048).rearrange("p (c t) -> p c t", c=NCH), f"rnsq{i}", 4 + i

                def ld():
                    xsrc3 = x_src[:, t0:t0 + GW].rearrange("(c p) t -> p c t", p=128)
                    P.add("sp", lambda e: e.dma_start(out=xw, in_=xsrc3), writes=[tk + "x"], chan=tk + "x")
                    if u_src is None:
                        return
                    usrc3 = u_src[:, t0:t0 + GW].rearrange("(c p) t -> p c t", p=128)
                    P.add("sp", lambda e: e.dma_start(out=uw, in_=usrc3), writes=[tk + "u"], chan=tk + "u")

                def s1():
                    if u_src is None:
                        return
                    sqb, sqt, sb = sq_buf()
                    P.add("act", lambda e: e.activation(out=sqb, in_=uw, func=AF.Square), reads=[tk + "u"], writes=[sqt])
                    for c in range(NCH):
                        P.add("pe", lambda e, c=c: e.matmul(bank(sb)[:, 0:GW], lhsT=ONES[:], rhs=sqb[:, c, :], start=(c == 0), stop=(c == NCH - 1)),
                              reads=[sqt, "ones"], writes=[f"ps{sb}"])
                    P.add("act", lambda e: e.activation(out=r1, in_=bank(sb)[:, 0:GW], func=AF.Ln, scale=1.0 / D, bias=EPS),
                          reads=[f"ps{sb}"], writes=[tk + "r1"])
                    P.add("act", lambda e: e.activation(out=r1, in_=r1, func=AF.Exp, scale=-0.5), reads=[tk + "r1"], writes=[tk + "r1"])
                    for c in range(NCH):
                        P.add("dve", lambda e, c=c: e.scalar_tensor_tensor(
                            out=uw[:, c, :], in0=uw[:, c, :], scalar=coef(l, ckind, c, w), in1=r1, op0=ALU.mult, op1=ALU.mult),
                            reads=[tk + "u", tk + "r1", "coef"], writes=[tk + "u"])
                    CS = 11
                    P.add("dve", lambda e: e.tensor_tensor(out=xw[:, 0:CS, :], in0=xw[:, 0:CS, :], in1=uw[:, 0:CS, :], op=ALU.add),
                          reads=[tk + "x", tk + "u"], writes=[tk + "xa"])
                    P.add("pool", lambda e: e.tensor_tensor(out=xw[:, CS:NCH, :], in0=xw[:, CS:NCH, :], in1=uw[:, CS:NCH, :], op=ALU.add),
                          reads=[tk + "x", tk + "u"], writes=[tk + "xb"])
                    if out_final:
                        dst3 = OUT[:, t0:t0 + GW].rearrange("(c p) t -> p c t", p=128)
                    else:
                        dst3 = x_dst[:, t0:t0 + GW].rearrange("(c p) t -> p c t", p=128)
                    P.add("pool", lambda e: e.dma_start(out=dst3, in_=xw), reads=[tk + "x", tk + "xa", tk + "xb"], writes=[f"xd{g}"], chan=tk + "o")

                def s2():
                    if nkind is None:
                        return
                    sqb, sqt, sb = sq_buf()
                    P.add("act", lambda e: e.activation(out=sqb, in_=xw, func=AF.Square), reads=[tk + "x", tk + "xa", tk + "xb"], writes=[sqt])
                    for c in range(NCH):
                        P.add("pe", lambda e, c=c: e.matmul(bank(sb)[:, 0:GW], lhsT=ONES[:], rhs=sqb[:, c, :], start=(c == 0), stop=(c == NCH - 1)),
                              reads=[sqt, "ones"], writes=[f"ps{sb}"])
                    P.add("act", lambda e: e.activation(out=r2, in_=bank(sb)[:, 0:GW], func=AF.Ln, scale=1.0 / D, bias=EPS),
                          reads=[f"ps{sb}"], writes=[tk + "r2"])
                    P.add("act", lambda e: e.activation(out=r2, in_=r2, func=AF.Exp, scale=-0.5), reads=[tk + "r2"], writes=[tk + "r2"])
                    r2b = r2.unsqueeze(1).broadcast_to([128, NCH, GW])
                    P.add("dve", lambda e: e.tensor_tensor(out=uw, in0=xw, in1=r2b, op=ALU.mult),
                          reads=[tk + "x", tk + "xa", tk + "xb", tk + "r2", tk + "u"], writes=[tk + "u"])

                def s3():
                    if nkind is None:
                        return
                    gk, sk, nl = nkind
                    for c in range(NCH):
                        P.add("act", lambda e, c=c: e.activation(
                            out=res3[:, c, t0:t0 + GW], in_=uw[:, c, :], func=AF.Identity, scale=coef(nl, gk, c, w), bias=coef(nl, sk, c, w)),
                            reads=[tk + "u", "coef"], writes=[f"res.{t0 // 512}"])
                return ld, s1, s2, s3
            st = [mk(g) for g in range(ng)]
            for g in range(min(3, ng)):
                st[g][0]()
            for g in range(min(2, ng)):
                st[g][1]()
            for g in range(ng):
                st[g][2]()
                if g + 2 < ng:
                    st[g + 2][1]()
                st[g][3]()
                if g + 3 < ng:
                    st[g + 3][0]()
            P.barrier()

        def tgroups(ntok):
            gs = []
            t0 = 0
            while t0 < ntok:
                n = min(512, ntok - t0)
                gs.append((t0, n))
                t0 += n
            return gs

        gemm_bank = {"n": 0}
        deferred = []

        def run_deferred():
            cur = list(deferred)
            del deferred[:]
            for f in cur:
                f()

        def gemm_fm(slabs, kc, rhs_fn, rhs_tok_fn, tgs, evac, post_ec=None, nslots=3, slot_words=4096, wbase=0, nbanks=4, order="ec"):
            strm = slab_stream([(s3, kc, nco) for (s3, nco, tg_) in slabs], nslots, slot_words, base=wbase)
            for si, (src3, ncols, tag) in enumerate(slabs):
                view, tok, per = next(strm)
                necs = (ncols + 127) // 128
                for ec in range(necs):
                    m = min(128, ncols - ec * 128)
                    for gi, (t0, n) in enumerate(tgs):
                        b = gemm_bank["n"] % nbanks
                        gemm_bank["n"] += 1
                        for k in range(kc):
                            P.add("pe", lambda e, b=b, m=m, n=n, view=view, k=k, ec=ec, gi=gi: e.matmul(
                                bank(b)[0:m, 0:n], lhsT=view[:, k, ec * 128:ec * 128 + m], rhs=rhs_fn(k, gi),
                                start=(k == 0), stop=(k == kc - 1)),
                                reads=[f"{tok}.{k // per}", rhs_tok_fn(k, gi)], writes=[f"ps{b}"])
                        run_deferred()
                        evac(tag, si, ec, gi, t0, n, m, b)
                    if post_ec is not None:
                        post_ec(tag, si, ec, m)
            run_deferred()
            run_deferred()

        def gemm_tm(slabs, kc, lhs_fn, lhs_tok_fn, ntok, evac, nslots=3, slot_words=4096, wbase=0, nbanks=4):
            strm = slab_stream([(s3, kc, nco) for (s3, nco, tg_) in slabs], nslots, slot_words, base=wbase)
            for si, (src3, ncols, tag) in enumerate(slabs):
                view, tok, per = next(strm)
                for tt in range(ntok // 128):
                    b = gemm_bank["n"] % nbanks
                    gemm_bank["n"] += 1
                    for k in range(kc):
                        P.add("pe", lambda e, b=b, view=view, k=k, tt=tt, ncols=ncols: e.matmul(
                            bank(b)[:, 0:ncols], lhsT=lhs_fn(k, tt), rhs=view[:, k, 0:ncols], start=(k == 0), stop=(k == kc - 1)),
                            reads=[f"{tok}.{k // per}", lhs_tok_fn(tt)], writes=[f"ps{b}"])
                    evac(tag, si, tt, ncols, b)

        res16 = RESb[:].rearrange("p (c t) -> p c t", c=NCH)

        stg_n = {"n": 0}

        def copy_alt(i, out, in_, reads, writes):
            if i % 2 == 0:
                P.add("act", lambda e: e.activation(out=out, in_=in_, func=AF.Copy), reads=reads, writes=writes)
            else:
                P.add("dve", lambda e: e.tensor_copy(out=out, in_=in_), reads=reads, writes=writes)

        def phase_inproj(l):
            w_in = inp(f"w_in{l}", [D, IN_W])
            STG = [arb(12288 + i * 1152, 1152) for i in range(3)]
            cosB = arf(15744, 2048)
            sinB = arf(17792, 2048)
            sqt = [arb(19840 + i * 256, 256) for i in range(2)]
            rt = [arf(20352 + i * 512, 512) for i in range(2)]
            qn32 = [arf(21376 + i * 512, 512) for i in range(2)]
            qnb = [arb(22400 + i * 256, 256) for i in range(2)]
            t1 = [arf(22912 + i * 512, 512) for i in range(2)]
            cqtmp = arf(23936, 2048).rearrange("p (c t) -> p c t", c=4)
            STG32 = [arf(25984 + i * 2304, 2304) for i in range(2)]
            P.add("sp", lambda e: e.dma_start(out=cosB, in_=ropeB[0]), writes=["cosB"], chan="k0")
            P.add("sp", lambda e: e.dma_start(out=sinB, in_=ropeB[1]), writes=["sinB"], chan="k1")
            tgs = tgroups(T)
            rhs_fn = lambda k, gi: res16[:, k, tgs[gi][0]:tgs[gi][0] + tgs[gi][1]]
            rhs_tok = lambda k, gi: f"res.{gi}"
            cnt = {"i": 0, "q": 0, "c": 0}
            cur = {}

            def stage_slot():
                s = stg_n["n"] % 3
                stg_n["n"] += 1
                return s

            def mk_plain(dst, row0_fn, func):
                def evac(tag, si, ec, gi, t0, n, m, b):
                    if gi == 0:
                        cur["s"] = stage_slot()
                    s = cur["s"]
                    i = cnt["i"]
                    cnt["i"] += 1
                    if func is None:
                        copy_alt(i, STG[s][0:m, t0:t0 + n], bank(b)[0:m, 0:n], [f"ps{b}"], [f"stg{s}.{gi}"])
                    else:
                        P.add("act", lambda e: e.activation(out=STG[s][0:m, t0:t0 + n], in_=bank(b)[0:m, 0:n], func=func),
                              reads=[f"ps{b}"], writes=[f"stg{s}.{gi}"])

                def post(tag, si, ec, m):
                    s = cur["s"]
                    r0 = row0_fn(si, ec)
                    P.add("sp", lambda e: e.dma_start(out=dst[r0:r0 + m, :], in_=STG[s][0:m, :]),
                          reads=[f"stg{s}.{gi}" for gi in range(len(tgs))], writes=[f"d.{id(dst)}.{r0}"], chan=f"st{s}")
                return evac, post

            def slabs_of(c0, width):
                out = []
                o = 0
                while o < width:
                    n = min(512, width - o)
                    out.append((wsrc(w_in, D, c0 + o, n), n, o))
                    o += n
                return out

            ev, po = mk_plain(QKA, lambda si, ec: si * 512 + ec * 128, None)
            gemm_fm(slabs_of(O_AQ, 2048), NCH, rhs_fn, rhs_tok, tgs, ev, po)

            def mk_tm(dst, c0_fn):
                def evac(tag, si, tt, ncols, b):
                    s = stage_slot()
                    i = cnt["i"]
                    cnt["i"] += 1
                    copy_alt(i, STG[s][:, 0:ncols], bank(b)[:, 0:ncols], [f"ps{b}"], [f"stg{s}.0"])
                    c0 = c0_fn(si)
                    P.add("sp", lambda e: e.dma_start(out=dst[tt * 128:(tt + 1) * 128, c0:c0 + ncols], in_=STG[s][:, 0:ncols]),
                          reads=[f"stg{s}.0"], writes=[f"d.{id(dst)}.{tt}.{c0}"], chan=f"st{s}")
                return evac
            lhs_fn = lambda k, tt: res16[:, k, tt * 128:(tt + 1) * 128]
            lhs_tok = lambda tt: f"res.{tt // 4}"
            gemm_tm(slabs_of(O_AV, 1024), NCH, lhs_fn, lhs_tok, T, mk_tm(VA, lambda si: si * 512))

            def mk_qk(dst, gcol):
                def evac(tag, si, ec, gi, t0, n, m, b):
                    if gi == 0:
                        cur["s"] = stage_slot()
                    s = cur["s"]
                    q = cnt["q"] % 2
                    cnt["q"] += 1
                    P.add("act", lambda e: e.activation(out=sqt[q][:, 0:n], in_=bank(b)[:, 0:n], func=AF.Square),
                          reads=[f"ps{b}"], writes=[f"sqt{q}"])
                    sb = 4 + q

                    def st1():
                        P.add("pe", lambda e: e.matmul(bank(sb)[:, 0:n], lhsT=ONES[:], rhs=sqt[q][:, 0:n], start=True, stop=True),
                              reads=[f"sqt{q}", "ones"], writes=[f"ps{sb}"])
                        P.add("act", lambda e: e.activation(out=rt[q][:, 0:n], in_=bank(sb)[:, 0:n], func=AF.Ln, scale=1.0 / 128, bias=EPS),
                              reads=[f"ps{sb}"], writes=[f"rt{q}"])
                        P.add("act", lambda e: e.activation(out=rt[q][:, 0:n], in_=rt[q][:, 0:n], func=AF.Exp, scale=-0.5), reads=[f"rt{q}"], writes=[f"rt{q}"])
                        P.add("dve", lambda e: e.scalar_tensor_tensor(out=qn32[q][:, 0:n], in0=bank(b)[:, 0:n], scalar=vec(l, gcol), in1=rt[q][:, 0:n],
                                                                      op0=ALU.mult, op1=ALU.mult),
                              reads=[f"ps{b}", f"rt{q}", "vec"], writes=[f"qn32{q}"])
                        if t0 >= TL:
                            P.add("act", lambda e: e.activation(out=STG[s][:, t0:t0 + n], in_=qn32[q][:, 0:n], func=AF.Copy),
                                  reads=[f"qn32{q}"], writes=[f"stg{s}.{gi}"])
                            return
                        P.add("act", lambda e: e.activation(out=qnb[q][:, 0:n], in_=qn32[q][:, 0:n], func=AF.Copy),
                              reads=[f"qn32{q}"], writes=[f"qnb{q}"])
                        rb = 6 + q

                        def st2():
                            P.add("pe", lambda e: e.matmul(bank(rb)[:, 0:n], lhsT=ROTB[:], rhs=qnb[q][:, 0:n], start=True, stop=True),
                                  reads=[f"qnb{q}", "rotb"], writes=[f"ps{rb}"])
                            P.add("dve", lambda e: e.tensor_tensor(out=t1[q][:, 0:n], in0=bank(rb)[:, 0:n], in1=sinB[:, t0:t0 + n], op=ALU.mult),
                                  reads=[f"ps{rb}", "sinB"], writes=[f"t1{q}"])
                            P.add("pool", lambda e: e.tensor_tensor(out=qn32[q][:, 0:n], in0=qn32[q][:, 0:n], in1=cosB[:, t0:t0 + n], op=ALU.mult),
                                  reads=[f"qn32{q}", "cosB"], writes=[f"qn32{q}"])
                            P.add("dve", lambda e: e.tensor_tensor(out=STG[s][:, t0:t0 + n], in0=qn32[q][:, 0:n], in1=t1[q][:, 0:n], op=ALU.add),
                                  reads=[f"qn32{q}", f"t1{q}"], writes=[f"stg{s}.{gi}"])
                        deferred.append(st2)
                    deferred.append(st1)

                def post(tag, si, ec, m):
                    s = cur["s"]
                    r0 = si * 512 + ec * 128

                    def do():
                        P.add("sp", lambda e: e.dma_start(out=dst[r0:r0 + m, :], in_=STG[s][0:m, :]),
                              reads=[f"stg{s}.{gi}" for gi in range(len(tgs))], writes=[f"d.{id(dst)}.{r0}"], chan=f"st{s}")
                    deferred.append(lambda: deferred.append(lambda: deferred.append(do)))
                return evac, post
            ev, po = mk_qk(QB, V_QN)
            gemm_fm(slabs_of(O_BQ, 1024), NCH, rhs_fn, rhs_tok, tgs, ev, po)
            run_deferred(); run_deferred()
            ev, po = mk_qk(KB, V_KN)
            gemm_fm(slabs_of(O_BK, 256), NCH, rhs_fn, rhs_tok, tgs, ev, po)
            run_deferred(); run_deferred()
            gemm_tm(slabs_of(O_BV, 256), NCH, lhs_fn, lhs_tok, T, mk_tm(VB, lambda si: 0))

            def mla_c(c0, row_base, gcol):
                src3 = wsrc(w_in, D, c0, 512)
                view, tok, per = load_slab(src3, NCH, 512, 3, 4096)
                ss = [stage_slot() for _ in range(4)]
                for gi, (t0, n) in enumerate(tgs):
                    sb = 4 + (gi % 2)
                    for ec in range(4):
                        b = gemm_bank["n"] % 4
                        gemm_bank["n"] += 1
                        for k in range(NCH):
                            P.add("pe", lambda e, b=b, n=n, k=k, ec=ec, gi=gi: e.matmul(
                                bank(b)[:, 0:n], lhsT=view[:, k, ec * 128:(ec + 1) * 128], rhs=rhs_fn(k, gi), start=(k == 0), stop=(k == NCH - 1)),
                                reads=[f"{tok}.{k // per}", rhs_tok(k, gi)], writes=[f"ps{b}"])
                        q = cnt["q"] % 2
                        cnt["q"] += 1
                        P.add("act", lambda e, b=b, n=n, ec=ec: e.activation(out=cqtmp[:, ec, 0:n], in_=bank(b)[:, 0:n], func=AF.Copy),
                              reads=[f"ps{b}"], writes=[f"cqt{ec}"])
                        P.add("act", lambda e, n=n, ec=ec, q=q: e.activation(out=sqt[q][:, 0:n], in_=cqtmp[:, ec, 0:n], func=AF.Square),
                              reads=[f"cqt{ec}"], writes=[f"sqt{q}"])
                        P.add("pe", lambda e, sb=sb, n=n, q=q, ec=ec: e.matmul(bank(sb)[:, 0:n], lhsT=ONES[:], rhs=sqt[q][:, 0:n], start=(ec == 0), stop=(ec == 3)),
                              reads=[f"sqt{q}", "ones"], writes=[f"ps{sb}"])
                    P.add("act", lambda e, sb=sb, n=n: e.activation(out=rt[0][:, 0:n], in_=bank(sb)[:, 0:n], func=AF.Ln, scale=1.0 / 512, bias=EPS),
                          reads=[f"ps{sb}"], writes=["rt0"])
                    P.add("act", lambda e, n=n: e.activation(out=rt[0][:, 0:n], in_=rt[0][:, 0:n], func=AF.Exp, scale=-0.5), reads=["rt0"], writes=["rt0"])
                    for ec in range(4):
                        P.add("dve", lambda e, n=n, ec=ec, t0=t0: e.scalar_tensor_tensor(
                            out=STG[ss[ec]][:, t0:t0 + n], in0=cqtmp[:, ec, 0:n], scalar=vec(l, gcol + ec), in1=rt[0][:, 0:n], op0=ALU.mult, op1=ALU.mult),
                            reads=[f"cqt{ec}", "rt0", "vec"], writes=[f"stg{ss[ec]}.{gi}"])
                for ec in range(4):
                    r0 = row_base + ec * 128
                    P.add("sp", lambda e, r0=r0, ec=ec: e.dma_start(out=CQN[r0:r0 + 128, :], in_=STG[ss[ec]][:, :]),
                          reads=[f"stg{ss[ec]}.{gi}" for gi in range(len(tgs))], writes=[f"d.cqn.{r0}"], chan=f"st{ss[ec]}")
            STG.append(STG32[0].bitcast(BF16)[:, 0:T])
            _orig_slot = stage_slot

            def stage_slot4():
                s = stg_n["n"] % 4
                stg_n["n"] += 1
                return s
            stage_slot = stage_slot4
            mla_c(O_CQ, 0, V_MQN)
            mla_c(O_CKV, 512, V_MKVN)
            P.barrier()
            stage_slot = _orig_slot
            stg_n["n"] = 0

            def ev_ckr(tag, si, ec, gi, t0, n, m, b):
                P.add("act", lambda e: e.activation(out=STG32[1][0:m, t0:t0 + n], in_=bank(b)[0:m, 0:n], func=AF.Copy),
                      reads=[f"ps{b}"], writes=[f"s32.{gi}"])

            def po_ckr(tag, si, ec, m):
                P.add("sp", lambda e: e.dma_start(out=CKR[:, :], in_=STG32[1][0:64, :]),
                      reads=[f"s32.{gi}" for gi in range(len(tgs))], writes=["d.ckr"], chan="st32")
            gemm_fm(slabs_of(O_CKR, 64), NCH, rhs_fn, rhs_tok, tgs, ev_ckr, po_ckr)

            ev, po = mk_plain(SG, lambda si, ec: si * 512 + ec * 128, AF.Sigmoid)
            gemm_fm(slabs_of(O_GA, 6144), NCH, rhs_fn, rhs_tok, tgs, ev, po)
            P.barrier()

        def phase_mla(l):
            w_uq = inp(f"w_uq{l}", [512, 1536])
            w_ukv = inp(f"w_ukv{l}", [512, 2048])
            STG = [arb(12288 + i * 1152, 1152) for i in range(3)]
            cosC = arf(15744, 2048)
            sinC = arf(17792, 2048)
            qr32 = [arf(19840 + i * 512, 512) for i in range(2)]
            qrb = [arb(20864 + i * 256, 256) for i in range(2)]
            t1 = [arf(21376 + i * 512, 512) for i in range(2)]
            ckr = arf(22400, 2304)
            krp = arb(24704, 1152)
            P.add("sp", lambda e: e.dma_start(out=cosC[0:64, :], in_=ropeC[0]), writes=["cosC"], chan="k0")
            P.add("sp", lambda e: e.dma_start(out=sinC[0:64, :], in_=ropeC[1]), writes=["sinC"], chan="k1")
            res4 = RESb[:, 0:4 * T].rearrange("p (c t) -> p c t", c=4)
            tgs = tgroups(T)
            rhs_fn = lambda k, gi: res4[:, k, tgs[gi][0]:tgs[gi][0] + tgs[gi][1]]
            rhs_tok = lambda k, gi: "res"
            cnt = {"i": 0, "q": 0}
            cur = {}

            def stage_slot():
                s = stg_n["n"] % 3
                stg_n["n"] += 1
                return s

            def rope64(src_ap, src_reads, n, t0, out_ap, out_writes, q):
                if t0 >= TL:
                    P.add("act", lambda e: e.activation(out=out_ap, in_=src_ap, func=AF.Copy), reads=src_reads, writes=out_writes)
                    return
                P.add("act", lambda e: e.activation(out=qrb[q][0:64, 0:n], in_=src_ap, func=AF.Copy), reads=src_reads, writes=[f"qrb{q}"])
                rb = 6 + q
                P.add("pe", lambda e: e.matmul(bank(rb)[0:64, 0:n], lhsT=ROTC[:], rhs=qrb[q][0:64, 0:n], start=True, stop=True),
                      reads=[f"qrb{q}", "rotc"], writes=[f"ps{rb}"])
                P.add("dve", lambda e: e.tensor_tensor(out=t1[q][0:64, 0:n], in0=bank(rb)[0:64, 0:n], in1=sinC[0:64, t0:t0 + n], op=ALU.mult),
                      reads=[f"ps{rb}", "sinC"], writes=[f"t1{q}"])
                P.add("pool", lambda e: e.tensor_tensor(out=src_ap, in0=src_ap, in1=cosC[0:64, t0:t0 + n], op=ALU.mult),
                      reads=src_reads + ["cosC"], writes=src_reads)
                P.add("dve", lambda e: e.tensor_tensor(out=out_ap, in0=src_ap, in1=t1[q][0:64, 0:n], op=ALU.add),
                      reads=src_reads + [f"t1{q}"], writes=out_writes)

            P.add("sp", lambda e: e.dma_start(out=ckr[0:64, :], in_=CKR[:, :]), writes=["ckr"], chan="k2")
            for gi, (t0, n) in enumerate(tgs):
                q = gi % 2
                P.add("act", lambda e, q=q, n=n, t0=t0: e.activation(out=qr32[q][0:64, 0:n], in_=ckr[0:64, t0:t0 + n], func=AF.Copy),
                      reads=["ckr"], writes=[f"qr32{q}"])
                rope64(qr32[q][0:64, 0:n], [f"qr32{q}"], n, t0, krp[0:64, t0:t0 + n], [f"krp.{gi}"], q)
            P.add("sp", lambda e: e.dma_start(out=KRP[:, :], in_=krp[0:64, :]), reads=[f"krp.{gi}" for gi in range(len(tgs))], writes=["d.krp"], chan="k3")

            for j in range(4):
                P.add("sp", lambda e, j=j: e.dma_start(out=res4[:, j, :], in_=CQN[j * 128:(j + 1) * 128, :]), writes=["res"], chan=f"k{4 + j}")

            def ev_q(tag, si, ec, gi, t0, n, m, b):
                h, part = tag
                if gi == 0:
                    cur["s"] = stage_slot()
                s = cur["s"]
                if part == 0:
                    i = cnt["i"]
                    cnt["i"] += 1
                    copy_alt(i, STG[s][:, t0:t0 + n], bank(b)[:, 0:n], [f"ps{b}"], [f"stg{s}.{gi}"])
                else:
                    q = cnt["q"] % 2
                    cnt["q"] += 1
                    P.add("act", lambda e: e.activation(out=qr32[q][0:64, 0:n], in_=bank(b)[0:64, 0:n], func=AF.Copy),
                          reads=[f"ps{b}"], writes=[f"qr32{q}"])

                    def later():
                        rope64(qr32[q][0:64, 0:n], [f"qr32{q}"], n, t0, STG[s][0:64, t0:t0 + n], [f"stg{s}.{gi}"], q)
                    deferred.append(later)

            def po_q(tag, si, ec, m):
                h, part = tag
                s = cur["s"]
                dst = QCN[h * 128:(h + 1) * 128, :] if part == 0 else QCR[h * 64:(h + 1) * 64, :]

                def do():
                    P.add("sp", lambda e: e.dma_start(out=dst, in_=STG[s][0:m, :]),
                          reads=[f"stg{s}.{gi}" for gi in range(len(tgs))], writes=[f"d.q{h}.{part}"], chan=f"st{s}")
                deferred.append(lambda: deferred.append(do))

            def small_gemm(wap, ncols_total, col_specs, evac, post):
                view, tok, per = load_slab(wsrc(wap, 512, 0, ncols_total), 4, ncols_total, 3, 4096)
                for (c0, m, tag) in col_specs:
                    for gi, (t0, n) in enumerate(tgs):
                        b = gemm_bank["n"] % 4
                        gemm_bank["n"] += 1
                        for k in range(4):
                            P.add("pe", lambda e, b=b, m=m, n=n, k=k, c0=c0, gi=gi: e.matmul(
                                bank(b)[0:m, 0:n], lhsT=view[:, k, c0:c0 + m], rhs=rhs_fn(k, gi), start=(k == 0), stop=(k == 3)),
                                reads=[f"{tok}.{k // per}", "res"], writes=[f"ps{b}"])
                        run_deferred()
                        evac(tag, 0, 0, gi, t0, n, m, b)
                    post(tag, 0, 0, m)
                run_deferred(); run_deferred(); run_deferred()
                return view, tok, per
            specs = []
            for h in range(8):
                specs.append((h * 192, 128, (h, 0)))
                specs.append((h * 192 + 128, 64, (h, 1)))
            small_gemm(w_uq, 1536, specs, ev_q, po_q)
            P.barrier()

            for j in range(4):
                P.add("sp", lambda e, j=j: e.dma_start(out=res4[:, j, :], in_=CQN[512 + j * 128:512 + (j + 1) * 128, :]), writes=["res"], chan=f"k{4 + j}")

            def ev_k(tag, si, ec, gi, t0, n, m, b):
                if gi == 0:
                    cur["s"] = stage_slot()
                s = cur["s"]
                i = cnt["i"]
                cnt["i"] += 1
                copy_alt(i, STG[s][:, t0:t0 + n], bank(b)[:, 0:n], [f"ps{b}"], [f"stg{s}.{gi}"])

            def po_k(tag, si, ec, m):
                h = tag
                s = cur["s"]
                P.add("sp", lambda e: e.dma_start(out=KCN[h * 128:(h + 1) * 128, :], in_=STG[s][:, :]),
                      reads=[f"stg{s}.{gi}" for gi in range(len(tgs))], writes=[f"d.k{h}"], chan=f"st{s}")
            view, tok, per = small_gemm(w_ukv, 2048, [(h * 256, 128, h) for h in range(8)], ev_k, po_k)
            for tt in range(T // 128):
                s = stage_slot()
                for h in range(8):
                    b = gemm_bank["n"] % 4
                    gemm_bank["n"] += 1
                    for k in range(4):
                        P.add("pe", lambda e, b=b, k=k, h=h, tt=tt: e.matmul(
                            bank(b)[:, 0:128], lhsT=res4[:, k, tt * 128:(tt + 1) * 128], rhs=view[:, k, h * 256 + 128:h * 256 + 256],
                            start=(k == 0), stop=(k == 3)),
                            reads=[f"{tok}.{k // per}", "res"], writes=[f"ps{b}"])
                    i = cnt["i"]
                    cnt["i"] += 1
                    copy_alt(i, STG[s][:, h * 128:(h + 1) * 128], bank(b)[:, 0:128], [f"ps{b}"], [f"stg{s}.{h}"])
                P.add("sp", lambda e, s=s, tt=tt: e.dma_start(out=VC[tt * 128:(tt + 1) * 128, :], in_=STG[s][:, 0:1024]),
                      reads=[f"stg{s}.{h}" for h in range(8)], writes=[f"d.vc{tt}"], chan=f"st{s}")
            P.barrier()

        def phase_attn(l, ctx_q, bg=None):
            abias = inp(f"abias{l}", [8 * 20 * 128, 512])
            Vall = arb(0, 9216)
            Kb = [arb(9216 + i * 1152, 1152) for i in range(2)]
            Qb = [arb(11520 + i * 1152, 1152) for i in range(2)]
            KR = arb(13824, 1152)
            QR = [arb(14976 + i * 1152, 1152) for i in range(2)]
            PT = [arb(17280 + i * 256, 256) for i in range(4)]
            BI = [arf(18304 + i * 512, 512) for i in range(3)]
            SS = [arf(19840 + i * 512, 512) for i in range(2)]
            RC = [arf(20864 + i * 512, 512) for i in range(2)]
            OS = [arb(21888 + i * 1152, 1152) for i in range(2)]
            ctr = {"step": 0, "blk": 0, "bi": 0, "os": 0, "kq": 0}
            pend = []

            def flush(depth):
                while len(pend) > depth:
                    pend.pop(0)()

            def qblocks():
                bl = [(g * 512, 512, g) for g in range(4)]
                if ctx_q:
                    bl.append((TL, 256, 4))
                return bl

            def run_head(mixer, h, kT, kT_tok, kR, qT, qT_tok, qR, v_fn, scale, orow, os_i):
                for (q0, n, g) in qblocks():
                    if g == 4:
                        chunks = [16, 17]
                    elif mixer == "a":
                        chunks = list(_a_chunks(g)) + [16, 17]
                    else:
                        chunks = list(range(18))
                    blk = ctr["blk"] % 2
                    ctr["blk"] += 1
                    ob, db = 3 + blk, 5 + blk
                    nck = len(chunks)
                    for ci, m in enumerate(chunks):
                        sbk = ctr["step"] % 3
                        pt = ctr["step"] % 4
                        ctr["step"] += 1
                        P.add("pe", lambda e, sbk=sbk, m=m, q0=q0, n=n: e.matmul(
                            bank(sbk)[:, 0:n], lhsT=kT[:, m * 128:(m + 1) * 128], rhs=qT[:, q0:q0 + n], start=True, stop=(kR is None)),
                            reads=[kT_tok, qT_tok], writes=[f"ps{sbk}"])
                        if kR is not None:
                            P.add("pe", lambda e, sbk=sbk, m=m, q0=q0, n=n: e.matmul(
                                bank(sbk)[:, 0:n], lhsT=kR[0:64, m * 128:(m + 1) * 128], rhs=qR[0:64, q0:q0 + n], start=False, stop=True),
                                reads=["kr", qT_tok], writes=[f"ps{sbk}"])
                        if mixer == "a" and m < 16:
                            bi = ctr["bi"] % 3
                            ctr["bi"] += 1
                            tid = (h * 20 + _abias_tile_id(g, m)) * 128
                            P.add("sp", lambda e, bi=bi, tid=tid: e.dma_start(out=BI[bi], in_=abias[tid:tid + 128, :]), writes=[f"bi{bi}"], chan=f"bi{bi}")
                            ss = ctr["step"] % 2
                            P.add("dve", lambda e, ss=ss, sbk=sbk, bi=bi, n=n: e.scalar_tensor_tensor(
                                out=SS[ss][:, 0:n], in0=bank(sbk)[:, 0:n], scalar=scale, in1=BI[bi][:, 0:n], op0=ALU.mult, op1=ALU.add),
                                reads=[f"ps{sbk}", f"bi{bi}"], writes=[f"ss{ss}"])
                            P.add("act", lambda e, ss=ss, pt=pt, n=n: e.activation(out=PT[pt][:, 0:n], in_=SS[ss][:, 0:n], func=AF.Exp),
                                  reads=[f"ss{ss}"], writes=[f"pt{pt}"])
                        else:
                            P.add("act", lambda e, sbk=sbk, pt=pt, n=n: e.activation(out=PT[pt][:, 0:n], in_=bank(sbk)[:, 0:n], func=AF.Exp, scale=scale),
                                  reads=[f"ps{sbk}"], writes=[f"pt{pt}"])

                        def pv(ci=ci, m=m, pt=pt, n=n, ob=ob, db=db, nck=nck, q0=q0, blk=blk):
                            P.add("pe", lambda e: e.matmul(bank(ob)[:, 0:n], lhsT=v_fn(m), rhs=PT[pt][:, 0:n], start=(ci == 0), stop=(ci == nck - 1)),
                                  reads=[f"pt{pt}", f"vall.{m // 6}"], writes=[f"ps{ob}"])
                            P.add("pe", lambda e: e.matmul(bank(db)[:, 0:n], lhsT=ONES[:], rhs=PT[pt][:, 0:n], start=(ci == 0), stop=(ci == nck - 1)),
                                  reads=[f"pt{pt}", "ones"], writes=[f"ps{db}"])
                            if ci == nck - 1:
                                if mixer == "a":
                                    P.add("act", lambda e: e.activation(out=RC[blk][:, 0:n], in_=bank(db)[:, 0:n], func=AF.Ln), reads=[f"ps{db}"], writes=[f"rc{blk}"])
                                    P.add("act", lambda e: e.activation(out=RC[blk][:, 0:n], in_=RC[blk][:, 0:n], func=AF.Exp, scale=-1.0), reads=[f"rc{blk}"], writes=[f"rc{blk}"])
                                else:
                                    P.add("dve", lambda e: e.reciprocal(out=RC[blk][:, 0:n], in_=bank(db)[:, 0:n]), reads=[f"ps{db}"], writes=[f"rc{blk}"])
                                P.add("dve", lambda e: e.tensor_tensor(out=OS[os_i][:, q0:q0 + n], in0=bank(ob)[:, 0:n], in1=RC[blk][:, 0:n], op=ALU.mult),
                                      reads=[f"ps{ob}", f"rc{blk}"], writes=[f"os{os_i}.{q0}"])
                        pend.append(pv)
                        flush(2)
                        if bg is not None:
                            ctr["bgc"] = ctr.get("bgc", 0) + 1
                            if ctr["bgc"] % 24 == 0:
                                next(bg, None)
                flush(0)
                ntok = T if ctx_q else TL
                P.add("sp", lambda e: e.dma_start(out=OT[orow:orow + 128, 0:ntok], in_=OS[os_i][:, 0:ntok]),
                      reads=[f"os{os_i}.{q0}" for (q0, n, g) in qblocks()], writes=[f"d.ot{orow}"], chan=f"os{os_i}")

            def load_row(buf, src, tokname, chan, rows=128):
                P.add("sp", lambda e: e.dma_start(out=buf[0:rows, :], in_=src), writes=[tokname], chan=chan)

            def load_v(src, width):
                v3 = Vall[:, 0:18 * width].rearrange("p (c d) -> p c d", c=18)
                s3 = src.rearrange("(c p) d -> p c d", p=128)
                for j in range(3):
                    P.add("sp", lambda e, j=j: e.dma_start(out=v3[:, 6 * j:6 * j + 6, :], in_=s3[:, 6 * j:6 * j + 6, :]),
                          writes=[f"vall.{j}"], chan=f"v{j}")
                return v3

            v3 = load_v(VA, 1024)

            def ld_a(h):
                i = h % 2
                load_row(Kb[i], QKA[1024 + h * 128:1024 + (h + 1) * 128, :], f"k{i}", f"k{i}")
                load_row(Qb[i], QKA[h * 128:(h + 1) * 128, :], f"q{i}", f"q{i}")
            ld_a(0)
            for h in range(8):
                i = h % 2
                if h + 1 < 8:
                    ld_a(h + 1)
                run_head("a", h, Kb[i], f"k{i}", None, Qb[i], f"q{i}", None, lambda m, h=h, v3=v3: v3[:, m, h * 128:(h + 1) * 128], 128 ** -0.5, h * 128, h % 2)
            P.barrier()
            v3 = load_v(VB, 256)

            def ld_b(h):
                kvh = h // 4
                if h % 4 == 0:
                    load_row(Kb[kvh % 2], KB[kvh * 128:(kvh + 1) * 128, :], f"k{kvh % 2}", f"k{kvh % 2}")
                load_row(Qb[h % 2], QB[h * 128:(h + 1) * 128, :], f"q{h % 2}", f"q{h % 2}")
            ld_b(0)
            for h in range(8):
                kvh = h // 4
                ki = kvh % 2
                i = h % 2
                if h + 1 < 8:
                    ld_b(h + 1)
                run_head("b", h, Kb[ki], f"k{ki}", None, Qb[i], f"q{i}", None, lambda m, kvh=kvh, v3=v3: v3[:, m, kvh * 128:(kvh + 1) * 128], 128 ** -0.5, 1024 + h * 128, h % 2)
            P.barrier()
            v3 = load_v(VC, 1024)
            load_row(KR, KRP[:, :], "kr", "kr", rows=64)

            def ld_c(h):
                i = h % 2
                load_row(Kb[i], KCN[h * 128:(h + 1) * 128, :], f"k{i}", f"k{i}")
                load_row(Qb[i], QCN[h * 128:(h + 1) * 128, :], f"q{i}", f"q{i}")
                load_row(QR[i], QCR[h * 64:(h + 1) * 64, :], f"q{i}", f"qr{i}", rows=64)
            ld_c(0)
            for h in range(8):
                i = h % 2
                if h + 1 < 8:
                    ld_c(h + 1)
                run_head("c", h, Kb[i], f"k{i}", KR, Qb[i], f"q{i}", QR[i], lambda m, h=h, v3=v3: v3[:, m, h * 128:(h + 1) * 128], 192 ** -0.5, 2048 + h * 128, h % 2)
            if bg is not None:
                for _ in bg:
                    pass
            P.barrier()

        def phase_merge(l, ntok):
            w_br = inp(f"w_br{l}", [3072, D])
            half = ntok // 2
            STG = [arb(9216 + i * 1152, 1152) for i in range(3)]
            GT = [arb(12672 + i * 1728, 1728).rearrange("p (b t) -> p b t", b=3) for i in range(2)]
            TM = [arf(16128 + i * 512, 512) for i in range(6)]
            res24 = RESb[:, 0:24 * half].rearrange("p (c t) -> p c t", c=24)
            tw = 384 if half % 384 == 0 else 512
            sg4 = SG.rearrange("(b c p) t -> c p b t", b=3, p=128)
            cnt = {"g": 0, "s": 0, "bk": 0, "g2": 0}
            for hf in range(2):
                h0 = hf * half
                for j in range(6):
                    src = OT[j * 512:(j + 1) * 512, h0:h0 + half].rearrange("(c p) t -> p c t", p=128)
                    P.add("sp", lambda e, j=j, src=src: e.dma_start(out=res24[:, 4 * j:4 * j + 4, :], in_=src),
                          writes=[f"res.{j}"], chan=f"r{j}")
                strm = slab_stream([(wsrc(w_br, 3072, ds_ * 256, 256), 24, 256) for ds_ in range(8)], 3, 3072, split=3)

                def gate_load(dc_):
                    gi2 = dc_ % 2
                    P.add("sp", lambda e, gt=GT[gi2], dc_=dc_, h0=h0: e.dma_start(out=gt[:, :, 0:half], in_=sg4[dc_][:, :, h0:h0 + half]),
                          writes=[f"gt{gi2}"], chan=f"gt{gi2}")
                gate_load(0)
                for ds in range(8):
                    view, tok, per = next(strm)
                    for ec in range(2):
                        dc = ds * 2 + ec
                        gi_ = dc % 2
                        gt = GT[gi_]
                        if dc + 1 < 16:
                            gate_load(dc + 1)
                        s = cnt["s"] % 3
                        cnt["s"] += 1
                        for ti in range(half // tw):
                            t0 = ti * tw
                            banks = []
                            for br in range(3):
                                b = cnt["bk"] % 6
                                cnt["bk"] += 1
                                banks.append(b)
                                for k in range(8):
                                    kk = br * 8 + k
                                    P.add("pe", lambda e, b=b, kk=kk, ec=ec, t0=t0, view=view: e.matmul(
                                        bank(b)[:, 0:tw], lhsT=view[:, kk, ec * 128:(ec + 1) * 128], rhs=res24[:, kk, t0:t0 + tw],
                                        start=(kk % 8 == 0), stop=(kk % 8 == 7)),
                                        reads=[f"{tok}.{kk // per}", f"res.{kk // 4}"], writes=[f"ps{b}"])
                            mset = cnt["g2"] % 2
                            cnt["g2"] += 1
                            ta, tb, tc = TM[mset * 3], TM[mset * 3 + 1], TM[mset * 3 + 2]
                            na, nb, ncn = f"tm{mset * 3}", f"tm{mset * 3 + 1}", f"tm{mset * 3 + 2}"
                            P.add("dve", lambda e, ta=ta, b=banks[0], gt=gt, t0=t0: e.tensor_tensor(out=ta[:, 0:tw], in0=bank(b)[:, 0:tw], in1=gt[:, 0, t0:t0 + tw], op=ALU.mult),
                                  reads=[f"ps{banks[0]}", f"gt{gi_}"], writes=[na])
                            P.add("dve", lambda e, tb=tb, b=banks[1], gt=gt, t0=t0: e.tensor_tensor(out=tb[:, 0:tw], in0=bank(b)[:, 0:tw], in1=gt[:, 1, t0:t0 + tw], op=ALU.mult),
                                  reads=[f"ps{banks[1]}", f"gt{gi_}"], writes=[nb])
                            P.add("dve", lambda e, tc=tc, b=banks[2], gt=gt, t0=t0: e.tensor_tensor(out=tc[:, 0:tw], in0=bank(b)[:, 0:tw], in1=gt[:, 2, t0:t0 + tw], op=ALU.mult),
                                  reads=[f"ps{banks[2]}", f"gt{gi_}"], writes=[ncn])
                            P.add("pool", lambda e, ta=ta, tb=tb: e.tensor_tensor(out=ta[:, 0:tw], in0=ta[:, 0:tw], in1=tb[:, 0:tw], op=ALU.add),
                                  reads=[na, nb], writes=[na])
                            P.add("pool", lambda e, ta=ta, tc=tc, s=s, t0=t0: e.tensor_tensor(out=STG[s][:, t0:t0 + tw], in0=ta[:, 0:tw], in1=tc[:, 0:tw], op=ALU.add),
                                  reads=[na, ncn], writes=[f"stg{s}.{ti}"])
                        P.add("sp", lambda e, s=s, dc=dc, h0=h0: e.dma_start(out=YT[dc * 128:(dc + 1) * 128, h0:h0 + half], in_=STG[s][:, 0:half]),
                              reads=[f"stg{s}.{ti}" for ti in range(half // tw)], writes=[f"d.yt{dc}.{hf}"], chan=f"st{s}")
                P.barrier()

        def load_res(src, kc, ntok, t0=0):
            r3 = RESb[:, 0:kc * ntok].rearrange("p (c t) -> p c t", c=kc)
            step = 4
            for j in range(0, kc, step):
                j1 = min(kc, j + step)
                s3 = src[j * 128:j1 * 128, t0:t0 + ntok].rearrange("(c p) t -> p c t", p=128)
                P.add("sp", lambda e, j=j, j1=j1, s3=s3: e.dma_start(out=r3[:, j:j1, :], in_=s3), writes=[f"res.{j // step}"], chan=f"r{j // step}")
            return r3, step

        def phase_wo(l, ntok):
            w_o = inp(f"w_o{l}", [D, D])
            STG32 = [arf(12288 + i * 2304, 2304) for i in range(2)]
            r3, step = load_res(YT, NCH, ntok)
            tgs = tgroups(ntok)
            cur = {"s": 0, "n": 0}

            def evac(tag, si, ec, gi, t0, n, m, b):
                if gi == 0:
                    cur["s"] = cur["n"] % 2
                    cur["n"] += 1
                s = cur["s"]
                copy_alt(gi, STG32[s][:, t0:t0 + n], bank(b)[:, 0:n], [f"ps{b}"], [f"s32{s}.{gi}"])

            def post(tag, si, ec, m):
                s = cur["s"]
                r0 = si * 512 + ec * 128
                P.add("sp", lambda e: e.dma_start(out=UT[r0:r0 + 128, 0:ntok], in_=STG32[s][:, 0:ntok]),
                      reads=[f"s32{s}.{gi}" for gi in range(len(tgs))], writes=[f"d.ut{r0}"], chan=f"s32{s}")
            slabs = [(wsrc(w_o, D, s * 512, 512), 512, s) for s in range(4)]
            gemm_fm(slabs, NCH, lambda k, gi: r3[:, k, tgs[gi][0]:tgs[gi][0] + tgs[gi][1]], lambda k, gi: f"res.{k // 4}", tgs, evac, post)
            P.barrier()

        def phase_ffn1(l, ntok):
            w1 = inp(f"w_ff1{l}", [D, DFF])
            w3 = inp(f"w_ff3{l}", [D, DFF])
            STG = [arb(16384 + i * 1152, 1152) for i in range(3)]
            TM = [arf(19840 + i * 512, 512) for i in range(3)]
            tgs = tgroups(ntok)
            cnt = {"s": 0, "b": 0, "t": 0}
            specs = []
            for fs_ in range(11):
                specs.append((wsrc(w1, D, fs_ * 512, 512), NCH, 512))
                specs.append((wsrc(w3, D, fs_ * 512, 512), NCH, 512))
            strm = slab_stream(specs, 4, 4096, ahead=3)
            for fs in range(11):
                va, ta, pa = next(strm)
                vg, tg_, pg = next(strm)
                for ec in range(4):
                    s = cnt["s"] % 3
                    cnt["s"] += 1
                    for gi, (t0, n) in enumerate(tgs):
                        ba = cnt["b"] % 8
                        bg = (cnt["b"] + 1) % 8
                        cnt["b"] += 2
                        for (view, tok, per, b) in ((va, ta, pa, ba), (vg, tg_, pg, bg)):
                            for k in range(NCH):
                                P.add("pe", lambda e, b=b, n=n, view=view, k=k, ec=ec, t0=t0: e.matmul(
                                    bank(b)[:, 0:n], lhsT=view[:, k, ec * 128:(ec + 1) * 128], rhs=res16[:, k, t0:t0 + n],
                                    start=(k == 0), stop=(k == NCH - 1)),
                                    reads=[f"{tok}.{k // per}", f"res.{t0 // 512}"], writes=[f"ps{b}"])
                        ti = cnt["t"] % 3
                        cnt["t"] += 1
                        P.add("act", lambda e, ti=ti, ba=ba, n=n: e.activation(out=TM[ti][:, 0:n], in_=bank(ba)[:, 0:n], func=AF.Silu),
                              reads=[f"ps{ba}"], writes=[f"tm{ti}"])
                        P.add("dve", lambda e, ti=ti, bg=bg, n=n, s=s, t0=t0: e.tensor_tensor(out=STG[s][:, t0:t0 + n], in0=bank(bg)[:, 0:n], in1=TM[ti][:, 0:n], op=ALU.mult),
                              reads=[f"ps{bg}", f"tm{ti}"], writes=[f"stg{s}.{gi}"])
                    r0 = fs * 512 + ec * 128
                    P.add("sp", lambda e, s=s, r0=r0: e.dma_start(out=ACTT[r0:r0 + 128, 0:ntok], in_=STG[s][:, 0:ntok]),
                          reads=[f"stg{s}.{gi}" for gi in range(len(tgs))], writes=[f"d.act{r0}"], chan=f"st{s}")
            P.barrier()

        def phase_ffn2(l, ntok):
            w2 = inp(f"w_ff2{l}", [DFF, D])
            S32 = [arf(11264 + i * 768, 768) for i in range(4)]
            parts = []
            t0 = 0
            while t0 < ntok:
                n = min(768, ntok - t0)
                parts.append((t0, n))
                t0 += n
            cnt = {"s": 0, "b": 0}
            for (p0, pn) in parts:
                r3, step = load_res(ACTT, FCH, pn, t0=p0)
                tiles = [(0, pn // 2), (pn // 2, pn // 2)]
                strm = slab_stream([(wsrc(w2, DFF, ds_ * 256, 256), FCH, 256) for ds_ in range(8)], 2, 5632, split=4)
                for ds in range(8):
                    view, tok, per = next(strm)
                    for ec in range(2):
                        s = cnt["s"] % 4
                        cnt["s"] += 1
                        for ti, (t0, n) in enumerate(tiles):
                            b = cnt["b"] % 4
                            cnt["b"] += 1
                            for k in range(FCH):
                                P.add("pe", lambda e, b=b, n=n, view=view, k=k, ec=ec, t0=t0, r3=r3: e.matmul(
                                    bank(b)[:, 0:n], lhsT=view[:, k, ec * 128:(ec + 1) * 128], rhs=r3[:, k, t0:t0 + n],
                                    start=(k == 0), stop=(k == FCH - 1)),
                                    reads=[f"{tok}.{k // per}", f"res.{k // step}"], writes=[f"ps{b}"])
                            copy_alt(ti, S32[s][:, t0:t0 + n], bank(b)[:, 0:n], [f"ps{b}"], [f"s32{s}.{ti}"])
                        r0 = ds * 256 + ec * 128
                        P.add("sp", lambda e, s=s, r0=r0, p0=p0, pn=pn: e.dma_start(out=U2T[r0:r0 + 128, p0:p0 + pn], in_=S32[s][:, 0:pn]),
                              reads=[f"s32{s}.0", f"s32{s}.1"], writes=[f"d.u2{r0}.{p0}"], chan=f"s32{s}")
                P.barrier()

        def program():
            phase_ada_first()
            phase_rn(0, X[0], None, None, None, (0, 1, 0), T)
            phase_end(0, "rn0")
            if stop["flag"]:
                return
            xi = 0
            for l in range(cfg.layers):
                last = (l == cfg.layers - 1)
                nq = TL if last else T
                phase_inproj(l)
                phase_end(l, "inproj")
                if stop["flag"]:
                    return
                phase_mla(l)
                phase_end(l, "mla")
                if stop["flag"]:
                    return
                bg = ada_units(ada_background_items()) if l == 0 else None
                phase_attn(l, ctx_q=not last, bg=bg)
                phase_end(l, "attn")
                if stop["flag"]:
                    return
                phase_merge(l, nq)
                phase_end(l, "merge")
                if stop["flag"]:
                    return
                phase_wo(l, nq)
                phase_end(l, "wo")
                if stop["flag"]:
                    return
                phase_rn(l, X[xi], X[xi + 1], UT, 2, (3, 4, l), nq)
                xi += 1
                phase_end(l, "rn1")
                if stop["flag"]:
                    return
                phase_ffn1(l, nq)
                phase_end(l, "ffn1")
                if stop["flag"]:
                    return
                phase_ffn2(l, nq)
                phase_end(l, "ffn2")
                if stop["flag"]:
                    return
                if last:
                    phase_rn(l, X[xi], None, U2T, 5, None, nq, out_final=True)
                else:
                    phase_rn(l, X[xi], X[xi + 1], U2T, 5, (0, 1, l + 1), T)
                    xi += 1
                phase_end(l, "rn2")
                if stop["flag"]:
                    return

        program()
        P.emit()
    return nc, in_shapes, out_names


def prep_core_inputs(inputs, b, shared):
    x = np.asarray(inputs["x"][b], np.float32)
    ctx = np.asarray(inputs["ctx"][b], np.float32)
    d = dict(shared)
    d["xT"] = np.ascontiguousarray(np.concatenate([x.T, ctx.T], axis=1))
    cv = np.stack([_pc(inputs["c"][b]), _pc(inputs["c_ctx"])], axis=2)
    d["cvec"] = np.ascontiguousarray(cv.reshape(128, 32))
    return d


def prep_shared(inputs, layers=2):
    (ropeB, rotB), (ropeC, rotC) = _rope_tables()
    sh = {"rotB": rotB, "rotC": rotC, "ropeB": ropeB, "ropeC": ropeC}
    vecs = []
    dr, dc, va = _abias_index()
    for l in range(layers):
        cols = [_pc(inputs["g_pre1"][l]), _pc(inputs["g_post1"][l]), _pc(inputs["g_pre2"][l]), _pc(inputs["g_post2"][l]),
                _pc(inputs["b_ada"][l]), _pc(inputs["gqa_q_norm"][l]), _pc(inputs["gqa_k_norm"][l]),
                _pc(inputs["mla_q_norm"][l]), _pc(inputs["mla_kv_norm"][l])]
        vecs.append(np.concatenate(cols, axis=1))
        rpb = np.asarray(inputs["rpb"][l], np.float32)
        g = rpb[:, dr, dc]
        g = np.where(va[None], g, np.float32(NEG)).astype(np.float32)
        sh[f"abias{l}"] = np.ascontiguousarray(g.reshape(8 * 20 * 128, 512))
        sh[f"w_ada{l}"] = np.asarray(inputs["w_ada"][l], np.float32)
        sh[f"w_in{l}"] = np.asarray(inputs["w_in"][l], np.float32)
        sh[f"w_uq{l}"] = np.asarray(inputs["w_uq"][l], np.float32)
        sh[f"w_ukv{l}"] = np.asarray(inputs["w_ukv"][l], np.float32)
        sh[f"w_br{l}"] = np.ascontiguousarray(np.concatenate(
            [inputs["w_br_a"][l], inputs["w_br_b"][l], inputs["w_br_c"][l]], axis=0).astype(np.float32))
        sh[f"w_o{l}"] = np.asarray(inputs["w_o"][l], np.float32)
        sh[f"w_ff1{l}"] = np.asarray(inputs["w_ff1"][l], np.float32)
        sh[f"w_ff3{l}"] = np.asarray(inputs["w_ff3"][l], np.float32)
        sh[f"w_ff2{l}"] = np.asarray(inputs["w_ff2"][l], np.float32)
    sh["vecs"] = np.ascontiguousarray(np.concatenate(vecs, axis=1))
    if layers == 1:
        sh["vecs"] = np.ascontiguousarray(np.concatenate([sh["vecs"], np.zeros_like(sh["vecs"])], axis=1))
    return sh


_CACHE = {}


def kernel(**inputs):
    n = 8
    if "nc" not in _CACHE:
        _CACHE["nc"] = build(Cfg())
    nc, in_shapes, out_names = _CACHE["nc"]
    shared = prep_shared(inputs)
    in_maps = []
    for b in range(n):
        d = prep_core_inputs(inputs, b, shared)
        in_maps.append({k: d[k] for k in in_shapes})
    res = run_bass_kernel_spmd(nc, in_maps, core_ids=list(range(n)))
    out = np.stack([np.ascontiguousarray(r["outT"].T) for r in res.results], axis=0)
    return out.astype(np.float32)
```

```python
import contextlib
import numpy as np
import concourse.bass as bass
import concourse.mybir as mybir
from concourse.bass_utils import run_bass_kernel_spmd

F32 = mybir.dt.float32
BF16 = mybir.dt.bfloat16
ALU = mybir.AluOpType
AF = mybir.ActivationFunctionType

D = 2048
TL = 2048
TC = 256
T = TL + TC
NCH = 16
DFF = 5632
FCH = 44
IN_W = 11840
EPS = 1e-6
NEG = -30000.0
NV = 170
O_AQ, O_AK, O_AV, O_BQ, O_BK, O_BV, O_CQ, O_CKV, O_CKR, O_GA = 0, 1024, 2048, 3072, 4096, 4352, 4608, 5120, 5632, 5696

ENGS = ["sp", "pe", "dve", "act", "pool"]


class Op:
    __slots__ = ("eng", "fn", "deps", "signal", "count", "chan", "chan_val", "kinds")

    def __init__(self, eng, fn):
        self.eng = eng
        self.fn = fn
        self.deps = ()
        self.signal = False
        self.count = 0
        self.chan = None
        self.chan_val = 0
        self.kinds = {}


class Prog:
    def __init__(self, nc, same_engine_sync=True):
        self.nc = nc
        self.streams = {e: [] for e in ENGS}
        self.last_w = {}
        self.readers = {}
        self.chan_count = {}
        self.last_dma = {}
        self.same_engine_sync = same_engine_sync

    def add(self, eng, fn, reads=(), writes=(), chan=None):
        op = Op(eng, fn)
        deps = {}
        kinds = {}
        lw = self.last_w
        rdrs = self.readers
        for r in reads:
            w = lw.get(r)
            if w is not None:
                deps[id(w)] = w
                kinds[id(w)] = 2
        for t in writes:
            w = lw.get(t)
            if w is not None:
                deps[id(w)] = w
                kinds.setdefault(id(w), 1)
            lst = rdrs.get(t)
            if lst:
                for rd in lst:
                    deps[id(rd)] = rd
                    kinds.setdefault(id(rd), 1)
        op.deps = list(deps.values())
        op.kinds = {k: v for k, v in kinds.items()}
        for r in reads:
            l = rdrs.get(r)
            if l is None:
                rdrs[r] = [op]
            elif chan is None:
                for i, o in enumerate(l):
                    if o.eng == eng and o.chan is None:
                        l[i] = op
                        break
                else:
                    l.append(op)
            else:
                l.append(op)
        for t in writes:
            lw[t] = op
            rdrs[t] = []
        if chan is not None:
            c = self.chan_count.get(chan, 0) + 1
            self.chan_count[chan] = c
            op.chan = chan
            op.chan_val = 16 * c
            self.last_dma[chan] = op
        self.streams[eng].append(op)
        return op

    def barrier(self):
        deps = []
        for e in ENGS:
            for x in reversed(self.streams[e]):
                if x.chan is None and x.fn is not None:
                    deps.append(x)
                    break
        deps.extend(self.last_dma.values())
        for e in ENGS:
            op = Op(e, None)
            op.deps = list(deps)
            self.streams[e].append(op)
        self.last_w = {}
        self.readers = {}

    def _needs_sem(self, x, d):
        if d.chan is not None:
            return True
        if d.eng == x.eng and x.chan is None:
            if x.eng == "pe":
                return False
            if x.fn is None:
                return True
            return self.same_engine_sync and x.kinds.get(id(d), 2) == 2
        return True

    def emit(self):
        nc = self.nc
        for e in ENGS:
            for x in self.streams[e]:
                for d in x.deps:
                    if d.chan is None and self._needs_sem(x, d):
                        d.signal = True
        for e in ENGS:
            c = 0
            for x in self.streams[e]:
                if x.chan is None and x.signal:
                    c += 1
                    x.count = c
        chans = sorted(self.chan_count.keys())
        with contextlib.ExitStack() as st:
            esem = {e: st.enter_context(nc.semaphore("s_" + e)) for e in ENGS}
            csem = {c: st.enter_context(nc.semaphore("c_" + str(c))) for c in chans}
            block = st.enter_context(nc.Block())

            def run_stream(ename, eng):
                waited = {}
                for x in self.streams[ename]:
                    need = {}
                    for d in x.deps:
                        if not self._needs_sem(x, d):
                            continue
                        if d.chan is not None:
                            key = ("c", d.chan)
                            val = d.chan_val
                        else:
                            key = ("e", d.eng)
                            val = d.count
                        if waited.get(key, 0) >= val:
                            continue
                        if need.get(key, 0) < val:
                            need[key] = val
                    for key, val in need.items():
                        sem = csem[key[1]] if key[0] == "c" else esem[key[1]]
                        eng.wait_ge(sem, val)
                        waited[key] = val
                    if x.fn is None:
                        continue
                    ins = x.fn(eng)
                    if x.chan is not None:
                        ins.then_inc(csem[x.chan], 16)
                    elif x.signal:
                        ins.then_inc(esem[ename], 1)
                if ename == "sp":
                    for c in chans:
                        v = 16 * self.chan_count[c]
                        if waited.get(("c", c), 0) < v:
                            eng.wait_ge(csem[c], v)

            @block.sync
            def _(e):
                run_stream("sp", e)

            @block.tensor
            def _(e):
                run_stream("pe", e)

            @block.vector
            def _(e):
                run_stream("dve", e)

            @block.scalar
            def _(e):
                run_stream("act", e)

            @block.gpsimd
            def _(e):
                run_stream("pool", e)


def _rope_tables():
    t = np.arange(TL)
    row, col = (t // 64).astype(np.float64), (t % 64).astype(np.float64)

    def tab(dh):
        half = dh // 2
        q = half // 2
        f = 10000.0 ** (-np.arange(0, half, 2, dtype=np.float64) / half)
        cos = np.zeros((dh, TL)); sin = np.zeros((dh, TL))
        for ax, pos in enumerate((row, col)):
            ang = pos[None, :] * f[:, None]
            b = ax * half
            cos[b:b + q] = np.cos(ang); cos[b + q:b + half] = np.cos(ang)
            sin[b:b + q] = np.sin(ang); sin[b + q:b + half] = np.sin(ang)
        rot = np.zeros((dh, dh))
        for ax in range(2):
            b = ax * half
            for i in range(q):
                rot[b + i, b + q + i] = -1.0
                rot[b + q + i, b + i] = 1.0
        return np.stack([cos, sin]).astype(np.float32), np.ascontiguousarray(rot.T).astype(np.float32)

    return tab(128), tab(64)


def _abias_index():
    tiles = []
    for g, ms in ((0, range(0, 6)), (1, range(2, 10)), (3, range(10, 16))):
        for m in ms:
            a = np.arange(128) // 64
            w = np.arange(128) % 64
            rr = np.arange(512) // 64
            qc = np.arange(512) % 64
            j = (2 * m + a)[:, None]
            r = (8 * g + rr)[None, :]
            r0 = np.clip(r - 4, 0, 24)
            valid = (j >= r0) & (j < r0 + 8)
            c0 = np.clip(qc - 8, 0, 48)[None, :]
            valid = valid & (w[:, None] >= c0) & (w[:, None] < c0 + 16)
            dr = np.clip(j - r + 7, 0, 14)
            dc = np.clip(w[:, None] - qc[None, :], -15, 15) + 15
            tiles.append((np.broadcast_to(dr, (128, 512)), dc, valid))
    dr = np.stack([t[0] for t in tiles]); dc = np.stack([t[1] for t in tiles]); va = np.stack([t[2] for t in tiles])
    return dr, dc, va


def _abias_tile_id(g, m):
    if g == 0:
        return m
    if g == 1:
        return 6 + (m - 2)
    if g == 2:
        return 6 + (m - 6)
    return 14 + (m - 10)


def _a_chunks(g):
    return {0: range(0, 6), 1: range(2, 10), 2: range(6, 14), 3: range(10, 16)}[g]


def _pc(v):
    v = np.asarray(v, np.float32)
    return np.ascontiguousarray(v.reshape(-1, 128).T)


class Cfg:
    def __init__(self, layers=2, taps=(), stop=None, x_in="xT"):
        self.layers = layers
        self.taps = set(taps)
        self.stop = stop
        self.x_in = x_in


def build(cfg):
    nc = bass.Bass("TRN2", target_bir_lowering=False)
    in_shapes = {}

    def inp(name, shape):
        in_shapes[name] = tuple(shape)
        return nc.dram_tensor(name, list(shape), F32, kind="ExternalInput").ap()

    out_names = []

    def scratch(name, shape, dt):
        if name in cfg.taps:
            out_names.append(name)
            return nc.dram_tensor(name, list(shape), dt, kind="ExternalOutput").ap()
        return nc.dram_tensor(name, list(shape), dt, kind="Internal").ap()

    st = contextlib.ExitStack()
    with st:
        RESb = st.enter_context(nc.sbuf_tensor("RES", [128, 36864], BF16))
        AR = st.enter_context(nc.sbuf_tensor("ARENA", [128, 30720], F32))
        ONES = st.enter_context(nc.sbuf_tensor("ONES", [128, 128], BF16))
        ONES32 = st.enter_context(nc.sbuf_tensor("ONES32", [128, 128], F32))
        ROTB = st.enter_context(nc.sbuf_tensor("ROTB", [128, 128], BF16))
        ROTC = st.enter_context(nc.sbuf_tensor("ROTC", [64, 64], BF16))
        CV = st.enter_context(nc.sbuf_tensor("CV", [128, 32], F32))
        CVS = st.enter_context(nc.sbuf_tensor("CVS", [128, 32], BF16))
        VEC = st.enter_context(nc.sbuf_tensor("VEC", [128, 2 * NV], F32))
        MOD = st.enter_context(nc.sbuf_tensor("MOD", [128, 2 * 192], F32))
        COEF = st.enter_context(nc.sbuf_tensor("COEF", [128, 2 * 192], F32))
        PS = st.enter_context(nc.psum_tensor("PS", [128, 8 * 512], F32))
        P = Prog(nc)

        def bank(b):
            return PS[:, b * 512:(b + 1) * 512]

        def arf(off, n):
            return AR[:, off:off + n]

        def arb(off, n):
            return AR[:, off:off + n].bitcast(BF16)

        def vec(l, col, n=1):
            return VEC[:, l * NV + col:l * NV + col + n]

        V_GPRE1, V_GPOST1, V_GPRE2, V_GPOST2, V_BADA, V_QN, V_KN, V_MQN, V_MKVN = 0, 16, 32, 48, 64, 160, 161, 162, 166

        def coef(l, kind, c, w):
            o = l * 192 + kind * 32 + c * 2 + w
            return COEF[:, o:o + 1]

        stop = {"flag": False}

        def phase_end(l, name):
            P.barrier()
            if cfg.stop == (l, name):
                stop["flag"] = True

        xT = inp("xT", [D, T])
        cvec = inp("cvec", [128, 32])
        rotb_in = inp("rotB", [128, 128])
        rotc_in = inp("rotC", [64, 64])
        ropeB = inp("ropeB", [2, 128, TL])
        ropeC = inp("ropeC", [2, 64, TL])
        vecs_in = inp("vecs", [128, 2 * NV])

        X = [xT] + [scratch(f"X{i}", [D, T], F32) for i in (1, 2, 3)]
        OUT = nc.dram_tensor("outT", [D, TL], F32, kind="ExternalOutput").ap()
        out_names.append("outT")
        QKA = scratch("QKA", [2048, T], BF16)
        VA = scratch("VA", [T, 1024], BF16)
        QB = scratch("QB", [1024, T], BF16)
        KB = scratch("KB", [256, T], BF16)
        VB = scratch("VB", [T, 256], BF16)
        CQN = scratch("CQN", [1024, T], BF16)
        CKR = scratch("CKR", [64, T], F32)
        SG = scratch("SG", [6144, T], BF16)
        QCN = scratch("QCN", [1024, T], BF16)
        QCR = scratch("QCR", [512, T], BF16)
        KCN = scratch("KCN", [1024, T], BF16)
        KRP = scratch("KRP", [64, T], BF16)
        VC = scratch("VC", [T, 1024], BF16)
        OT = scratch("OT", [3072, T], BF16)
        YT = scratch("YT", [D, T], BF16)
        UT = scratch("UT", [D, T], F32)
        U2T = scratch("U2T", [D, T], F32)
        ACTT = scratch("ACTT", [DFF, T], BF16)

        P.add("pool", lambda e: e.memset(ONES[:], 1.0), writes=["ones"])
        P.add("pool", lambda e: e.memset(ONES32[:], 1.0), writes=["ones32"])
        P.add("pool", lambda e: e.dma_start(out=ROTB[:], in_=rotb_in), writes=["rotb"], chan="k0")
        P.add("pool", lambda e: e.dma_start(out=ROTC[:], in_=rotc_in), writes=["rotc"], chan="k1")
        P.add("sp", lambda e: e.dma_start(out=CV[:], in_=cvec), writes=["cv"], chan="k2")
        P.add("sp", lambda e: e.dma_start(out=VEC[:], in_=vecs_in), writes=["vec"], chan="k3")
        P.add("act", lambda e: e.activation(out=CVS[:], in_=CV[:], func=AF.Silu), reads=["cv"], writes=["cvs"])
        P.barrier()

        wcount = {"n": 0}

        def load_slab(src3, kc, ncols, nslots, slot_words, base=0, split=4):
            s = wcount["n"] % nslots
            wcount["n"] += 1
            view = arb(base + s * slot_words, slot_words)[:, 0:kc * ncols].rearrange("p (c e) -> p c e", c=kc)
            per = (kc + split - 1) // split
            for j in range(split):
                k0, k1 = j * per, min(kc, (j + 1) * per)
                if k0 >= k1:
                    continue
                P.add("pool", lambda e, v=view, k0=k0, k1=k1: e.dma_start(out=v[:, k0:k1, :], in_=src3[:, k0:k1, :]),
                      writes=[f"wb{s}.{j}"], chan=f"w{s}.{j}")
            return view, f"wb{s}", per

        def slab_stream(specs, nslots, slot_words, base=0, split=4, ahead=None):
            n = len(specs)
            out = []
            li = 0
            ahead = nslots if ahead is None else ahead
            for i in range(n):
                while li < min(n, i + ahead):
                    src3, kc_, ncols_ = specs[li]
                    out.append(load_slab(src3, kc_, ncols_, nslots, slot_words, base=base, split=split))
                    li += 1
                yield out[i]

        def wsrc(wap, K, c0, nc_):
            return wap[0:K, c0:c0 + nc_].rearrange("(c p) e -> p c e", p=128)

        ada_state = {"n": 0}
        w_ada_aps = {}

        def ada_w(l):
            if l not in w_ada_aps:
                w_ada_aps[l] = inp(f"w_ada{l}", [D, 12 * 1024])
            return w_ada_aps[l]

        def ada_load(l, s):
            slot = ada_state["n"] % 4
            ada_state["n"] += 1
            view = RESb[:, slot * 8192:(slot + 1) * 8192].rearrange("p (c e) -> p c e", c=16)
            src3 = wsrc(ada_w(l), D, s * 512, 512)
            for j in range(4):
                P.add("pool", lambda e, v=view, j=j, src3=src3: e.dma_start(out=v[:, 4 * j:4 * j + 4, :], in_=src3[:, 4 * j:4 * j + 4, :]),
                      writes=[f"adw{slot}.{j}"], chan=f"aw{slot}.{j}")
            return view, slot

        def ada_mm(l, s, view, slot):
            psm = bank(7)
            cvs3 = CVS[:].rearrange("p (c w) -> p c w", w=2)
            for ec in range(4):
                col = (s * 4 + ec) * 2
                for k in range(16):
                    P.add("pe", lambda e, view=view, ec=ec, k=k, col=col: e.matmul(
                        psm[:, col:col + 2], lhsT=view[:, k, ec * 128:(ec + 1) * 128], rhs=cvs3[:, k, :],
                        start=(k == 0), stop=(k == 15)),
                        reads=[f"adw{slot}.{k // 4}", "cvs"], writes=["ps7"])

        def ada_finish(l, part):
            psm = bank(7)
            mod = MOD[:, l * 192:(l + 1) * 192]
            mod3 = mod.rearrange("p (j w) -> p j w", w=2)
            j0, j1 = (0, 32) if part == "A" else (32, 96)
            bada = vec(l, V_BADA + j0, j1 - j0).unsqueeze(2).broadcast_to([128, j1 - j0, 2])
            P.add("dve", lambda e: e.tensor_tensor(out=mod3[:, j0:j1, :], in0=psm[:, 2 * j0:2 * j1].rearrange("p (j w) -> p j w", w=2), in1=bada, op=ALU.add),
                  reads=["ps7", "vec"], writes=["mod"])
            cf = COEF[:, l * 192:(l + 1) * 192].rearrange("p (k c w) -> p k c w", k=6, w=2)

            def g2(col):
                return vec(l, col, 16).unsqueeze(2).broadcast_to([128, 16, 2])

            def m3(j):
                return mod3[:, j * 16:(j + 1) * 16, :]
            gains = ((0, 1, V_GPRE1),) if part == "A" else ((3, 4, V_GPRE2),)
            shifts = ((1, 0),) if part == "A" else ((4, 3),)
            coefs = () if part == "A" else ((2, 2, V_GPOST1), (5, 5, V_GPOST2))
            for kind, jsc, gcol in gains:
                P.add("dve", lambda e, kind=kind, jsc=jsc: e.tensor_scalar_add(out=cf[:, kind], in0=m3(jsc), scalar1=1.0),
                      reads=["mod"], writes=["coef"])
                P.add("dve", lambda e, kind=kind, gcol=gcol: e.tensor_tensor(out=cf[:, kind], in0=cf[:, kind], in1=g2(gcol), op=ALU.mult),
                      reads=["coef", "vec"], writes=["coef"])
            for kind, j in shifts:
                P.add("dve", lambda e, kind=kind, j=j: e.tensor_copy(out=cf[:, kind], in_=m3(j)), reads=["mod"], writes=["coef"])
            for kind, j, gcol in coefs:
                P.add("dve", lambda e, kind=kind, j=j, gcol=gcol: e.tensor_tensor(out=cf[:, kind], in0=m3(j), in1=g2(gcol), op=ALU.mult),
                      reads=["mod", "vec"], writes=["coef"])

        def ada_units(items):
            slabs = [it for it in items if it[0] == "s"]
            loaded = []
            li = 0
            for _ in range(3):
                if li < len(slabs):
                    loaded.append(ada_load(slabs[li][1], slabs[li][2]))
                    li += 1
            si = 0
            for it in items:
                if it[0] == "s":
                    if li < len(slabs):
                        loaded.append(ada_load(slabs[li][1], slabs[li][2]))
                        li += 1
                    view, slot = loaded[si]
                    si += 1
                    ada_mm(it[1], it[2], view, slot)
                    yield
                else:
                    ada_finish(it[1], it[2])

        def phase_ada_first():
            for _ in ada_units([("s", 0, s) for s in range(8)] + [("f", 0, "A")]):
                pass
            P.barrier()

        def ada_background_items():
            items = [("s", 0, s) for s in range(8, 24)] + [("f", 0, "B")]
            for l in range(1, cfg.layers):
                items += [("s", l, s) for s in range(8)] + [("f", l, "A")] + [("s", l, s) for s in range(8, 24)] + [("f", l, "B")]
            return items

        def phase_rn(l, x_src, x_dst, u_src, ckind, nkind, ntok, out_final=False):
            GW = 256
            res3 = RESb[:].rearrange("p (c t) -> p c t", c=NCH)
            ng = ntok // GW
            sqn = {"n": 0}

            def mk(g):
                t0 = g * GW
                w = 1 if t0 >= TL else 0
                pb = g % 3
                uw = arf(pb * 8192, 4096).rearrange("p (c t) -> p c t", c=NCH)
                xw = arf(pb * 8192 + 4096, 4096).rearrange("p (c t) -> p c t", c=NCH)
                r1 = arf(28672 + pb * 512, 256)
                r2 = arf(28672 + pb * 512 + 256, 256)
                tk = f"rn{pb}"

                def sq_buf():
                    i = sqn["n"] % 2
                    sqn["n"] += 1
                    return arb(24576 + i * 2048, 2048).rearrange("p (c t) -> p c t", c=NCH), f"rnsq{i}", 4 + i

                def ld():
                    xsrc3 = x_src[:, t0:t0 + GW].rearrange("(c p) t -> p c t", p=128)
                    P.add("sp", lambda e: e.dma_start(out=xw, in_=xsrc3), writes=[tk + "x"], chan=tk + "x")
                    if u_src is None:
                        return
                    usrc3 = u_src[:, t0:t0 + GW].rearrange("(c p) t -> p c t", p=128)
                    P.add("sp", lambda e: e.dma_start(out=uw, in_=usrc3), writes=[tk + "u"], chan=tk + "u")

                def s1():
                    if u_src is None:
                        return
                    sqb, sqt, sb = sq_buf()
                    P.add("act", lambda e: e.activation(out=sqb, in_=uw, func=AF.Square), reads=[tk + "u"], writes=[sqt])
                    for c in range(NCH):
                        P.add("pe", lambda e, c=c: e.matmul(bank(sb)[:, 0:GW], lhsT=ONES[:], rhs=sqb[:, c, :], start=(c == 0), stop=(c == NCH - 1)),
                              reads=[sqt, "ones"], writes=[f"ps{sb}"])
                    P.add("act", lambda e: e.activation(out=r1, in_=bank(sb)[:, 0:GW], func=AF.Ln, scale=1.0 / D, bias=EPS),
                          reads=[f"ps{sb}"], writes=[tk + "r1"])
                    P.add("act", lambda e: e.activation(out=r1, in_=r1, func=AF.Exp, scale=-0.5), reads=[tk + "r1"], writes=[tk + "r1"])
                    for c in range(NCH):
                        P.add("dve", lambda e, c=c: e.scalar_tensor_tensor(
                            out=uw[:, c, :], in0=uw[:, c, :], scalar=coef(l, ckind, c, w), in1=r1, op0=ALU.mult, op1=ALU.mult),
                            reads=[tk + "u", tk + "r1", "coef"], writes=[tk + f"v{c}"])
                    CS = 11
                    P.add("dve", lambda e: e.tensor_tensor(out=xw[:, 0:CS, :], in0=xw[:, 0:CS, :], in1=uw[:, 0:CS, :], op=ALU.add),
                          reads=[tk + "x", tk + "u"] + [tk + f"v{c}" for c in range(CS)], writes=[tk + "xa"])
                    P.add("pool", lambda e: e.tensor_tensor(out=xw[:, CS:NCH, :], in0=xw[:, CS:NCH, :], in1=uw[:, CS:NCH, :], op=ALU.add),
                          reads=[tk + "x", tk + "u"] + [tk + f"v{c}" for c in range(CS, NCH)], writes=[tk + "xb"])
                    if out_final:
                        dst3 = OUT[:, t0:t0 + GW].rearrange("(c p) t -> p c t", p=128)
                    else:
                        dst3 = x_dst[:, t0:t0 + GW].rearrange("(c p) t -> p c t", p=128)
                    P.add("pool", lambda e: e.dma_start(out=dst3, in_=xw), reads=[tk + "x", tk + "xa", tk + "xb"], writes=[f"xd{g}"], chan=tk + "o")

                def s2():
                    if nkind is None:
                        return
                    sqb, sqt, sb = sq_buf()
                    P.add("act", lambda e: e.activation(out=sqb, in_=xw, func=AF.Square), reads=[tk + "x", tk + "xa", tk + "xb"], writes=[sqt])
                    for c in range(NCH):
                        P.add("pe", lambda e, c=c: e.matmul(bank(sb)[:, 0:GW], lhsT=ONES[:], rhs=sqb[:, c, :], start=(c == 0), stop=(c == NCH - 1)),
                              reads=[sqt, "ones"], writes=[f"ps{sb}"])
                    P.add("act", lambda e: e.activation(out=r2, in_=bank(sb)[:, 0:GW], func=AF.Ln, scale=1.0 / D, bias=EPS),
                          reads=[f"ps{sb}"], writes=[tk + "r2"])
                    P.add("act", lambda e: e.activation(out=r2, in_=r2, func=AF.Exp, scale=-0.5), reads=[tk + "r2"], writes=[tk + "r2"])
                    r2b = r2.unsqueeze(1).broadcast_to([128, NCH, GW])
                    P.add("dve", lambda e: e.tensor_tensor(out=uw, in0=xw, in1=r2b, op=ALU.mult),
                          reads=[tk + "x", tk + "xa", tk + "xb", tk + "r2", tk + "u"], writes=[tk + "u"] + [tk + f"v{c}" for c in range(NCH)])

                def s3():
                    if nkind is None:
                        return
                    gk, sk, nl = nkind
                    for c in range(NCH):
                        P.add("act", lambda e, c=c: e.activation(
                            out=res3[:, c, t0:t0 + GW], in_=uw[:, c, :], func=AF.Identity, scale=coef(nl, gk, c, w), bias=coef(nl, sk, c, w)),
                            reads=[tk + "u", "coef"], writes=[f"res.{t0 // 512}"])
                return ld, s1, s2, s3
            st = [mk(g) for g in range(ng)]
            for g in range(min(3, ng)):
                st[g][0]()
            for g in range(min(2, ng)):
                st[g][1]()
            for g in range(ng):
                st[g][2]()
                if g + 2 < ng:
                    st[g + 2][1]()
                st[g][3]()
                if g + 3 < ng:
                    st[g + 3][0]()
            P.barrier()

        def tgroups(ntok):
            gs = []
            t0 = 0
            while t0 < ntok:
                n = min(512, ntok - t0)
                gs.append((t0, n))
                t0 += n
            return gs

        gemm_bank = {"n": 0}
        deferred = []

        def run_deferred():
            cur = list(deferred)
            del deferred[:]
            for f in cur:
                f()

        def gemm_fm(slabs, kc, rhs_fn, rhs_tok_fn, tgs, evac, post_ec=None, nslots=3, slot_words=4096, wbase=0, nbanks=4, order="ec"):
            strm = slab_stream([(s3, kc, nco) for (s3, nco, tg_) in slabs], nslots, slot_words, base=wbase)
            for si, (src3, ncols, tag) in enumerate(slabs):
                view, tok, per = next(strm)
                necs = (ncols + 127) // 128
                for ec in range(necs):
                    m = min(128, ncols - ec * 128)
                    for gi, (t0, n) in enumerate(tgs):
                        b = gemm_bank["n"] % nbanks
                        gemm_bank["n"] += 1
                        for k in range(kc):
                            P.add("pe", lambda e, b=b, m=m, n=n, view=view, k=k, ec=ec, gi=gi: e.matmul(
                                bank(b)[0:m, 0:n], lhsT=view[:, k, ec * 128:ec * 128 + m], rhs=rhs_fn(k, gi),
                                start=(k == 0), stop=(k == kc - 1)),
                                reads=[f"{tok}.{k // per}", rhs_tok_fn(k, gi)], writes=[f"ps{b}"])
                        run_deferred()
                        evac(tag, si, ec, gi, t0, n, m, b)
                    if post_ec is not None:
                        post_ec(tag, si, ec, m)
            run_deferred()
            run_deferred()

        def gemm_tm(slabs, kc, lhs_fn, lhs_tok_fn, ntok, evac, nslots=3, slot_words=4096, wbase=0, nbanks=4):
            strm = slab_stream([(s3, kc, nco) for (s3, nco, tg_) in slabs], nslots, slot_words, base=wbase)
            for si, (src3, ncols, tag) in enumerate(slabs):
                view, tok, per = next(strm)
                for tt in range(ntok // 128):
                    b = gemm_bank["n"] % nbanks
                    gemm_bank["n"] += 1
                    for k in range(kc):
                        P.add("pe", lambda e, b=b, view=view, k=k, tt=tt, ncols=ncols: e.matmul(
                            bank(b)[:, 0:ncols], lhsT=lhs_fn(k, tt), rhs=view[:, k, 0:ncols], start=(k == 0), stop=(k == kc - 1)),
                            reads=[f"{tok}.{k // per}", lhs_tok_fn(tt)], writes=[f"ps{b}"])
                    evac(tag, si, tt, ncols, b)

        res16 = RESb[:].rearrange("p (c t) -> p c t", c=NCH)

        stg_n = {"n": 0}

        def copy_alt(i, out, in_, reads, writes):
            if i % 2 == 0:
                P.add("act", lambda e: e.activation(out=out, in_=in_, func=AF.Copy), reads=reads, writes=writes)
            else:
                P.add("dve", lambda e: e.tensor_copy(out=out, in_=in_), reads=reads, writes=writes)

        def phase_inproj(l):
            w_in = inp(f"w_in{l}", [D, IN_W])
            STG = [arb(12288 + i * 1152, 1152) for i in range(3)]
            cosB = arf(15744, 2048)
            sinB = arf(17792, 2048)
            sqt = [arb(19840 + i * 256, 256) for i in range(2)]
            rt = [arf(20352 + i * 512, 512) for i in range(2)]
            qn32 = [arf(21376 + i * 512, 512) for i in range(2)]
            qnb = [arb(22400 + i * 256, 256) for i in range(2)]
            t1 = [arf(22912 + i * 512, 512) for i in range(2)]
            cqtmp = arf(23936, 2048).rearrange("p (c t) -> p c t", c=4)
            STG32 = [arf(25984 + i * 2304, 2304) for i in range(2)]
            P.add("sp", lambda e: e.dma_start(out=cosB, in_=ropeB[0]), writes=["cosB"], chan="k0")
            P.add("sp", lambda e: e.dma_start(out=sinB, in_=ropeB[1]), writes=["sinB"], chan="k1")
            tgs = tgroups(T)
            rhs_fn = lambda k, gi: res16[:, k, tgs[gi][0]:tgs[gi][0] + tgs[gi][1]]
            rhs_tok = lambda k, gi: f"res.{gi}"
            cnt = {"i": 0, "q": 0, "c": 0}
            cur = {}

            def stage_slot():
                s = stg_n["n"] % 3
                stg_n["n"] += 1
                return s

            def mk_plain(dst, row0_fn, func):
                def evac(tag, si, ec, gi, t0, n, m, b):
                    if gi == 0:
                        cur["s"] = stage_slot()
                    s = cur["s"]
                    i = cnt["i"]
                    cnt["i"] += 1
                    if func is None:
                        copy_alt(i, STG[s][0:m, t0:t0 + n], bank(b)[0:m, 0:n], [f"ps{b}"], [f"stg{s}.{gi}"])
                    else:
                        P.add("act", lambda e: e.activation(out=STG[s][0:m, t0:t0 + n], in_=bank(b)[0:m, 0:n], func=func),
                              reads=[f"ps{b}"], writes=[f"stg{s}.{gi}"])

                def post(tag, si, ec, m):
                    s = cur["s"]
                    r0 = row0_fn(si, ec)
                    P.add("sp", lambda e: e.dma_start(out=dst[r0:r0 + m, :], in_=STG[s][0:m, :]),
                          reads=[f"stg{s}.{gi}" for gi in range(len(tgs))], writes=[f"d.{id(dst)}.{r0}"], chan=f"st{s}")
                return evac, post

            def slabs_of(c0, width):
                out = []
                o = 0
                while o < width:
                    n = min(512, width - o)
                    out.append((wsrc(w_in, D, c0 + o, n), n, o))
                    o += n
                return out

            ev, po = mk_plain(QKA, lambda si, ec: si * 512 + ec * 128, None)
            gemm_fm(slabs_of(O_AQ, 2048), NCH, rhs_fn, rhs_tok, tgs, ev, po)

            def mk_tm(dst, c0_fn):
                def evac(tag, si, tt, ncols, b):
                    s = stage_slot()
                    i = cnt["i"]
                    cnt["i"] += 1
                    copy_alt(i, STG[s][:, 0:ncols], bank(b)[:, 0:ncols], [f"ps{b}"], [f"stg{s}.0"])
                    c0 = c0_fn(si)
                    P.add("sp", lambda e: e.dma_start(out=dst[tt * 128:(tt + 1) * 128, c0:c0 + ncols], in_=STG[s][:, 0:ncols]),
                          reads=[f"stg{s}.0"], writes=[f"d.{id(dst)}.{tt}.{c0}"], chan=f"st{s}")
                return evac
            lhs_fn = lambda k, tt: res16[:, k, tt * 128:(tt + 1) * 128]
            lhs_tok = lambda tt: f"res.{tt // 4}"
            gemm_tm(slabs_of(O_AV, 1024), NCH, lhs_fn, lhs_tok, T, mk_tm(VA, lambda si: si * 512))

            def mk_qk(dst, gcol):
                def evac(tag, si, ec, gi, t0, n, m, b):
                    if gi == 0:
                        cur["s"] = stage_slot()
                    s = cur["s"]
                    q = cnt["q"] % 2
                    cnt["q"] += 1
                    P.add("act", lambda e: e.activation(out=sqt[q][:, 0:n], in_=bank(b)[:, 0:n], func=AF.Square),
                          reads=[f"ps{b}"], writes=[f"sqt{q}"])
                    sb = 4 + q

                    def st1():
                        P.add("pe", lambda e: e.matmul(bank(sb)[:, 0:n], lhsT=ONES[:], rhs=sqt[q][:, 0:n], start=True, stop=True),
                              reads=[f"sqt{q}", "ones"], writes=[f"ps{sb}"])
                        P.add("act", lambda e: e.activation(out=rt[q][:, 0:n], in_=bank(sb)[:, 0:n], func=AF.Ln, scale=1.0 / 128, bias=EPS),
                              reads=[f"ps{sb}"], writes=[f"rt{q}"])
                        P.add("act", lambda e: e.activation(out=rt[q][:, 0:n], in_=rt[q][:, 0:n], func=AF.Exp, scale=-0.5), reads=[f"rt{q}"], writes=[f"rt{q}"])
                        P.add("dve", lambda e: e.scalar_tensor_tensor(out=qn32[q][:, 0:n], in0=bank(b)[:, 0:n], scalar=vec(l, gcol), in1=rt[q][:, 0:n],
                                                                      op0=ALU.mult, op1=ALU.mult),
                              reads=[f"ps{b}", f"rt{q}", "vec"], writes=[f"qn32{q}"])
                        if t0 >= TL:
                            P.add("act", lambda e: e.activation(out=STG[s][:, t0:t0 + n], in_=qn32[q][:, 0:n], func=AF.Copy),
                                  reads=[f"qn32{q}"], writes=[f"stg{s}.{gi}"])
                            return
                        P.add("act", lambda e: e.activation(out=qnb[q][:, 0:n], in_=qn32[q][:, 0:n], func=AF.Copy),
                              reads=[f"qn32{q}"], writes=[f"qnb{q}"])
                        rb = 6 + q

                        def st2():
                            P.add("pe", lambda e: e.matmul(bank(rb)[:, 0:n], lhsT=ROTB[:], rhs=qnb[q][:, 0:n], start=True, stop=True),
                                  reads=[f"qnb{q}", "rotb"], writes=[f"ps{rb}"])
                            P.add("dve", lambda e: e.tensor_tensor(out=t1[q][:, 0:n], in0=bank(rb)[:, 0:n], in1=sinB[:, t0:t0 + n], op=ALU.mult),
                                  reads=[f"ps{rb}", "sinB"], writes=[f"t1{q}"])
                            P.add("pool", lambda e: e.tensor_tensor(out=qn32[q][:, 0:n], in0=qn32[q][:, 0:n], in1=cosB[:, t0:t0 + n], op=ALU.mult),
                                  reads=[f"qn32{q}", "cosB"], writes=[f"qn32{q}"])
                            P.add("dve", lambda e: e.tensor_tensor(out=STG[s][:, t0:t0 + n], in0=qn32[q][:, 0:n], in1=t1[q][:, 0:n], op=ALU.add),
                                  reads=[f"qn32{q}", f"t1{q}"], writes=[f"stg{s}.{gi}"])
                        deferred.append(st2)
                    deferred.append(st1)

                def post(tag, si, ec, m):
                    s = cur["s"]
                    r0 = si * 512 + ec * 128

                    def do():
                        P.add("sp", lambda e: e.dma_start(out=dst[r0:r0 + m, :], in_=STG[s][0:m, :]),
                              reads=[f"stg{s}.{gi}" for gi in range(len(tgs))], writes=[f"d.{id(dst)}.{r0}"], chan=f"st{s}")
                    deferred.append(lambda: deferred.append(lambda: deferred.append(do)))
                return evac, post
            ev, po = mk_qk(QB, V_QN)
            gemm_fm(slabs_of(O_BQ, 1024), NCH, rhs_fn, rhs_tok, tgs, ev, po)
            run_deferred(); run_deferred()
            ev, po = mk_qk(KB, V_KN)
            gemm_fm(slabs_of(O_BK, 256), NCH, rhs_fn, rhs_tok, tgs, ev, po)
            run_deferred(); run_deferred()
            gemm_tm(slabs_of(O_BV, 256), NCH, lhs_fn, lhs_tok, T, mk_tm(VB, lambda si: 0))

            def mla_c(c0, row_base, gcol):
                src3 = wsrc(w_in, D, c0, 512)
                view, tok, per = load_slab(src3, NCH, 512, 3, 4096)
                ss = [stage_slot() for _ in range(4)]
                pendq = []
                for gi, (t0, n) in enumerate(tgs):
                    sb = 4 + (gi % 2)
                    for ec in range(4):
                        b = gemm_bank["n"] % 4
                        gemm_bank["n"] += 1
                        for k in range(NCH):
                            P.add("pe", lambda e, b=b, n=n, k=k, ec=ec, gi=gi: e.matmul(
                                bank(b)[:, 0:n], lhsT=view[:, k, ec * 128:(ec + 1) * 128], rhs=rhs_fn(k, gi), start=(k == 0), stop=(k == NCH - 1)),
                                reads=[f"{tok}.{k // per}", rhs_tok(k, gi)], writes=[f"ps{b}"])
                        while pendq:
                            pendq.pop(0)()
                        q = cnt["q"] % 2
                        cnt["q"] += 1
                        P.add("act", lambda e, b=b, n=n, ec=ec: e.activation(out=cqtmp[:, ec, 0:n], in_=bank(b)[:, 0:n], func=AF.Copy),
                              reads=[f"ps{b}"], writes=[f"cqt{ec}"])
                        P.add("act", lambda e, n=n, ec=ec, q=q: e.activation(out=sqt[q][:, 0:n], in_=cqtmp[:, ec, 0:n], func=AF.Square),
                              reads=[f"cqt{ec}"], writes=[f"sqt{q}"])

                        def later(sb=sb, n=n, q=q, ec=ec, gi=gi, t0=t0):
                            P.add("pe", lambda e: e.matmul(bank(sb)[:, 0:n], lhsT=ONES[:], rhs=sqt[q][:, 0:n], start=(ec == 0), stop=(ec == 3)),
                                  reads=[f"sqt{q}", "ones"], writes=[f"ps{sb}"])
                            if ec != 3:
                                return
                            P.add("act", lambda e: e.activation(out=rt[0][:, 0:n], in_=bank(sb)[:, 0:n], func=AF.Ln, scale=1.0 / 512, bias=EPS),
                                  reads=[f"ps{sb}"], writes=["rt0"])
                            P.add("act", lambda e: e.activation(out=rt[0][:, 0:n], in_=rt[0][:, 0:n], func=AF.Exp, scale=-0.5), reads=["rt0"], writes=["rt0"])
                            for ec2 in range(4):
                                P.add("dve", lambda e, ec2=ec2: e.scalar_tensor_tensor(
                                    out=STG[ss[ec2]][:, t0:t0 + n], in0=cqtmp[:, ec2, 0:n], scalar=vec(l, gcol + ec2), in1=rt[0][:, 0:n], op0=ALU.mult, op1=ALU.mult),
                                    reads=[f"cqt{ec2}", "rt0", "vec"], writes=[f"stg{ss[ec2]}.{gi}"])
                        pendq.append(later)
                while pendq:
                    pendq.pop(0)()
                for ec in range(4):
                    r0 = row_base + ec * 128
                    P.add("sp", lambda e, r0=r0, ec=ec: e.dma_start(out=CQN[r0:r0 + 128, :], in_=STG[ss[ec]][:, :]),
                          reads=[f"stg{ss[ec]}.{gi}" for gi in range(len(tgs))], writes=[f"d.cqn.{r0}"], chan=f"st{ss[ec]}")
            STG.append(STG32[0].bitcast(BF16)[:, 0:T])
            _orig_slot = stage_slot

            def stage_slot4():
                s = stg_n["n"] % 4
                stg_n["n"] += 1
                return s
            stage_slot = stage_slot4
            mla_c(O_CQ, 0, V_MQN)
            mla_c(O_CKV, 512, V_MKVN)
            P.barrier()
            stage_slot = _orig_slot
            stg_n["n"] = 0

            def ev_ckr(tag, si, ec, gi, t0, n, m, b):
                P.add("act", lambda e: e.activation(out=STG32[1][0:m, t0:t0 + n], in_=bank(b)[0:m, 0:n], func=AF.Copy),
                      reads=[f"ps{b}"], writes=[f"s32.{gi}"])

            def po_ckr(tag, si, ec, m):
                P.add("sp", lambda e: e.dma_start(out=CKR[:, :], in_=STG32[1][0:64, :]),
                      reads=[f"s32.{gi}" for gi in range(len(tgs))], writes=["d.ckr"], chan="st32")
            gemm_fm(slabs_of(O_CKR, 64), NCH, rhs_fn, rhs_tok, tgs, ev_ckr, po_ckr)

            ev, po = mk_plain(SG, lambda si, ec: si * 512 + ec * 128, AF.Sigmoid)
            gemm_fm(slabs_of(O_GA, 6144), NCH, rhs_fn, rhs_tok, tgs, ev, po)
            P.barrier()

        def phase_mla(l):
            w_uq = inp(f"w_uq{l}", [512, 1536])
            w_ukv = inp(f"w_ukv{l}", [512, 2048])
            STG = [arb(12288 + i * 1152, 1152) for i in range(3)]
            cosC = arf(15744, 2048)
            sinC = arf(17792, 2048)
            qr32 = [arf(19840 + i * 512, 512) for i in range(2)]
            qrb = [arb(20864 + i * 256, 256) for i in range(2)]
            t1 = [arf(21376 + i * 512, 512) for i in range(2)]
            ckr = arf(22400, 2304)
            krp = arb(24704, 1152)
            P.add("sp", lambda e: e.dma_start(out=cosC[0:64, :], in_=ropeC[0]), writes=["cosC"], chan="k0")
            P.add("sp", lambda e: e.dma_start(out=sinC[0:64, :], in_=ropeC[1]), writes=["sinC"], chan="k1")
            res4 = RESb[:, 0:4 * T].rearrange("p (c t) -> p c t", c=4)
            tgs = tgroups(T)
            rhs_fn = lambda k, gi: res4[:, k, tgs[gi][0]:tgs[gi][0] + tgs[gi][1]]
            rhs_tok = lambda k, gi: "res"
            cnt = {"i": 0, "q": 0}
            cur = {}

            def stage_slot():
                s = stg_n["n"] % 3
                stg_n["n"] += 1
                return s

            def rope64(src_ap, src_reads, n, t0, out_ap, out_writes, q):
                if t0 >= TL:
                    P.add("act", lambda e: e.activation(out=out_ap, in_=src_ap, func=AF.Copy), reads=src_reads, writes=out_writes)
                    return
                P.add("act", lambda e: e.activation(out=qrb[q][0:64, 0:n], in_=src_ap, func=AF.Copy), reads=src_reads, writes=[f"qrb{q}"])
                rb = 6 + q
                P.add("pe", lambda e: e.matmul(bank(rb)[0:64, 0:n], lhsT=ROTC[:], rhs=qrb[q][0:64, 0:n], start=True, stop=True),
                      reads=[f"qrb{q}", "rotc"], writes=[f"ps{rb}"])
                P.add("dve", lambda e: e.tensor_tensor(out=t1[q][0:64, 0:n], in0=bank(rb)[0:64, 0:n], in1=sinC[0:64, t0:t0 + n], op=ALU.mult),
                      reads=[f"ps{rb}", "sinC"], writes=[f"t1{q}"])
                P.add("pool", lambda e: e.tensor_tensor(out=src_ap, in0=src_ap, in1=cosC[0:64, t0:t0 + n], op=ALU.mult),
                      reads=src_reads + ["cosC"], writes=src_reads)
                P.add("dve", lambda e: e.tensor_tensor(out=out_ap, in0=src_ap, in1=t1[q][0:64, 0:n], op=ALU.add),
                      reads=src_reads + [f"t1{q}"], writes=out_writes)

            P.add("sp", lambda e: e.dma_start(out=ckr[0:64, :], in_=CKR[:, :]), writes=["ckr"], chan="k2")
            for gi, (t0, n) in enumerate(tgs):
                q = gi % 2
                P.add("act", lambda e, q=q, n=n, t0=t0: e.activation(out=qr32[q][0:64, 0:n], in_=ckr[0:64, t0:t0 + n], func=AF.Copy),
                      reads=["ckr"], writes=[f"qr32{q}"])
                rope64(qr32[q][0:64, 0:n], [f"qr32{q}"], n, t0, krp[0:64, t0:t0 + n], [f"krp.{gi}"], q)
            P.add("sp", lambda e: e.dma_start(out=KRP[:, :], in_=krp[0:64, :]), reads=[f"krp.{gi}" for gi in range(len(tgs))], writes=["d.krp"], chan="k3")

            for j in range(4):
                P.add("sp", lambda e, j=j: e.dma_start(out=res4[:, j, :], in_=CQN[j * 128:(j + 1) * 128, :]), writes=["res"], chan=f"k{4 + j}")

            def ev_q(tag, si, ec, gi, t0, n, m, b):
                h, part = tag
                if gi == 0:
                    cur["s"] = stage_slot()
                s = cur["s"]
                if part == 0:
                    i = cnt["i"]
                    cnt["i"] += 1
                    copy_alt(i, STG[s][:, t0:t0 + n], bank(b)[:, 0:n], [f"ps{b}"], [f"stg{s}.{gi}"])
                else:
                    q = cnt["q"] % 2
                    cnt["q"] += 1
                    P.add("act", lambda e: e.activation(out=qr32[q][0:64, 0:n], in_=bank(b)[0:64, 0:n], func=AF.Copy),
                          reads=[f"ps{b}"], writes=[f"qr32{q}"])

                    def later():
                        rope64(qr32[q][0:64, 0:n], [f"qr32{q}"], n, t0, STG[s][0:64, t0:t0 + n], [f"stg{s}.{gi}"], q)
                    deferred.append(later)

            def po_q(tag, si, ec, m):
                h, part = tag
                s = cur["s"]
                dst = QCN[h * 128:(h + 1) * 128, :] if part == 0 else QCR[h * 64:(h + 1) * 64, :]

                def do():
                    P.add("sp", lambda e: e.dma_start(out=dst, in_=STG[s][0:m, :]),
                          reads=[f"stg{s}.{gi}" for gi in range(len(tgs))], writes=[f"d.q{h}.{part}"], chan=f"st{s}")
                deferred.append(lambda: deferred.append(do))

            def small_gemm(wap, ncols_total, col_specs, evac, post):
                view, tok, per = load_slab(wsrc(wap, 512, 0, ncols_total), 4, ncols_total, 3, 4096)
                for (c0, m, tag) in col_specs:
                    for gi, (t0, n) in enumerate(tgs):
                        b = gemm_bank["n"] % 4
                        gemm_bank["n"] += 1
                        for k in range(4):
                            P.add("pe", lambda e, b=b, m=m, n=n, k=k, c0=c0, gi=gi: e.matmul(
                                bank(b)[0:m, 0:n], lhsT=view[:, k, c0:c0 + m], rhs=rhs_fn(k, gi), start=(k == 0), stop=(k == 3)),
                                reads=[f"{tok}.{k // per}", "res"], writes=[f"ps{b}"])
                        run_deferred()
                        evac(tag, 0, 0, gi, t0, n, m, b)
                    post(tag, 0, 0, m)
                run_deferred(); run_deferred(); run_deferred()
                return view, tok, per
            specs = []
            for h in range(8):
                specs.append((h * 192, 128, (h, 0)))
                specs.append((h * 192 + 128, 64, (h, 1)))
            small_gemm(w_uq, 1536, specs, ev_q, po_q)
            P.barrier()

            for j in range(4):
                P.add("sp", lambda e, j=j: e.dma_start(out=res4[:, j, :], in_=CQN[512 + j * 128:512 + (j + 1) * 128, :]), writes=["res"], chan=f"k{4 + j}")

            def ev_k(tag, si, ec, gi, t0, n, m, b):
                if gi == 0:
                    cur["s"] = stage_slot()
                s = cur["s"]
                i = cnt["i"]
                cnt["i"] += 1
                copy_alt(i, STG[s][:, t0:t0 + n], bank(b)[:, 0:n], [f"ps{b}"], [f"stg{s}.{gi}"])

            def po_k(tag, si, ec, m):
                h = tag
                s = cur["s"]
                P.add("sp", lambda e: e.dma_start(out=KCN[h * 128:(h + 1) * 128, :], in_=STG[s][:, :]),
                      reads=[f"stg{s}.{gi}" for gi in range(len(tgs))], writes=[f"d.k{h}"], chan=f"st{s}")
            view, tok, per = small_gemm(w_ukv, 2048, [(h * 256, 128, h) for h in range(8)], ev_k, po_k)
            for tt in range(T // 128):
                s = stage_slot()
                for h in range(8):
                    b = gemm_bank["n"] % 4
                    gemm_bank["n"] += 1
                    for k in range(4):
                        P.add("pe", lambda e, b=b, k=k, h=h, tt=tt: e.matmul(
                            bank(b)[:, 0:128], lhsT=res4[:, k, tt * 128:(tt + 1) * 128], rhs=view[:, k, h * 256 + 128:h * 256 + 256],
                            start=(k == 0), stop=(k == 3)),
                            reads=[f"{tok}.{k // per}", "res"], writes=[f"ps{b}"])
                    i = cnt["i"]
                    cnt["i"] += 1
                    copy_alt(i, STG[s][:, h * 128:(h + 1) * 128], bank(b)[:, 0:128], [f"ps{b}"], [f"stg{s}.{h}"])
                P.add("sp", lambda e, s=s, tt=tt: e.dma_start(out=VC[tt * 128:(tt + 1) * 128, :], in_=STG[s][:, 0:1024]),
                      reads=[f"stg{s}.{h}" for h in range(8)], writes=[f"d.vc{tt}"], chan=f"st{s}")
            P.barrier()

        def phase_attn(l, ctx_q, bg=None):
            abias = inp(f"abias{l}", [8 * 20 * 128, 512])
            Vall = arb(0, 9216)
            Kb = [arb(9216 + i * 1152, 1152) for i in range(2)]
            Qb = [arb(11520 + i * 1152, 1152) for i in range(2)]
            KR = arb(13824, 1152)
            QR = [arb(14976 + i * 1152, 1152) for i in range(2)]
            PT = [arb(17280 + i * 256, 256) for i in range(4)]
            BI = [arf(18304 + i * 512, 512) for i in range(3)]
            SS = [arf(19840 + i * 512, 512) for i in range(2)]
            RC = [arf(20864 + i * 512, 512) for i in range(2)]
            OS = [arb(21888 + i * 1152, 1152) for i in range(2)]
            ACC = [arf(24192 + i * 512, 512) for i in range(2)]
            ACCP = [arf(25216 + i * 512, 512) for i in range(2)]
            ctr = {"step": 0, "blk": 0, "bi": 0, "os": 0, "kq": 0}
            pend = []

            def flush(depth):
                while len(pend) > depth:
                    pend.pop(0)()

            def qblocks():
                bl = [(g * 512, 512, g) for g in range(4)]
                if ctx_q:
                    bl.append((TL, 256, 4))
                return bl

            def run_head(mixer, h, kT, kT_tok, kR, qT, qT_tok, qR, v_fn, scale, orow, os_i):
                for (q0, n, g) in qblocks():
                    if g == 4:
                        chunks = [16, 17]
                    elif mixer == "a":
                        chunks = list(_a_chunks(g)) + [16, 17]
                    else:
                        chunks = list(range(18))
                    blk = ctr["blk"] % 2
                    ctr["blk"] += 1
                    ob, db = 3 + blk, 5 + blk
                    nck = len(chunks)
                    for ci, m in enumerate(chunks):
                        sbk = ctr["step"] % 3
                        pt = ctr["step"] % 4
                        ctr["step"] += 1
                        P.add("pe", lambda e, sbk=sbk, m=m, q0=q0, n=n: e.matmul(
                            bank(sbk)[:, 0:n], lhsT=kT[:, m * 128:(m + 1) * 128], rhs=qT[:, q0:q0 + n], start=True, stop=(kR is None)),
                            reads=[kT_tok, qT_tok], writes=[f"ps{sbk}"])
                        if kR is not None:
                            P.add("pe", lambda e, sbk=sbk, m=m, q0=q0, n=n: e.matmul(
                                bank(sbk)[:, 0:n], lhsT=kR[0:64, m * 128:(m + 1) * 128], rhs=qR[0:64, q0:q0 + n], start=False, stop=True),
                                reads=["kr", qT_tok], writes=[f"ps{sbk}"])
                        if mixer == "a" and m < 16:
                            bi = ctr["bi"] % 3
                            ctr["bi"] += 1
                            tid = (h * 20 + _abias_tile_id(g, m)) * 128
                            P.add("sp", lambda e, bi=bi, tid=tid: e.dma_start(out=BI[bi], in_=abias[tid:tid + 128, :]), writes=[f"bi{bi}"], chan=f"bi{bi}")
                            ss = ctr["step"] % 2
                            P.add("dve", lambda e, ss=ss, sbk=sbk, bi=bi, n=n: e.scalar_tensor_tensor(
                                out=SS[ss][:, 0:n], in0=bank(sbk)[:, 0:n], scalar=scale, in1=BI[bi][:, 0:n], op0=ALU.mult, op1=ALU.add),
                                reads=[f"ps{sbk}", f"bi{bi}"], writes=[f"ss{ss}"])
                            P.add("act", lambda e, ss=ss, pt=pt, n=n: e.activation(out=PT[pt][:, 0:n], in_=SS[ss][:, 0:n], func=AF.Exp),
                                  reads=[f"ss{ss}"], writes=[f"pt{pt}"])
                        else:
                            P.add("act", lambda e, sbk=sbk, pt=pt, n=n: e.activation(out=PT[pt][:, 0:n], in_=bank(sbk)[:, 0:n], func=AF.Exp, scale=scale),
                                  reads=[f"ps{sbk}"], writes=[f"pt{pt}"])

                        def pv(ci=ci, m=m, pt=pt, n=n, ob=ob, db=db, nck=nck, q0=q0, blk=blk):
                            use_acc = False
                            P.add("pe", lambda e: e.matmul(bank(ob)[:, 0:n], lhsT=v_fn(m), rhs=PT[pt][:, 0:n], start=(ci == 0), stop=(ci == nck - 1)),
                                  reads=[f"pt{pt}", f"vall.{m // 6}"], writes=[f"ps{ob}"])
                            if not use_acc:
                                P.add("pe", lambda e: e.matmul(bank(db)[:, 0:n], lhsT=ONES[:], rhs=PT[pt][:, 0:n], start=(ci == 0), stop=(ci == nck - 1)),
                                      reads=[f"pt{pt}", "ones"], writes=[f"ps{db}"])
                            else:
                                on_pool = (ci % 3 == 2)
                                eng = "pool" if on_pool else "dve"
                                ab, at = (ACCP[blk], f"accp{blk}") if on_pool else (ACC[blk], f"acc{blk}")
                                first = (ci == 2) if on_pool else (ci == 0)
                                if first:
                                    P.add(eng, lambda e: e.tensor_copy(out=ab[:, 0:n], in_=PT[pt][:, 0:n]), reads=[f"pt{pt}"], writes=[at])
                                else:
                                    P.add(eng, lambda e: e.tensor_tensor(out=ab[:, 0:n], in0=ab[:, 0:n], in1=PT[pt][:, 0:n], op=ALU.add),
                                          reads=[f"pt{pt}", at], writes=[at])
                            if ci == nck - 1:
                                if use_acc:
                                    if nck >= 3:
                                        P.add("pool", lambda e: e.tensor_tensor(out=ACCP[blk][:, 0:n], in0=ACCP[blk][:, 0:n], in1=ACC[blk][:, 0:n], op=ALU.add),
                                              reads=[f"accp{blk}", f"acc{blk}"], writes=[f"accp{blk}"])
                                        src, stok = ACCP[blk], f"accp{blk}"
                                    else:
                                        src, stok = ACC[blk], f"acc{blk}"
                                    P.add("pe", lambda e: e.matmul(bank(db)[:, 0:n], lhsT=ONES32[:], rhs=src[:, 0:n], start=True, stop=True),
                                          reads=[stok, "ones32"], writes=[f"ps{db}"])
                                if mixer == "a":
                                    P.add("act", lambda e: e.activation(out=RC[blk][:, 0:n], in_=bank(db)[:, 0:n], func=AF.Ln), reads=[f"ps{db}"], writes=[f"rc{blk}"])
                                    P.add("act", lambda e: e.activation(out=RC[blk][:, 0:n], in_=RC[blk][:, 0:n], func=AF.Exp, scale=-1.0), reads=[f"rc{blk}"], writes=[f"rc{blk}"])
                                else:
                                    P.add("dve", lambda e: e.reciprocal(out=RC[blk][:, 0:n], in_=bank(db)[:, 0:n]), reads=[f"ps{db}"], writes=[f"rc{blk}"])
                                P.add("dve", lambda e: e.tensor_tensor(out=OS[os_i][:, q0:q0 + n], in0=bank(ob)[:, 0:n], in1=RC[blk][:, 0:n], op=ALU.mult),
                                      reads=[f"ps{ob}", f"rc{blk}"], writes=[f"os{os_i}.{q0}"])
                        pend.append(pv)
                        flush(2)
                        if bg is not None:
                            ctr["bgc"] = ctr.get("bgc", 0) + 1
                            if ctr["bgc"] % 24 == 0:
                                next(bg, None)
                flush(0)
                ntok = T if ctx_q else TL
                P.add("sp", lambda e: e.dma_start(out=OT[orow:orow + 128, 0:ntok], in_=OS[os_i][:, 0:ntok]),
                      reads=[f"os{os_i}.{q0}" for (q0, n, g) in qblocks()], writes=[f"d.ot{orow}"], chan=f"os{os_i}")

            def load_row(buf, src, tokname, chan, rows=128):
                P.add("sp", lambda e: e.dma_start(out=buf[0:rows, :], in_=src), writes=[tokname], chan=chan)

            def load_v(src, width):
                v3 = Vall[:, 0:18 * width].rearrange("p (c d) -> p c d", c=18)
                s3 = src.rearrange("(c p) d -> p c d", p=128)
                for j in range(3):
                    P.add("sp", lambda e, j=j: e.dma_start(out=v3[:, 6 * j:6 * j + 6, :], in_=s3[:, 6 * j:6 * j + 6, :]),
                          writes=[f"vall.{j}"], chan=f"v{j}")
                return v3

            v3 = load_v(VA, 1024)

            def ld_a(h):
                i = h % 2
                load_row(Kb[i], QKA[1024 + h * 128:1024 + (h + 1) * 128, :], f"k{i}", f"k{i}")
                load_row(Qb[i], QKA[h * 128:(h + 1) * 128, :], f"q{i}", f"q{i}")
            ld_a(0)
            for h in range(8):
                i = h % 2
                if h + 1 < 8:
                    ld_a(h + 1)
                run_head("a", h, Kb[i], f"k{i}", None, Qb[i], f"q{i}", None, lambda m, h=h, v3=v3: v3[:, m, h * 128:(h + 1) * 128], 128 ** -0.5, h * 128, h % 2)
            P.barrier()
            v3 = load_v(VB, 256)

            def ld_b(h):
                kvh = h // 4
                if h % 4 == 0:
                    load_row(Kb[kvh % 2], KB[kvh * 128:(kvh + 1) * 128, :], f"k{kvh % 2}", f"k{kvh % 2}")
                load_row(Qb[h % 2], QB[h * 128:(h + 1) * 128, :], f"q{h % 2}", f"q{h % 2}")
            ld_b(0)
            for h in range(8):
                kvh = h // 4
                ki = kvh % 2
                i = h % 2
                if h + 1 < 8:
                    ld_b(h + 1)
                run_head("b", h, Kb[ki], f"k{ki}", None, Qb[i], f"q{i}", None, lambda m, kvh=kvh, v3=v3: v3[:, m, kvh * 128:(kvh + 1) * 128], 128 ** -0.5, 1024 + h * 128, h % 2)
            P.barrier()
            v3 = load_v(VC, 1024)
            load_row(KR, KRP[:, :], "kr", "kr", rows=64)

            def ld_c(h):
                i = h % 2
                load_row(Kb[i], KCN[h * 128:(h + 1) * 128, :], f"k{i}", f"k{i}")
                load_row(Qb[i], QCN[h * 128:(h + 1) * 128, :], f"q{i}", f"q{i}")
                load_row(QR[i], QCR[h * 64:(h + 1) * 64, :], f"q{i}", f"qr{i}", rows=64)
            ld_c(0)
            for h in range(8):
                i = h % 2
                if h + 1 < 8:
                    ld_c(h + 1)
                run_head("c", h, Kb[i], f"k{i}", KR, Qb[i], f"q{i}", QR[i], lambda m, h=h, v3=v3: v3[:, m, h * 128:(h + 1) * 128], 192 ** -0.5, 2048 + h * 128, h % 2)
            if bg is not None:
                for _ in bg:
                    pass
            P.barrier()

        def phase_merge(l, ntok):
            w_br = inp(f"w_br{l}", [3072, D])
            half = ntok // 2
            STG = [arb(9216 + i * 1152, 1152) for i in range(3)]
            GT = [arb(12672 + i * 1728, 1728).rearrange("p (b t) -> p b t", b=3) for i in range(2)]
            TM = [arf(16128 + i * 512, 512) for i in range(6)]
            res24 = RESb[:, 0:24 * half].rearrange("p (c t) -> p c t", c=24)
            tw = 384 if half % 384 == 0 else 512
            sg4 = SG.rearrange("(b c p) t -> c p b t", b=3, p=128)
            cnt = {"g": 0, "s": 0, "bk": 0, "g2": 0}
            for hf in range(2):
                h0 = hf * half
                for j in range(6):
                    src = OT[j * 512:(j + 1) * 512, h0:h0 + half].rearrange("(c p) t -> p c t", p=128)
                    P.add("sp", lambda e, j=j, src=src: e.dma_start(out=res24[:, 4 * j:4 * j + 4, :], in_=src),
                          writes=[f"res.{j}"], chan=f"r{j}")
                strm = slab_stream([(wsrc(w_br, 3072, ds_ * 256, 256), 24, 256) for ds_ in range(8)], 3, 3072, split=3)

                def gate_load(dc_):
                    gi2 = dc_ % 2
                    P.add("sp", lambda e, gt=GT[gi2], dc_=dc_, h0=h0: e.dma_start(out=gt[:, :, 0:half], in_=sg4[dc_][:, :, h0:h0 + half]),
                          writes=[f"gt{gi2}"], chan=f"gt{gi2}")
                gate_load(0)
                for ds in range(8):
                    view, tok, per = next(strm)
                    for ec in range(2):
                        dc = ds * 2 + ec
                        gi_ = dc % 2
                        gt = GT[gi_]
                        if dc + 1 < 16:
                            gate_load(dc + 1)
                        s = cnt["s"] % 3
                        cnt["s"] += 1
                        for ti in range(half // tw):
                            t0 = ti * tw
                            banks = []
                            for br in range(3):
                                b = cnt["bk"] % 6
                                cnt["bk"] += 1
                                banks.append(b)
                                for k in range(8):
                                    kk = br * 8 + k
                                    P.add("pe", lambda e, b=b, kk=kk, ec=ec, t0=t0, view=view: e.matmul(
                                        bank(b)[:, 0:tw], lhsT=view[:, kk, ec * 128:(ec + 1) * 128], rhs=res24[:, kk, t0:t0 + tw],
                                        start=(kk % 8 == 0), stop=(kk % 8 == 7)),
                                        reads=[f"{tok}.{kk // per}", f"res.{kk // 4}"], writes=[f"ps{b}"])
                            mset = cnt["g2"] % 2
                            cnt["g2"] += 1
                            ta, tb, tc = TM[mset * 3], TM[mset * 3 + 1], TM[mset * 3 + 2]
                            na, nb, ncn = f"tm{mset * 3}", f"tm{mset * 3 + 1}", f"tm{mset * 3 + 2}"
                            P.add("dve", lambda e, ta=ta, b=banks[0], gt=gt, t0=t0: e.tensor_tensor(out=ta[:, 0:tw], in0=bank(b)[:, 0:tw], in1=gt[:, 0, t0:t0 + tw], op=ALU.mult),
                                  reads=[f"ps{banks[0]}", f"gt{gi_}"], writes=[na])
                            P.add("dve", lambda e, tb=tb, b=banks[1], gt=gt, t0=t0: e.tensor_tensor(out=tb[:, 0:tw], in0=bank(b)[:, 0:tw], in1=gt[:, 1, t0:t0 + tw], op=ALU.mult),
                                  reads=[f"ps{banks[1]}", f"gt{gi_}"], writes=[nb])
                            P.add("dve", lambda e, tc=tc, b=banks[2], gt=gt, t0=t0: e.tensor_tensor(out=tc[:, 0:tw], in0=bank(b)[:, 0:tw], in1=gt[:, 2, t0:t0 + tw], op=ALU.mult),
                                  reads=[f"ps{banks[2]}", f"gt{gi_}"], writes=[ncn])
                            P.add("pool", lambda e, ta=ta, tb=tb: e.tensor_tensor(out=ta[:, 0:tw], in0=ta[:, 0:tw], in1=tb[:, 0:tw], op=ALU.add),
                                  reads=[na, nb], writes=[na])
                            P.add("pool", lambda e, ta=ta, tc=tc, s=s, t0=t0: e.tensor_tensor(out=STG[s][:, t0:t0 + tw], in0=ta[:, 0:tw], in1=tc[:, 0:tw], op=ALU.add),
                                  reads=[na, ncn], writes=[f"stg{s}.{ti}"])
                        P.add("sp", lambda e, s=s, dc=dc, h0=h0: e.dma_start(out=YT[dc * 128:(dc + 1) * 128, h0:h0 + half], in_=STG[s][:, 0:half]),
                              reads=[f"stg{s}.{ti}" for ti in range(half // tw)], writes=[f"d.yt{dc}.{hf}"], chan=f"st{s}")
                P.barrier()

        def load_res(src, kc, ntok, t0=0):
            r3 = RESb[:, 0:kc * ntok].rearrange("p (c t) -> p c t", c=kc)
            step = 4
            for j in range(0, kc, step):
                j1 = min(kc, j + step)
                s3 = src[j * 128:j1 * 128, t0:t0 + ntok].rearrange("(c p) t -> p c t", p=128)
                P.add("sp", lambda e, j=j, j1=j1, s3=s3: e.dma_start(out=r3[:, j:j1, :], in_=s3), writes=[f"res.{j // step}"], chan=f"r{j // step}")
            return r3, step

        def phase_wo(l, ntok):
            w_o = inp(f"w_o{l}", [D, D])
            STG32 = [arf(12288 + i * 2304, 2304) for i in range(2)]
            r3, step = load_res(YT, NCH, ntok)
            tgs = tgroups(ntok)
            cur = {"s": 0, "n": 0}

            def evac(tag, si, ec, gi, t0, n, m, b):
                if gi == 0:
                    cur["s"] = cur["n"] % 2
                    cur["n"] += 1
                s = cur["s"]
                copy_alt(gi, STG32[s][:, t0:t0 + n], bank(b)[:, 0:n], [f"ps{b}"], [f"s32{s}.{gi}"])

            def post(tag, si, ec, m):
                s = cur["s"]
                r0 = si * 512 + ec * 128
                P.add("sp", lambda e: e.dma_start(out=UT[r0:r0 + 128, 0:ntok], in_=STG32[s][:, 0:ntok]),
                      reads=[f"s32{s}.{gi}" for gi in range(len(tgs))], writes=[f"d.ut{r0}"], chan=f"s32{s}")
            slabs = [(wsrc(w_o, D, s * 512, 512), 512, s) for s in range(4)]
            gemm_fm(slabs, NCH, lambda k, gi: r3[:, k, tgs[gi][0]:tgs[gi][0] + tgs[gi][1]], lambda k, gi: f"res.{k // 4}", tgs, evac, post)
            P.barrier()

        def phase_ffn1(l, ntok):
            w1 = inp(f"w_ff1{l}", [D, DFF])
            w3 = inp(f"w_ff3{l}", [D, DFF])
            STG = [arb(16384 + i * 1152, 1152) for i in range(3)]
            TM = [arf(19840 + i * 512, 512) for i in range(3)]
            tgs = tgroups(ntok)
            cnt = {"s": 0, "b": 0, "t": 0}
            specs = []
            for fs_ in range(11):
                specs.append((wsrc(w1, D, fs_ * 512, 512), NCH, 512))
                specs.append((wsrc(w3, D, fs_ * 512, 512), NCH, 512))
            strm = slab_stream(specs, 4, 4096, ahead=3)
            for fs in range(11):
                va, ta, pa = next(strm)
                vg, tg_, pg = next(strm)
                for ec in range(4):
                    s = cnt["s"] % 3
                    cnt["s"] += 1
                    for gi, (t0, n) in enumerate(tgs):
                        ba = cnt["b"] % 8
                        bg = (cnt["b"] + 1) % 8
                        cnt["b"] += 2
                        for (view, tok, per, b) in ((va, ta, pa, ba), (vg, tg_, pg, bg)):
                            for k in range(NCH):
                                P.add("pe", lambda e, b=b, n=n, view=view, k=k, ec=ec, t0=t0: e.matmul(
                                    bank(b)[:, 0:n], lhsT=view[:, k, ec * 128:(ec + 1) * 128], rhs=res16[:, k, t0:t0 + n],
                                    start=(k == 0), stop=(k == NCH - 1)),
                                    reads=[f"{tok}.{k // per}", f"res.{t0 // 512}"], writes=[f"ps{b}"])
                        ti = cnt["t"] % 3
                        cnt["t"] += 1
                        P.add("act", lambda e, ti=ti, ba=ba, n=n: e.activation(out=TM[ti][:, 0:n], in_=bank(ba)[:, 0:n], func=AF.Silu),
                              reads=[f"ps{ba}"], writes=[f"tm{ti}"])
                        P.add("dve", lambda e, ti=ti, bg=bg, n=n, s=s, t0=t0: e.tensor_tensor(out=STG[s][:, t0:t0 + n], in0=bank(bg)[:, 0:n], in1=TM[ti][:, 0:n], op=ALU.mult),
                              reads=[f"ps{bg}", f"tm{ti}"], writes=[f"stg{s}.{gi}"])
                    r0 = fs * 512 + ec * 128
                    P.add("sp", lambda e, s=s, r0=r0: e.dma_start(out=ACTT[r0:r0 + 128, 0:ntok], in_=STG[s][:, 0:ntok]),
                          reads=[f"stg{s}.{gi}" for gi in range(len(tgs))], writes=[f"d.act{r0}"], chan=f"st{s}")
            P.barrier()

        def phase_ffn2(l, ntok):
            w2 = inp(f"w_ff2{l}", [DFF, D])
            S32 = [arf(11264 + i * 768, 768) for i in range(4)]
            parts = []
            t0 = 0
            while t0 < ntok:
                n = min(768, ntok - t0)
                parts.append((t0, n))
                t0 += n
            cnt = {"s": 0, "b": 0}
            for (p0, pn) in parts:
                r3, step = load_res(ACTT, FCH, pn, t0=p0)
                tiles = [(0, pn // 2), (pn // 2, pn // 2)]
                strm = slab_stream([(wsrc(w2, DFF, ds_ * 256, 256), FCH, 256) for ds_ in range(8)], 2, 5632, split=4)
                for ds in range(8):
                    view, tok, per = next(strm)
                    for ec in range(2):
                        s = cnt["s"] % 4
                        cnt["s"] += 1
                        for ti, (t0, n) in enumerate(tiles):
                            b = cnt["b"] % 4
                            cnt["b"] += 1
                            for k in range(FCH):
                                P.add("pe", lambda e, b=b, n=n, view=view, k=k, ec=ec, t0=t0, r3=r3: e.matmul(
                                    bank(b)[:, 0:n], lhsT=view[:, k, ec * 128:(ec + 1) * 128], rhs=r3[:, k, t0:t0 + n],
                                    start=(k == 0), stop=(k == FCH - 1)),
                                    reads=[f"{tok}.{k // per}", f"res.{k // step}"], writes=[f"ps{b}"])
                            copy_alt(ti, S32[s][:, t0:t0 + n], bank(b)[:, 0:n], [f"ps{b}"], [f"s32{s}.{ti}"])
                        r0 = ds * 256 + ec * 128
                        P.add("sp", lambda e, s=s, r0=r0, p0=p0, pn=pn: e.dma_start(out=U2T[r0:r0 + 128, p0:p0 + pn], in_=S32[s][:, 0:pn]),
                              reads=[f"s32{s}.0", f"s32{s}.1"], writes=[f"d.u2{r0}.{p0}"], chan=f"s32{s}")
                P.barrier()

        def program():
            phase_ada_first()
            phase_rn(0, X[0], None, None, None, (0, 1, 0), T)
            phase_end(0, "rn0")
            if stop["flag"]:
                return
            xi = 0
            for l in range(cfg.layers):
                last = (l == cfg.layers - 1)
                nq = TL if last else T
                phase_inproj(l)
                phase_end(l, "inproj")
                if stop["flag"]:
                    return
                phase_mla(l)
                phase_end(l, "mla")
                if stop["flag"]:
                    return
                bg = ada_units(ada_background_items()) if l == 0 else None
                phase_attn(l, ctx_q=not last, bg=bg)
                phase_end(l, "attn")
                if stop["flag"]:
                    return
                phase_merge(l, nq)
                phase_end(l, "merge")
                if stop["flag"]:
                    return
                phase_wo(l, nq)
                phase_end(l, "wo")
                if stop["flag"]:
                    return
                phase_rn(l, X[xi], X[xi + 1], UT, 2, (3, 4, l), nq)
                xi += 1
                phase_end(l, "rn1")
                if stop["flag"]:
                    return
                phase_ffn1(l, nq)
                phase_end(l, "ffn1")
                if stop["flag"]:
                    return
                phase_ffn2(l, nq)
                phase_end(l, "ffn2")
                if stop["flag"]:
                    return
                if last:
                    phase_rn(l, X[xi], None, U2T, 5, None, nq, out_final=True)
                else:
                    phase_rn(l, X[xi], X[xi + 1], U2T, 5, (0, 1, l + 1), T)
                    xi += 1
                phase_end(l, "rn2")
                if stop["flag"]:
                    return

        program()
        P.emit()
    return nc, in_shapes, out_names


def prep_core_inputs(inputs, b, shared):
    x = np.asarray(inputs["x"][b], np.float32)
    ctx = np.asarray(inputs["ctx"][b], np.float32)
    d = dict(shared)
    d["xT"] = np.ascontiguousarray(np.concatenate([x.T, ctx.T], axis=1))
    cv = np.stack([_pc(inputs["c"][b]), _pc(inputs["c_ctx"])], axis=2)
    d["cvec"] = np.ascontiguousarray(cv.reshape(128, 32))
    return d


def prep_shared(inputs, layers=2):
    (ropeB, rotB), (ropeC, rotC) = _rope_tables()
    sh = {"rotB": rotB, "rotC": rotC, "ropeB": ropeB, "ropeC": ropeC}
    vecs = []
    dr, dc, va = _abias_index()
    for l in range(layers):
        cols = [_pc(inputs["g_pre1"][l]), _pc(inputs["g_post1"][l]), _pc(inputs["g_pre2"][l]), _pc(inputs["g_post2"][l]),
                _pc(inputs["b_ada"][l]), _pc(inputs["gqa_q_norm"][l]), _pc(inputs["gqa_k_norm"][l]),
                _pc(inputs["mla_q_norm"][l]), _pc(inputs["mla_kv_norm"][l])]
        vecs.append(np.concatenate(cols, axis=1))
        rpb = np.asarray(inputs["rpb"][l], np.float32)
        g = rpb[:, dr, dc]
        g = np.where(va[None], g, np.float32(NEG)).astype(np.float32)
        sh[f"abias{l}"] = np.ascontiguousarray(g.reshape(8 * 20 * 128, 512))
        sh[f"w_ada{l}"] = np.asarray(inputs["w_ada"][l], np.float32)
        sh[f"w_in{l}"] = np.asarray(inputs["w_in"][l], np.float32)
        sh[f"w_uq{l}"] = np.asarray(inputs["w_uq"][l], np.float32)
        sh[f"w_ukv{l}"] = np.asarray(inputs["w_ukv"][l], np.float32)
        sh[f"w_br{l}"] = np.ascontiguousarray(np.concatenate(
            [inputs["w_br_a"][l], inputs["w_br_b"][l], inputs["w_br_c"][l]], axis=0).astype(np.float32))
        sh[f"w_o{l}"] = np.asarray(inputs["w_o"][l], np.float32)
        sh[f"w_ff1{l}"] = np.asarray(inputs["w_ff1"][l], np.float32)
        sh[f"w_ff3{l}"] = np.asarray(inputs["w_ff3"][l], np.float32)
        sh[f"w_ff2{l}"] = np.asarray(inputs["w_ff2"][l], np.float32)
    sh["vecs"] = np.ascontiguousarray(np.concatenate(vecs, axis=1))
    if layers == 1:
        sh["vecs"] = np.ascontiguousarray(np.concatenate([sh["vecs"], np.zeros_like(sh["vecs"])], axis=1))
    return sh


_CACHE = {}


def kernel(**inputs):
    n = 8
    if "nc" not in _CACHE:
        _CACHE["nc"] = build(Cfg())
    nc, in_shapes, out_names = _CACHE["nc"]
    shared = prep_shared(inputs)
    in_maps = []
    for b in range(n):
        d = prep_core_inputs(inputs, b, shared)
        in_maps.append({k: d[k] for k in in_shapes})
    res = run_bass_kernel_spmd(nc, in_maps, core_ids=list(range(n)))
    out = np.stack([np.ascontiguousarray(r["outT"].T) for r in res.results], axis=0)
    return out.astype(np.float32)
```
